# Optimizing a Trainium2 kernel written in Bass

```python
import jax, jax.numpy as jnp
from jax import lax
import numpy as np

D_MODEL = 1024
BATCH = 16
SEQ = 256
DEPTH = 1
DEC_BATCH = 8
DEC_SEQ = 2048
PAST_LEN = 256

GRID_W = 64
W_A = D_MODEL // 2
DK = 128
DV = 128
H_A = W_A // DV
W_B = D_MODEL - W_A
WG = 128
G_B = W_B // WG
SGU_CHUNK = 128
CONV_K = 5
DN_CHUNK = 64
D_FF = -(-8 * D_MODEL // (3 * 256)) * 256
N_MOD = 6
IN_W = 4 * W_A + 2 * W_B + 4 * H_A
EPS = 1e-6

kernel_name = "hybrid_deltanet_sgu_diffusion_step"


def rmsnorm(x, g):
    xf = x.astype(jnp.float32)
    y = xf * lax.rsqrt(jnp.mean(xf * xf, axis=-1, keepdims=True) + EPS)
    return (y * g.astype(jnp.float32)).astype(x.dtype)


def layernorm(x, g, b):
    xf = x.astype(jnp.float32)
    mu = jnp.mean(xf, axis=-1, keepdims=True)
    var = jnp.mean(jnp.square(xf - mu), axis=-1, keepdims=True)
    y = (xf - mu) * lax.rsqrt(var + EPS)
    return (y * g.astype(jnp.float32) + b.astype(jnp.float32)).astype(x.dtype)


def l2norm(x):
    xf = x.astype(jnp.float32)
    return (xf * lax.rsqrt(jnp.sum(xf * xf, axis=-1, keepdims=True) + EPS)).astype(x.dtype)


def short_conv(x, w):
    return lax.conv_general_dilated(
        x, w[:, None, :].astype(x.dtype), window_strides=(1,),
        padding=[(CONV_K // 2, CONV_K // 2)],
        dimension_numbers=("NWC", "WIO", "NWC"),
        feature_group_count=x.shape[-1])


def _to_chunks(x, n):
    b, t, h = x.shape[:3]
    x = x.reshape((b, n, DN_CHUNK, h) + x.shape[3:])
    return jnp.moveaxis(x, (1, 3), (0, 2))


def gated_delta_rule(q, k, v, g, beta, s0):
    b, t, h, _ = q.shape
    n = t // DN_CHUNK
    f32 = jnp.float32
    q, k, v, g, beta = (_to_chunks(a.astype(f32), n) for a in (q, k, v, g, beta))
    gc = jnp.cumsum(g, axis=-1)
    idx = jnp.arange(DN_CHUNK)
    incl = idx[:, None] >= idx[None, :]
    strict = idx[:, None] > idx[None, :]
    decay = jnp.exp(jnp.where(incl, gc[..., :, None] - gc[..., None, :], -jnp.inf))
    k_beta = k * beta[..., None]
    nmat = jnp.where(strict, jnp.einsum("nbhik,nbhjk->nbhij", k_beta, k) * decay, 0.0)
    eye = jnp.eye(DN_CHUNK, dtype=f32)
    tmat = lax.linalg.triangular_solve(eye + nmat, jnp.broadcast_to(eye, nmat.shape),
                                       left_side=True, lower=True, unit_diagonal=True)
    u = jnp.einsum("nbhij,nbhjd->nbhid", tmat, v * beta[..., None])
    w = jnp.einsum("nbhij,nbhjk->nbhik", tmat, k_beta * jnp.exp(gc)[..., None])
    qk = jnp.einsum("nbhik,nbhjk->nbhij", q, k) * decay
    q_dec = q * jnp.exp(gc)[..., None]
    k_dec = k * jnp.exp(gc[..., -1:] - gc)[..., None]
    g_last = jnp.exp(gc[..., -1])

    def step(s, inp):
        u_i, w_i, qk_i, q_i, k_i, gl_i = inp
        v_new = u_i - jnp.einsum("bhck,bhkd->bhcd", w_i, s)
        o_i = jnp.einsum("bhck,bhkd->bhcd", q_i, s) + jnp.einsum("bhij,bhjd->bhid", qk_i, v_new)
        s = s * gl_i[..., None, None] + jnp.einsum("bhck,bhcd->bhkd", k_i, v_new)
        return s, o_i

    s_fin, o = lax.scan(step, s0.astype(f32), (u, w, qk, q_dec, k_dec, g_last))
    o = jnp.moveaxis(o, (0, 2), (1, 3)).reshape(b, t, h, -1)
    return o, s_fin


def chunk_sgu(u, v, ln_g, ln_b, w_s, b_s):
    bsz, t, _ = v.shape
    n = t // SGU_CHUNK
    vh = layernorm(v, ln_g, ln_b).reshape(bsz, n, SGU_CHUNK, G_B, WG)
    mixed = jnp.einsum("gts,bnsgc->bntgc", w_s.astype(vh.dtype), vh) + jnp.transpose(b_s)[None, None, :, :, None]
    return u * mixed.reshape(bsz, t, W_B)


def hybrid_mixer(h, s_f0, s_b0, w_in, conv_w, a_log, dt_bias, dn_norm_g,
                 sgu_ln_g, sgu_ln_b, sgu_w, sgu_b, w_out):
    bsz, t, _ = h.shape
    proj = h @ w_in
    qkv, z, uv, beta_logit, a_logit = jnp.split(
        proj, [3 * W_A, 4 * W_A, 4 * W_A + 2 * W_B, 4 * W_A + 2 * W_B + 2 * H_A], axis=-1)
    qkv = jax.nn.silu(short_conv(qkv, conv_w))
    q, k, v = jnp.split(qkv, 3, axis=-1)
    q = l2norm(q.reshape(bsz, t, H_A, DK)) * (DK ** -0.5)
    k = l2norm(k.reshape(bsz, t, H_A, DK))
    v = v.reshape(bsz, t, H_A, DV)
    beta = jax.nn.sigmoid(beta_logit.astype(jnp.float32)).reshape(bsz, t, 2, H_A)
    g = -jnp.exp(a_log.astype(jnp.float32)) * jax.nn.softplus(
        a_logit.astype(jnp.float32).reshape(bsz, t, 2, H_A) + dt_bias.astype(jnp.float32))
    o_f, s_f = gated_delta_rule(q, k, v, g[:, :, 0], beta[:, :, 0], s_f0)
    o_b, s_b = gated_delta_rule(jnp.flip(q, 1), jnp.flip(k, 1), jnp.flip(v, 1),
                                jnp.flip(g[:, :, 1], 1), jnp.flip(beta[:, :, 1], 1), s_b0)
    o = o_f + jnp.flip(o_b, 1)
    o = rmsnorm(o, dn_norm_g) * jax.nn.silu(z.reshape(bsz, t, H_A, DV).astype(jnp.float32))
    y_a = o.reshape(bsz, t, W_A).astype(h.dtype)
    u, vb = jnp.split(jax.nn.gelu(uv), 2, axis=-1)
    y_b = chunk_sgu(u, vb, sgu_ln_g, sgu_ln_b, sgu_w, sgu_b)
    y = jnp.concatenate([y_a, y_b.astype(h.dtype)], axis=-1) @ w_out
    return y, s_f, s_b


def swiglu(h, w_gate, w_up, w_down):
    return (jax.nn.silu(h @ w_gate) * (h @ w_up)) @ w_down


def trunk_layer(x, c, s_f0, s_b0, w_ada, b_ada, norm1_g, norm2_g, w_in, conv_w, a_log,
                dt_bias, dn_norm_g, sgu_ln_g, sgu_ln_b, sgu_w, sgu_b, w_out,
                w_gate, w_up, w_down):
    mod = (jax.nn.silu(c) @ w_ada + b_ada)[:, None, :]
    sh1, sc1, g1, sh2, sc2, g2 = jnp.split(mod, N_MOD, axis=-1)
    h = rmsnorm(x, norm1_g) * (1 + sc1) + sh1
    y, s_f, s_b = hybrid_mixer(h, s_f0, s_b0, w_in, conv_w, a_log, dt_bias, dn_norm_g,
                               sgu_ln_g, sgu_ln_b, sgu_w, sgu_b, w_out)
    x = x + g1 * y
    h = rmsnorm(x, norm2_g) * (1 + sc2) + sh2
    x = x + g2 * swiglu(h, w_gate, w_up, w_down)
    return x, s_f, s_b


def setup_inputs(seed: int = 0) -> dict:
    key = jax.random.key(seed)
    ks = jax.random.split(key, 26)
    f32 = jnp.float32
    nrm = lambda k, shape, s: jax.random.normal(k, shape, f32) * s
    dt = jnp.exp(jax.random.uniform(ks[10], (DEPTH, 2, H_A), f32, np.log(1e-3), np.log(1e-1)))
    return {
        "x_prompt": nrm(ks[0], (BATCH, SEQ, D_MODEL), 1.0),
        "x_sample": nrm(ks[1], (DEC_BATCH, DEC_SEQ, D_MODEL), 1.0),
        "state_fwd": nrm(ks[2], (DEC_BATCH, DEPTH, H_A, DK, DV), DK ** -0.5),
        "state_bwd": nrm(ks[3], (DEC_BATCH, DEPTH, H_A, DK, DV), DK ** -0.5),
        "c": nrm(ks[4], (DEC_BATCH, D_MODEL), 1.0),
        "c_ctx": nrm(ks[5], (D_MODEL,), 1.0),
        "w_ada": nrm(ks[6], (DEPTH, D_MODEL, N_MOD * D_MODEL), 0.5 * D_MODEL ** -0.5),
        "b_ada": nrm(ks[7], (DEPTH, N_MOD * D_MODEL), 0.02),
        "norm1_g": 1.0 + nrm(ks[8], (DEPTH, D_MODEL), 0.02),
        "norm2_g": 1.0 + nrm(ks[9], (DEPTH, D_MODEL), 0.02),
        "w_in": nrm(ks[11], (DEPTH, D_MODEL, IN_W), D_MODEL ** -0.5),
        "conv_w": nrm(ks[12], (DEPTH, CONV_K, 3 * W_A), CONV_K ** -0.5),
        "a_log": jnp.log(jax.random.uniform(ks[13], (DEPTH, 2, H_A), f32, 1.0, 16.0)),
        "dt_bias": dt + jnp.log(-jnp.expm1(-dt)),
        "dn_norm_g": 1.0 + nrm(ks[14], (DEPTH, DV), 0.02),
        "sgu_ln_g": 1.0 + nrm(ks[15], (DEPTH, W_B), 0.02),
        "sgu_ln_b": nrm(ks[16], (DEPTH, W_B), 0.02),
        "sgu_w": nrm(ks[17], (DEPTH, G_B, SGU_CHUNK, SGU_CHUNK), 0.5 * SGU_CHUNK ** -0.5),
        "sgu_b": 1.0 + nrm(ks[18], (DEPTH, G_B, SGU_CHUNK), 0.01),
        "w_out": nrm(ks[19], (DEPTH, D_MODEL, D_MODEL), D_MODEL ** -0.5),
        "w_gate": nrm(ks[20], (DEPTH, D_MODEL, D_FF), D_MODEL ** -0.5),
        "w_up": nrm(ks[21], (DEPTH, D_MODEL, D_FF), D_MODEL ** -0.5),
        "w_down": nrm(ks[22], (DEPTH, D_FF, D_MODEL), D_FF ** -0.5),
        "final_g": 1.0 + nrm(ks[23], (D_MODEL,), 0.02),
    }


def reference(x_prompt, x_sample, state_fwd, state_bwd, c, c_ctx, w_ada, b_ada, norm1_g,
              norm2_g, w_in, conv_w, a_log, dt_bias, dn_norm_g, sgu_ln_g, sgu_ln_b, sgu_w,
              sgu_b, w_out, w_gate, w_up, w_down, final_g):
    s_zero = jnp.zeros((x_prompt.shape[0], H_A, DK, DV), jnp.float32)
    c_prompt = c_ctx[None, :]
    xp, xs = x_prompt, x_sample
    new_f, new_b = [], []
    for l in range(DEPTH):
        params = (w_ada[l], b_ada[l], norm1_g[l], norm2_g[l], w_in[l], conv_w[l], a_log[l],
                  dt_bias[l], dn_norm_g[l], sgu_ln_g[l], sgu_ln_b[l], sgu_w[l], sgu_b[l],
                  w_out[l], w_gate[l], w_up[l], w_down[l])
        xp, sf, sb = trunk_layer(xp, c_prompt, s_zero, s_zero, *params)
        new_f.append(sf)
        new_b.append(sb)
        xs, _, _ = trunk_layer(xs, c, state_fwd[:, l], state_bwd[:, l], *params)
    y_prompt = rmsnorm(xp, final_g)
    y_sample = rmsnorm(xs, final_g)
    new_state_fwd = jnp.stack(new_f, axis=1).astype(x_prompt.dtype)
    new_state_bwd = jnp.stack(new_b, axis=1).astype(x_prompt.dtype)
    return (y_prompt, y_sample, new_state_fwd, new_state_bwd)
```

```python
import contextlib
import numpy as np
import concourse.bass as bass
import concourse.mybir as mybir
from concourse.bass_utils import run_bass_kernel_spmd

F32 = mybir.dt.float32
BF16 = mybir.dt.bfloat16
AF = mybir.ActivationFunctionType
ALU = mybir.AluOpType
AX = mybir.AxisListType

NT = 2560
NCH = 20
D = 1024
DFF = 2816
NF = 22
EPS = 1e-6
NEG = -30000.0
SEQS = [(0, 2, 0), (2, 2, 0), (4, 16, 1)]
CT = [(0, 256, 2), (256, 256, 260)] + [(512 + 512 * k, 512, 518 + 512 * k) for k in range(4)]
PCW = 2568

DEBUG = {}


class Buf:
    __slots__ = ("name", "w", "rs")

    def __init__(self, name):
        self.name = name
        self.w = None
        self.rs = []


class Sched:
    EP = 16000

    def __init__(self, nc, es):
        self.nc = nc
        self.es = es
        self.ops = []
        self.eng = {"pe": nc.tensor, "act": nc.scalar, "dve": nc.vector, "pool": nc.gpsimd, "sp": nc.sync}
        self.nb = 0

    def buf(self, name=None):
        self.nb += 1
        return Buf(name or f"b{self.nb}")

    def op(self, eng, fn, reads=(), writes=()):
        self.ops.append(("c", eng, fn, tuple(reads), tuple(writes), None))

    def dma(self, q, fn, reads=(), writes=(), stream="d"):
        self.ops.append(("d", q, fn, tuple(reads), tuple(writes), stream))

    def barrier(self):
        self.ops.append(("b", None, None, (), (), None))

    def mark(self, name):
        self.marks = getattr(self, 'marks', {})
        self.marks[name] = len(self.ops)

    def emit(self):
        ops = self.ops
        n = len(ops)
        deps = [None] * n
        need_inc = [False] * n
        last_eng = {}
        last_stream = {}
        pending_barrier = {}
        dma_since = []
        for i, (kind, eng, fn, reads, writes, stream) in enumerate(ops):
            if kind == "b":
                snap = set(last_eng.values()) | set(dma_since)
                dma_since = []
                for e in self.eng:
                    pending_barrier[e] = pending_barrier.get(e, set()) | snap
                continue
            ds = {}
            for b in reads:
                if b.w is not None:
                    ds[b.w] = "raw"
            for b in writes:
                if b.w is not None and b.w not in ds:
                    ds[b.w] = "waw"
                lastr = {}
                for r in b.rs:
                    key = (ops[r][0], ops[r][1], ops[r][5])
                    if ops[r][0] == "c":
                        lastr[key] = max(lastr.get(key, -1), r)
                    else:
                        lastr[(key, r)] = r
                for r in lastr.values():
                    if r not in ds:
                        ds[r] = "war"
            if eng in pending_barrier and pending_barrier[eng]:
                for j in pending_barrier[eng]:
                    ds[j] = "raw"
                pending_barrier[eng] = set()
            for b in reads:
                b.rs.append(i)
            for b in writes:
                b.w = i
                b.rs = []
            keep = []
            for j, typ in ds.items():
                if j == i:
                    continue
                pk, pe = ops[j][0], ops[j][1]
                if pk == "c" and kind == "c" and pe == eng and typ != "raw":
                    continue
                if pk == "c" and kind == "c" and pe == eng and eng == "pe":
                    continue
                keep.append(j)
                if pk == "c":
                    need_inc[j] = True
            deps[i] = keep
            if kind == "c":
                last_eng[eng] = i
            else:
                last_stream[stream] = i
                if stream not in ("wc", "dbg"):
                    dma_since.append(i)
        sems = {}
        ord_of = {}
        cnt = {e: 0 for e in self.eng}
        known = {e: {} for e in self.eng}
        POOLN = {"sp": 10, "pool": 4, "act": 4}
        pool_next = {q: 0 for q in POOLN}
        pool_cnt = {}
        dma_ev = {}

        def sem(key):
            if key not in sems:
                sems[key] = self.es.enter_context(self.nc.semaphore("s_" + "_".join(str(k) for k in key)))
            return sems[key]

        nwaits = 0
        for i, (kind, eng, fn, reads, writes, stream) in enumerate(ops):
            if kind == "b":
                continue
            h = self.eng[eng]
            want = {}
            for j in deps[i]:
                pk, pe = ops[j][0], ops[j][1]
                if pk == "c":
                    key = ("e", pe)
                    want[key] = max(want.get(key, 0), ord_of[j])
                else:
                    key, v = dma_ev[j]
                    want[key] = max(want.get(key, 0), v)
            if kind == "d":
                slot = pool_next[eng] % POOLN[eng]
                pool_next[eng] += 1
                key = ("d", eng, slot)
                prev = pool_cnt.get(key, 0)
                if prev:
                    want[key] = max(want.get(key, 0), prev)
                pool_cnt[key] = prev + 16
                assert pool_cnt[key] < 60000
                dma_ev[i] = (key, prev + 16)
            for key, o in want.items():
                if known[eng].get(key, 0) >= o:
                    continue
                known[eng][key] = o
                if key[0] == "e":
                    ep, v = (o - 1) // self.EP, (o - 1) % self.EP + 1
                    h.wait_ge(sem((key[1], ep)), v)
                else:
                    h.wait_ge(sem(key), o)
                nwaits += 1
            inst = fn()
            if kind == "c":
                if need_inc[i]:
                    cnt[eng] += 1
                    ord_of[i] = cnt[eng]
                    ep = (cnt[eng] - 1) // self.EP
                    inst.then_inc(sem((eng, ep)), 1)
            else:
                inst.then_inc(sem(dma_ev[i][0]), 16)
        for key, c in pool_cnt.items():
            self.nc.sync.wait_ge(sem(key), c)
        self.stats = dict(nops=n, nwaits=nwaits, nsems=len(sems), cnt=cnt)


class Ring:
    def __init__(self, S, aps, name):
        self.items = [(ap, S.buf(f"{name}{i}")) for i, ap in enumerate(aps)]
        self.i = 0

    def next(self):
        it = self.items[self.i % len(self.items)]
        self.i += 1
        return it


def build_nc(debug=False, stop=None):
    nc = bass.Bass("TRN2", target_bir_lowering=False)
    es = contextlib.ExitStack()
    S = Sched(nc, es)

    def din(name, shape, dt=F32):
        return nc.dram_tensor(name, list(shape), dt, kind="ExternalInput").ap()

    def dout(name, shape, dt=F32):
        return nc.dram_tensor(name, list(shape), dt, kind="ExternalOutput").ap()

    def dint(name, shape, dt=F32):
        return nc.dram_tensor(name, list(shape), dt, kind="Internal").ap()

    x_d = din("x", [NT, D])
    cT_d = din("cT", [128, 8, 2])
    s0_d = din("s0", [2, 128, 512])
    w_ada_d = din("w_ada", [D, 6 * D])
    b_ada_d = din("b_ada2", [2, 6 * D])
    w_in_d = din("w_in", [D, 3088])
    w_out_d = din("w_out", [D, D])
    wg_d = din("wg_t", [NF * 128, D])
    wu_d = din("wu_t", [NF * 128, D])
    wd_d = din("w_down", [DFF, D])
    convw_d = din("conv_wT", [128, 12, 5])
    n1g_d = din("n1g", [128, 8])
    n2g_d = din("n2g", [128, 8])
    adt_d = din("adt", [1, 16])
    dng_d = din("dn_g", [1, 128])
    lngT_d = din("lngT", [128, 4])
    lnb_d = din("ln_b", [1, 512])
    sguw_d = din("sgu_wT", [128, 4, 128])
    sgub_d = din("sgu_b", [1, 512])
    fg_d = din("final_g", [1, D])
    consts_d = din("consts", [128, 11 * 128])
    sel_d = din("sel", [2, 2, 128])

    y_d = dout("y", [NT, D])
    sf_d = dout("sf", [2, 4, 128, 128])
    sb_d = dout("sb", [2, 4, 128, 128])
    dbg_d = {}

    mod_d = dint("mod_scr", [2, 6 * D])
    wo_s = dint("wo_scr", [D, D], BF16)
    wd_s = dint("wd_scr", [DFF, D], BF16)
    wg_s = dint("wg_scr", [NF * 128, D], BF16)
    wu_s = dint("wu_scr", [NF * 128, D], BF16)

    def sb(name, shape, dt=F32, stack=None):
        return (stack or es).enter_context(nc.sbuf_tensor("sb_" + name, list(shape), dt))

    banks = [es.enter_context(nc.psum_tensor(f"bank{i}", [128, 512], F32)) for i in range(8)]
    PS = Ring(S, [b[:] for b in banks], "ps")

    def psum():
        return PS.next()

    consts = sb("consts", [128, 11 * 128])
    ident_f = consts[:, 0:128]
    U = [consts[:, 128:256], consts[:, 256:384]]
    MN = [consts[:, 384:512], consts[:, 512:640]]
    offd_f = consts[:, 640:768]
    ones_f = consts[:, 768:896]
    cbf = sb("cbf", [128, 7 * 128], BF16)
    ident_b = cbf[:, 0:128]
    ones_b = cbf[:, 128:256]
    offd_b = cbf[:, 256:384]
    MASKS = [cbf[:, 384 + 128 * m:512 + 128 * m] for m in range(4)]
    sel = sb("sel", [2, 2, 128])
    B_const = S.buf("consts")
    cols = sb("cols", [128, 8])
    mh4 = sb("mh4", [128, 4])
    sc_sh = sb("sc_sh", [128, 4, 8, 2])
    B_mod1 = S.buf("mod1")
    B_mod2 = S.buf("mod2")
    B_mblk = [S.buf(f"mblk{i}") for i in range(24)]
    n1g = sb("n1g", [128, 8])
    n2g = sb("n2g", [128, 8])
    convw = sb("convw", [128, 12, 5])
    adt_b = sb("adt_b", [128, 16])
    nea_b = sb("nea_b", [128, 8])
    lngT = sb("lngT", [128, 4])
    BB = sb("BB", [128, 512])
    sguw = sb("sguw", [128, 4, 128], BF16)
    B_par = S.buf("params")
    qT = sb("qT", [128, 4, NT], BF16)
    ybT = sb("ybT", [128, 4, NT], BF16)
    B_q = [[S.buf(f"q{h}_{c}") for c in range(NCH)] for h in range(4)]
    B_yb = [S.buf(f"yb{c}") for c in range(NCH)]

    eng = S.eng

    def dbg(name, ap, shape, dt, reads):
        if not debug:
            return
        dd = dout("dbg_" + name, shape, dt)
        S.dma("sp", lambda: nc.sync.dma_start(out=dd, in_=ap), reads=reads, stream="dbg")

    S.dma("sp", lambda: nc.sync.dma_start(out=consts[:], in_=consts_d), writes=[B_const], stream="c0")
    S.dma("sp", lambda: nc.sync.dma_start(out=sel[:], in_=sel_d), writes=[B_const], stream="c0")
    for (t, d_) in [(n1g, n1g_d), (n2g, n2g_d), (convw, convw_d), (lngT, lngT_d)]:
        S.dma("sp", lambda t=t, d_=d_: nc.sync.dma_start(out=t[:], in_=d_), writes=[B_par], stream="c0")
    for (t, d_, w) in [(adt_b, adt_d, 16)]:
        S.dma("sp", lambda t=t, d_=d_: nc.sync.dma_start(out=t[:], in_=d_.partition_broadcast(128)[:, 0, :]),
              writes=[B_par], stream="c0")
    S.op("dve", lambda: nc.vector.tensor_copy(out=cbf[:, 0:128], in_=ident_f), reads=[B_const], writes=[B_const])
    S.op("dve", lambda: nc.vector.tensor_copy(out=cbf[:, 128:256], in_=ones_f), reads=[B_const], writes=[B_const])
    S.op("dve", lambda: nc.vector.tensor_copy(out=cbf[:, 256:384], in_=offd_f), reads=[B_const], writes=[B_const])
    S.op("dve", lambda: nc.vector.tensor_copy(out=cbf[:, 384:896], in_=consts[:, 896:1408]), reads=[B_const], writes=[B_const])
    for k, v in enumerate([EPS, -0.5, -0.5 * float(np.log(128.0)), 1.0, 4 * EPS, 0.0]):
        S.op("pool", lambda k=k, v=v: nc.gpsimd.memset(cols[:, k:k + 1], v), writes=[B_const])
    S.op("pool", lambda: nc.gpsimd.memset(mh4[:], -0.5), writes=[B_const])
    S.op("act", lambda: nc.scalar.activation(out=nea_b[:], in_=adt_b[:, 0:8], func=AF.Exp), reads=[B_par], writes=[B_par])
    S.op("dve", lambda: nc.vector.tensor_scalar(out=nea_b[:], in0=nea_b[:], scalar1=-1.0, scalar2=None, op0=ALU.mult),
         reads=[B_par], writes=[B_par])
    S.mark('p0b')
    B_wscr = {"wo": [], "wd": [], "wg": [], "wu": []}

    with contextlib.ExitStack() as st0:
        lnb_b = sb("lnb_b", [128, 512], stack=st0)
        sgub_b = sb("sgub_b", [128, 512], stack=st0)
        sguw_f = sb("sguw_f", [128, 4, 128], stack=st0)
        S.dma("sp", lambda: nc.sync.dma_start(out=sguw_f[:], in_=sguw_d), writes=[B_par], stream="c0")
        for (t, d_) in [(lnb_b, lnb_d), (sgub_b, sgub_d)]:
            S.dma("sp", lambda t=t, d_=d_: nc.sync.dma_start(out=t[:], in_=d_.partition_broadcast(128)[:, 0, :]),
                  writes=[B_par], stream="c0")
        S.op("dve", lambda: nc.vector.tensor_copy(out=sguw[:], in_=sguw_f[:]), reads=[B_par], writes=[B_par])
        S.mark('p0a')
        pB, bB = psum()
        for g in range(4):
            S.op("pe", lambda g=g: nc.tensor.matmul(pB[:, g * 128:(g + 1) * 128], lhsT=lnb_b[:, g * 128:(g + 1) * 128],
                                                      rhs=sguw_f[:, g, :], start=True, stop=True),
                 reads=[B_par], writes=[bB])
        S.op("dve", lambda: nc.vector.tensor_tensor(out=BB[:], in0=pB, in1=sgub_b[:], op=ALU.add),
             reads=[bB, B_par], writes=[B_par])

        S.barrier()
    S.barrier()

    def norm_stage_a(xt, bx, ring_sq, ring_xh, tmp_small):
        sq, bsq = ring_sq.next()
        ss, bss = tmp_small.next()
        S.op("act", lambda: nc.scalar.activation(out=sq, in_=xt, func=AF.Square, accum_out=ss[:, 0:1]),
             reads=[bx], writes=[bsq, bss])
        S.op("pool", lambda: nc.gpsimd.tensor_scalar(out=ss[:, 1:2], in0=ss[:, 0:1], scalar1=1.0 / D, scalar2=EPS,
                                                     op0=ALU.mult, op1=ALU.add), reads=[bss], writes=[bss])
        S.op("pool", lambda: nc.gpsimd.tensor_tensor(out=ss[:, 2:3], in0=ss[:, 1:2], in1=cols[:, 1:2], op=ALU.pow),
             reads=[bss, B_const], writes=[bss])
        xh, bxh = ring_xh.next()
        S.op("dve", lambda: nc.vector.tensor_scalar(out=xh, in0=xt, scalar1=ss[:, 2:3], scalar2=None, op0=ALU.mult),
             reads=[bx, bss], writes=[bxh])
        return xh, bxh

    def norm_stage_b(xh, bxh, which_scale, which_shift, r, dstT, tok0, bdst, bmod):
        pt, bpt = psum()
        ptb = pt.bitcast(BF16)
        for kc in range(8):
            S.op("pe", lambda kc=kc: nc.tensor.transpose(ptb[:, kc * 128:(kc + 1) * 128], xh[:, kc * 128:(kc + 1) * 128], ident_b),
                 reads=[bxh, B_const], writes=[bpt])
        for kc in range(8):
            S.op("act", lambda kc=kc: nc.scalar.activation(out=dstT[:, kc, tok0:tok0 + 128], in_=ptb[:, kc * 128:(kc + 1) * 128],
                                                           func=AF.Identity, scale=sc_sh[:, which_scale, kc, r:r + 1],
                                                           bias=sc_sh[:, which_shift, kc, r:r + 1]),
                 reads=[bpt, bmod], writes=[bdst])

    def norm_to_featmajor(xt, bx, which_scale, which_shift, r, dstT, tok0, bdst, ring_sq, ring_xh, tmp_small, bmod):
        xh, bxh = norm_stage_a(xt, bx, ring_sq, ring_xh, tmp_small)
        norm_stage_b(xh, bxh, which_scale, which_shift, r, dstT, tok0, bdst, bmod)

    with contextlib.ExitStack() as stAB:
        hT = sb("hT", [128, 8, NT], BF16, stack=stAB)
        kT = sb("kT", [128, 4, NT], BF16, stack=stAB)
        vT = sb("vT", [128, 4, NT], BF16, stack=stAB)
        zs = sb("zs", [128, NCH, 512], BF16, stack=stAB)
        lg = sb("lg", [128, NCH, 16], stack=stAB)
        beta = sb("beta", [128, NCH, 8], stack=stAB)
        negb = sb("negb", [128, NCH, 8], stack=stAB)
        gg = sb("gg", [128, NCH, 8], stack=stAB)
        negg = sb("negg", [128, NCH, 8], stack=stAB)
        gcl = sb("gcl", [128, NCH, 16], stack=stAB)
        egc = sb("egc", [128, NCH, 8], stack=stAB)
        egl = sb("egl", [128, NCH, 8], stack=stAB)
        ekd = sb("ekd", [128, NCH, 8], stack=stAB)
        bege = sb("bege", [128, NCH, 8], stack=stAB)
        B_h = [S.buf(f"h{c}") for c in range(NCH)]
        B_k = [[S.buf(f"k{h}_{c}") for c in range(NCH)] for h in range(4)]
        B_v = [[S.buf(f"v{h}_{c}") for c in range(NCH)] for h in range(4)]
        B_z = [S.buf(f"z{c}") for c in range(NCH)]
        B_lg = S.buf("lg")
        B_coef = S.buf("coef")

        def chunks_of(t0, n):
            return range(t0 // 128, (t0 + n + 127) // 128)

        with contextlib.ExitStack() as stA:
            xr = [sb(f"xr{i}", [128, D], stack=stA) for i in range(3)]
            xR = Ring(S, [t[:] for t in xr], "xr")
            sqj = sb("sqj", [128, D], BF16, stack=stA)
            sqR = Ring(S, [sqj[:]], "sqj")
            xhr = [sb(f"xh{i}", [128, D], BF16, stack=stA) for i in range(2)]
            xhR = Ring(S, [t[:] for t in xhr], "xh")
            ssr = [sb(f"ss{i}", [128, 4], stack=stA) for i in range(4)]
            ssR = Ring(S, [t[:] for t in ssr], "ss")

            cT = sb("cT", [128, 8, 2], stack=stA)
            cs = sb("cs", [128, 8, 2], stack=stA)
            B_c = S.buf("c")
            waR = Ring(S, [sb(f"wa{i}", [128, 8, 256], stack=stA)[:] for i in range(2)], "wa")
            bbR = Ring(S, [sb(f"bbt{i}", [2, 256], stack=stA)[:] for i in range(2)], "bbt")
            mrR = Ring(S, [sb(f"mrw{i}", [2, 256], stack=stA)[:] for i in range(2)], "mrw")
            S.dma("sp", lambda: nc.sync.dma_start(out=cT[:], in_=cT_d), writes=[B_c], stream="c0")
            S.op("act", lambda: nc.scalar.activation(out=cs[:], in_=cT[:], func=AF.Silu), reads=[B_c], writes=[B_c])
            wav = w_ada_d.rearrange("(kc p) n -> p kc n", p=128)

            def mod_block(nb):
                cs0 = nb * 256
                wt, bw = waR.next()
                S.dma("sp", lambda: nc.sync.dma_start(out=wt, in_=wav[:, :, cs0:cs0 + 256]), writes=[bw], stream="wa")
                bt, bb_ = bbR.next()
                S.dma("sp", lambda: nc.sync.dma_start(out=bt, in_=b_ada_d[:, cs0:cs0 + 256]), writes=[bb_], stream="wa")
                pm, bm = psum()
                for kc in range(8):
                    S.op("pe", lambda kc=kc: nc.tensor.matmul(pm[0:2, 0:256], lhsT=cs[:, kc, :], rhs=wt[:, kc, :],
                                                              start=(kc == 0), stop=(kc == 7)), reads=[B_c, bw], writes=[bm])
                mr, bmr = mrR.next()
                S.op("dve", lambda: nc.vector.tensor_tensor(out=mr, in0=pm[0:2, 0:256], in1=bt, op=ALU.add), reads=[bm, bb_], writes=[bmr])
                S.dma("pool", lambda: nc.gpsimd.dma_start(out=mod_d[:, cs0:cs0 + 256], in_=mr), reads=[bmr], writes=[B_mblk[nb]], stream="md")

            def load_scsh(wis, bmodx):
                for wi in wis:
                    seg = [0, 1, 3, 4][wi]
                    for r in range(2):
                        S.dma("sp", lambda wi=wi, seg=seg, r=r: nc.sync.dma_start(
                            out=sc_sh[:, wi, :, r], in_=mod_d[r, seg * D:(seg + 1) * D].rearrange("(kc p) -> p kc", p=128),
                            allow_slow_non_contiguous=True), reads=B_mblk[seg * 4:seg * 4 + 4], writes=[bmodx], stream="md")
                wi, gt = (1, n1g) if 1 in wis else (3, n2g)
                for r in range(2):
                    S.op("dve", lambda wi=wi, gt=gt, r=r: nc.vector.scalar_tensor_tensor(
                        out=sc_sh[:, wi, :, r], in0=sc_sh[:, wi, :, r], scalar=1.0, in1=gt[:], op0=ALU.add, op1=ALU.mult),
                        reads=[bmodx, B_par], writes=[bmodx])

            for nb in range(8):
                mod_block(nb)
            load_scsh([0, 1], B_mod1)
            S.mark('p0')

            def a1_stage_a(c):
                xt, bx = xR.next()
                S.dma("sp", lambda xt=xt, c=c: nc.sync.dma_start(out=xt, in_=x_d[c * 128:(c + 1) * 128, :]),
                      writes=[bx], stream=f"x{c % 3}")
                return norm_stage_a(xt, bx, sqR, xhR, ssR)

            cur = a1_stage_a(0)
            for c in range(NCH):
                if c < 16:
                    mod_block(8 + c)
                nxt = a1_stage_a(c + 1) if c + 1 < NCH else None
                r = 0 if c < 4 else 1
                norm_stage_b(cur[0], cur[1], 1, 0, r, hT, c * 128, B_h[c], B_mod1)
                cur = nxt
            load_scsh([2, 3], B_mod2)

            S.barrier()
        S.barrier()
        with contextlib.ExitStack() as stA:
            wblk = [sb(f"wblk{i}", [128, 8, 512], BF16, stack=stA) for i in range(2)]
            wR = Ring(S, [t[:] for t in wblk], "wblk")
            pre = [sb(f"pre{i}", [128, PCW], BF16, stack=stA) for i in range(2)]
            preR = Ring(S, [t[:] for t in pre], "pre")
            dg = [sb(f"dg{i}", [128, 5, 128], BF16, stack=stA) for i in range(2)]
            dgR = Ring(S, [t[:] for t in dg], "dg")
            tmpA = [sb(f"tmpA{i}", [128, 512], stack=stA) for i in range(3)]
            tAR = Ring(S, [t[:] for t in tmpA], "tmpA")
            tmpB = [sb(f"tmpB{i}", [128, 512], BF16, stack=stA) for i in range(4)]
            tBR = Ring(S, [t[:] for t in tmpB], "tmpB")
            us = sb("us", [128, 4, 512], BF16, stack=stA)
            B_us = S.buf("us")
            stt = [sb(f"stt{i}", [128, 8], stack=stA) for i in range(2)]
            stR = Ring(S, [t[:] for t in stt], "stt")

            S.mark('a1')
            w_in_v = w_in_d.rearrange("(kc p) n -> p kc n", p=128)

            def load_wblk(col0, ncol):
                wt, bw = wR.next()
                S.dma("pool", lambda: nc.gpsimd.dma_start(out=wt[:, :, 0:ncol], in_=w_in_v[:, :, col0:col0 + ncol]),
                      writes=[bw], stream=f"wi{(wR.i) % 2}")
                return wt, bw

            for blk in range(3):
                wt, bw = load_wblk(blk * 512, 512)
                for ch in range(4):
                    cch = blk * 4 + ch
                    pc, bpc = preR.next()
                    S.op("pool", lambda pc=pc: nc.gpsimd.memset(pc, 0.0), writes=[bpc])
                    for st5 in range(5):
                        pp, bp = psum()
                        for kc in range(8):
                            S.op("pe", lambda pp=pp, kc=kc, ch=ch, st5=st5, wt=wt: nc.tensor.matmul(
                                pp, lhsT=wt[:, kc, ch * 128:(ch + 1) * 128], rhs=hT[:, kc, st5 * 512:(st5 + 1) * 512],
                                start=(kc == 0), stop=(kc == 7)),
                                reads=[bw] + B_h[st5 * 4:st5 * 4 + 4], writes=[bp])
                        if st5 == 0:
                            S.op("act", lambda pp=pp, pc=pc: nc.scalar.copy(out=pc[:, 2:258], in_=pp[:, 0:256]),
                                 reads=[bp], writes=[bpc])
                            S.op("act", lambda pp=pp, pc=pc: nc.scalar.copy(out=pc[:, 260:516], in_=pp[:, 256:512]),
                                 reads=[bp], writes=[bpc])
                        else:
                            c0 = 518 + 512 * (st5 - 1)
                            S.op("act", lambda pp=pp, pc=pc, c0=c0: nc.scalar.copy(out=pc[:, c0:c0 + 512], in_=pp),
                                 reads=[bp], writes=[bpc])
                    dgt, bdg = dgR.next()
                    for j in range(5):
                        S.op("dve", lambda dgt=dgt, j=j, cch=cch: nc.vector.tensor_scalar(
                            out=dgt[:, j, :], in0=ident_f, scalar1=convw[:, cch, j:j + 1], scalar2=None, op0=ALU.mult),
                            reads=[B_const, B_par], writes=[bdg])
                    dst, Bd = [(qT, B_q), (kT, B_k), (vT, B_v)][blk]
                    for (t0, n_, c0) in CT:
                        pp, bp = psum()
                        for j in range(5):
                            S.op("pe", lambda pp=pp, j=j, dgt=dgt, pc=pc, c0=c0, n_=n_: nc.tensor.matmul(
                                pp[:, 0:n_], lhsT=dgt[:, j, :], rhs=pc[:, c0 + j - 2:c0 + j - 2 + n_],
                                start=(j == 0), stop=(j == 4)), reads=[bdg, bpc], writes=[bp])
                        S.op("act", lambda pp=pp, dst=dst, ch=ch, t0=t0, n_=n_: nc.scalar.activation(
                            out=dst[:, ch, t0:t0 + n_], in_=pp[:, 0:n_], func=AF.Silu),
                            reads=[bp], writes=[Bd[ch][c] for c in chunks_of(t0, n_)])

            wt, bw = load_wblk(3 * 512, 512)
            for c in range(NCH):
                pp, bp = psum()
                for kc in range(8):
                    S.op("pe", lambda pp=pp, kc=kc, c=c, wt=wt: nc.tensor.matmul(
                        pp, lhsT=hT[:, kc, c * 128:(c + 1) * 128], rhs=wt[:, kc, :], start=(kc == 0), stop=(kc == 7)),
                        reads=[bw, B_h[c]], writes=[bp])
                S.op("act", lambda pp=pp, c=c: nc.scalar.activation(out=zs[:, c, :], in_=pp, func=AF.Silu),
                     reads=[bp], writes=[B_z[c]])

            wt16, bw16 = load_wblk(3072, 16)
            pl, bpl = psum()
            for c in range(NCH):
                for kc in range(8):
                    S.op("pe", lambda kc=kc, c=c: nc.tensor.matmul(
                        pl[:, c * 16:(c + 1) * 16], lhsT=hT[:, kc, c * 128:(c + 1) * 128], rhs=wt16[:, kc, 0:16],
                        start=(kc == 0), stop=(kc == 7)), reads=[bw16, B_h[c]], writes=[bpl])
            S.op("dve", lambda: nc.vector.tensor_copy(out=lg[:].rearrange("p c k -> p (c k)"), in_=pl[:, 0:NCH * 16]),
                 reads=[bpl], writes=[B_lg])

            C1 = 0.7978845608028654
            C2 = 0.044715

            def gelu2(pp, bp, out_ap, bout, n_=512):
                S.op("act", lambda: nc.scalar.activation(out=out_ap, in_=pp[:, 0:n_], func=AF.Gelu_apprx_tanh), reads=[bp], writes=[bout])

            wtu, bwu = load_wblk(4 * 512, 512)
            wtv, bwv = load_wblk(5 * 512, 512)
            for st5 in range(5):
                for ch in range(4):
                    pp, bp = psum()
                    for kc in range(8):
                        S.op("pe", lambda pp=pp, kc=kc, ch=ch, st5=st5: nc.tensor.matmul(
                            pp, lhsT=wtu[:, kc, ch * 128:(ch + 1) * 128], rhs=hT[:, kc, st5 * 512:(st5 + 1) * 512],
                            start=(kc == 0), stop=(kc == 7)), reads=[bwu] + B_h[st5 * 4:st5 * 4 + 4], writes=[bp])
                    gelu2(pp, bp, us[:, ch, :], B_us)
                def sgu_a(c):
                    pp, bp = psum()
                    for kc in range(8):
                        S.op("pe", lambda pp=pp, kc=kc, c=c: nc.tensor.matmul(
                            pp, lhsT=hT[:, kc, c * 128:(c + 1) * 128], rhs=wtv[:, kc, :], start=(kc == 0), stop=(kc == 7)),
                            reads=[bwv, B_h[c]], writes=[bp])
                    vg, bvg = tAR.next()
                    gelu2(pp, bp, vg, bvg)
                    return vg, bvg

                cur_sgu = sgu_a(st5 * 4)
                for sub in range(4):
                    c = st5 * 4 + sub
                    nxt_sgu = sgu_a(c + 1) if sub + 1 < 4 else None
                    vg, bvg = cur_sgu
                    cur_sgu = nxt_sgu
                    stt_, bst = stR.next()
                    S.op("dve", lambda vg=vg, stt_=stt_: nc.vector.bn_stats(out=stt_[:, 0:6], in_=vg), reads=[bvg], writes=[bst])
                    S.op("dve", lambda stt_=stt_: nc.vector.bn_aggr(out=stt_[:, 6:8], in_=stt_[:, 0:6]), reads=[bst], writes=[bst])
                    S.op("pool", lambda stt_=stt_: nc.gpsimd.tensor_scalar(out=stt_[:, 0:1], in0=stt_[:, 7:8], scalar1=EPS,
                                                                         scalar2=None, op0=ALU.add), reads=[bst], writes=[bst])
                    S.op("pool", lambda stt_=stt_: nc.gpsimd.tensor_tensor(out=stt_[:, 1:2], in0=stt_[:, 0:1], in1=cols[:, 1:2],
                                                                         op=ALU.pow), reads=[bst, B_const], writes=[bst])
                    vn, bvn = tBR.next()
                    S.op("dve", lambda vg=vg, vn=vn, stt_=stt_: nc.vector.tensor_scalar(
                        out=vn, in0=vg, scalar1=stt_[:, 6:7], scalar2=stt_[:, 1:2], op0=ALU.subtract, op1=ALU.mult),
                        reads=[bvg, bst], writes=[bvn])
                    pm, bm = psum()
                    for g in range(4):
                        S.op("pe", lambda pm=pm, g=g, vn=vn: nc.tensor.matmul(
                            pm[:, g * 128:(g + 1) * 128], lhsT=vn[:, g * 128:(g + 1) * 128], rhs=sguw[:, g, :],
                            start=True, stop=True), reads=[bvn, B_par], writes=[bm])
                    mx, bmx = tAR.next()
                    for g in range(4):
                        S.op("dve", lambda pm=pm, g=g, mx=mx: nc.vector.scalar_tensor_tensor(
                            out=mx[:, g * 128:(g + 1) * 128], in0=pm[:, g * 128:(g + 1) * 128], scalar=lngT[:, g:g + 1],
                            in1=BB[:, g * 128:(g + 1) * 128], op0=ALU.mult, op1=ALU.add),
                            reads=[bm, B_par], writes=[bmx])
                    S.op("dve", lambda mx=mx, c=c, sub=sub: nc.vector.scalar_tensor_tensor(
                        out=ybT[:, :, c * 128:(c + 1) * 128], in0=mx.rearrange("p (g t) -> p g t", g=4), scalar=1.0,
                        in1=us[:, :, sub * 128:(sub + 1) * 128], op0=ALU.mult, op1=ALU.mult),
                        reads=[bmx, B_us], writes=[B_yb[c]])

            S.op("act", lambda: nc.scalar.activation(out=beta[:], in_=lg[:, :, 0:8], func=AF.Tanh, scale=0.5),
                 reads=[B_lg], writes=[B_coef])
            S.op("dve", lambda: nc.vector.tensor_scalar(out=beta[:], in0=beta[:], scalar1=0.5, scalar2=0.5, op0=ALU.mult,
                                                        op1=ALU.add), reads=[B_coef], writes=[B_coef])
            S.op("dve", lambda: nc.vector.tensor_scalar(out=negb[:], in0=beta[:], scalar1=-1.0, scalar2=None, op0=ALU.mult),
                 reads=[B_coef], writes=[B_coef])
            S.op("dve", lambda: nc.vector.tensor_tensor(out=gg[:], in0=lg[:, :, 8:16],
                                                        in1=adt_b[:, 8:16].unsqueeze(1).to_broadcast([128, NCH, 8]), op=ALU.add),
                 reads=[B_lg, B_par], writes=[B_coef])
            S.op("dve", lambda: nc.vector.tensor_scalar(out=negg[:], in0=gg[:], scalar1=-1.0, scalar2=None, op0=ALU.mult),
                 reads=[B_coef], writes=[B_coef])
            S.op("dve", lambda: nc.vector.tensor_tensor(out=negg[:], in0=negg[:], in1=gg[:], op=ALU.max),
                 reads=[B_coef], writes=[B_coef])
            S.op("act", lambda: nc.scalar.activation(out=negg[:], in_=negg[:], func=AF.Exp, scale=-1.0),
                 reads=[B_coef], writes=[B_coef])
            S.op("act", lambda: nc.scalar.activation(out=negg[:], in_=negg[:], func=AF.Ln, bias=cols[:, 3:4], scale=1.0),
                 reads=[B_coef, B_const], writes=[B_coef])
            S.op("dve", lambda: nc.vector.tensor_scalar(out=gg[:], in0=gg[:], scalar1=0.0, scalar2=None, op0=ALU.max),
                 reads=[B_coef], writes=[B_coef])
            S.op("dve", lambda: nc.vector.tensor_tensor(out=gg[:], in0=gg[:], in1=negg[:], op=ALU.add),
                 reads=[B_coef], writes=[B_coef])
            S.op("dve", lambda: nc.vector.tensor_tensor(out=gg[:], in0=gg[:],
                                                        in1=nea_b[:].unsqueeze(1).to_broadcast([128, NCH, 8]), op=ALU.mult),
                 reads=[B_coef, B_par], writes=[B_coef])
            S.op("dve", lambda: nc.vector.tensor_scalar(out=negg[:], in0=gg[:], scalar1=-1.0, scalar2=None, op0=ALU.mult),
                 reads=[B_coef], writes=[B_coef])

            for (dst, Bd, bias_col) in [(qT, B_q, 2), (kT, B_k, 5)]:
                for h in range(4):
                    for st5 in range(5):
                        bsl = Bd[h][st5 * 4:st5 * 4 + 4]
                        sl = dst[:, h, st5 * 512:(st5 + 1) * 512]
                        sq_, bsq = tBR.next()
                        S.op("pool", lambda sq_=sq_, sl=sl: nc.gpsimd.tensor_tensor(out=sq_, in0=sl, in1=sl, op=ALU.mult),
                             reads=bsl, writes=[bsq])
                        pp, bp = psum()
                        S.op("pe", lambda pp=pp, sq_=sq_: nc.tensor.matmul(pp, lhsT=ones_b, rhs=sq_, start=True, stop=True),
                             reads=[bsq, B_const], writes=[bp])
                        rs_, brs = tAR.next()
                        S.op("act", lambda pp=pp, rs_=rs_: nc.scalar.activation(out=rs_, in_=pp, func=AF.Ln, bias=cols[:, 0:1],
                                                                                scale=1.0), reads=[bp, B_const], writes=[brs])
                        S.op("act", lambda rs_=rs_, bias_col=bias_col: nc.scalar.activation(
                            out=rs_, in_=rs_, func=AF.Exp, bias=cols[:, bias_col:bias_col + 1], scale=-0.5),
                            reads=[brs, B_const], writes=[brs])
                        S.op("dve", lambda sl=sl, rs_=rs_: nc.vector.tensor_tensor(out=sl, in0=sl, in1=rs_, op=ALU.mult),
                             reads=[brs] + bsl, writes=bsl)
            S.barrier()
        S.barrier()

        S.mark('a2')
        if debug:
            for nm, t_, shp in [("qT", qT, [128, 4, NT]), ("kT", kT, [128, 4, NT]), ("vT", vT, [128, 4, NT]),
                                ("ybT", ybT, [128, 4, NT]), ("zs", zs, [128, NCH, 512]), ("hT", hT, [128, 8, NT])]:
                dd = dout("dbg_" + nm, shp, BF16)
                S.dma("sp", lambda dd=dd, t_=t_: nc.sync.dma_start(out=dd, in_=t_[:]), stream="dbg")
            for nm, t_, shp in [("beta", beta, [128, NCH, 8]), ("gg", gg, [128, NCH, 8])]:
                dd = dout("dbg_" + nm, shp, F32)
                S.dma("sp", lambda dd=dd, t_=t_: nc.sync.dma_start(out=dd, in_=t_[:]), stream="dbg")
            S.barrier()

        wc_jobs = []
        for (nm, src, dst, rows) in [("wo", w_out_d, wo_s, D), ("wd", wd_d, wd_s, DFF), ("wg", wg_d, wg_s, NF * 128), ("wu", wu_d, wu_s, NF * 128)]:
            r0 = 0
            while r0 < rows:
                r1 = min(rows, r0 + 512)
                bq = S.buf(f"wscr_{nm}_{r0}")
                B_wscr[nm].append(bq)
                wc_jobs.append((src, dst, r0, r1, bq))
                r0 = r1

        def issue_wc(k):
            for _ in range(k):
                if wc_jobs:
                    src, dst, r0, r1, bq = wc_jobs.pop(0)
                    S.dma("pool", lambda src=src, dst=dst, r0=r0, r1=r1: nc.gpsimd.dma_start(out=dst[r0:r1, :], in_=src[r0:r1, :]),
                          writes=[bq], stream="wc")

        with contextlib.ExitStack() as stB:
            gn_b = sb("gn_b", [128, 512], stack=stB)
            B_gn = S.buf("gn")
            for h in range(4):
                S.dma("sp", lambda h=h: nc.sync.dma_start(out=gn_b[:, h * 128:(h + 1) * 128],
                                                          in_=dng_d.partition_broadcast(128)[:, 0, :]),
                      writes=[B_gn], stream="c0")
            hflat = hT[:].rearrange("p a b -> p (a b)")
            O = hflat[:, 0:NCH * 512].rearrange("p (c f) -> p c f", c=NCH)
            B_O = [S.buf(f"O{c}") for c in range(NCH)]
            spare = [NCH * 512]

            def carve(name, n):
                aps = []
                for i in range(n):
                    aps.append(hflat[:, spare[0]:spare[0] + 512])
                    spare[0] += 512
                assert spare[0] <= 8 * NT
                return Ring(S, aps, name)

            def t512(name, dt, n):
                ts = [sb(f"{name}{i}", [128, 512], dt, stack=stB) for i in range(n)]
                return Ring(S, [t[:] for t in ts], name)

            _car = carve("chainc", 10)
            _all = t512("chaina", BF16, 11)
            chR = Ring(S, [it[0] for it in _car.items] + [it[0] for it in _all.items], "chain")
            M0R = t512("M0", BF16, 2)
            MT0R = t512("MT0", BF16, 2)
            vbR = carve("vbeta", 4)
            def mix(name, ncar, nall):
                a = carve(name + "c", ncar)
                b = t512(name + "a", BF16, nall)
                return Ring(S, [it[0] for it in a.items] + [it[0] for it in b.items], name)
            wTR = mix("wT", 2, 2)
            qkTR = mix("qkT", 2, 2)
            kdR = mix("kdec", 2, 2)
            f32R = t512("f32t", F32, 2)
            ER = t512("E", BF16, 2)
            uR = t512("u", BF16, 4)
            vnR = t512("vn", BF16, 2)
            otR = t512("ot", F32, 2)
            Sst = [sb(f"Sst{d}", [128, 512], stack=stB)[:] for d in range(2)]
            Sbf = [sb(f"Sbf{d}", [128, 512], BF16, stack=stB)[:] for d in range(2)]
            B_S = [S.buf("S0"), S.buf("S1")]
            B_Sb = [S.buf("Sb0"), S.buf("Sb1")]

            pg, bpg = psum()
            for c in range(NCH):
                for d in range(2):
                    S.op("pe", lambda c=c, d=d: nc.tensor.matmul(pg[:, c * 16 + d * 4:c * 16 + d * 4 + 4], lhsT=U[d],
                                                                 rhs=gg[:, c, d * 4:d * 4 + 4], start=True, stop=True),
                         reads=[B_coef, B_const], writes=[bpg])
                S.op("pe", lambda c=c: nc.tensor.matmul(pg[:, c * 16 + 8:c * 16 + 16], lhsT=ones_f, rhs=gg[:, c, :],
                                                        start=True, stop=True), reads=[B_coef, B_const], writes=[bpg])
            S.op("dve", lambda: nc.vector.tensor_copy(out=gcl[:].rearrange("p c k -> p (c k)"), in_=pg[:, 0:NCH * 16]),
                 reads=[bpg], writes=[B_coef])
            S.op("act", lambda: nc.scalar.activation(out=egc[:], in_=gcl[:, :, 0:8], func=AF.Exp), reads=[B_coef], writes=[B_coef])
            S.op("act", lambda: nc.scalar.activation(out=egl[:], in_=gcl[:, :, 8:16], func=AF.Exp), reads=[B_coef], writes=[B_coef])
            S.op("dve", lambda: nc.vector.tensor_tensor(out=ekd[:], in0=gcl[:, :, 8:16], in1=gcl[:, :, 0:8], op=ALU.subtract),
                 reads=[B_coef], writes=[B_coef])
            S.op("act", lambda: nc.scalar.activation(out=ekd[:], in_=ekd[:], func=AF.Exp), reads=[B_coef], writes=[B_coef])
            S.op("dve", lambda: nc.vector.tensor_tensor(out=bege[:], in0=beta[:], in1=egc[:], op=ALU.mult),
                 reads=[B_coef], writes=[B_coef])

            S.mark('b0')
            def bc(t, c, d):
                return t[:, c, d * 4:d * 4 + 4].unsqueeze(2).to_broadcast([128, 4, 128])

            def v3(ap):
                return ap.rearrange("p (h j) -> p h j", h=4)

            def per_head(engname, out512, in_fn, coef, c, d, op, reads, writes):
                for h in range(4):
                    hs = slice(h * 128, (h + 1) * 128)
                    sc = coef[:, c, d * 4 + h:d * 4 + h + 1]
                    if engname == "pool":
                        if op == "mult":
                            S.op("pool", lambda hs=hs, sc=sc, h=h: nc.gpsimd.tensor_scalar(
                                out=out512[:, hs], in0=in_fn(h), scalar1=sc, scalar2=0.0, op0=ALU.mult, op1=ALU.add),
                                reads=reads, writes=writes)
                        else:
                            S.op("pool", lambda hs=hs, sc=sc, h=h: nc.gpsimd.tensor_scalar(
                                out=out512[:, hs], in0=in_fn(h), scalar1=sc, scalar2=1.0, op0=ALU.add, op1=ALU.mult),
                                reads=reads, writes=writes)
                    else:
                        S.op("dve", lambda hs=hs, sc=sc, h=h: nc.vector.tensor_scalar(
                            out=out512[:, hs], in0=in_fn(h), scalar1=sc, scalar2=None,
                            op0=(ALU.mult if op == "mult" else ALU.add)), reads=reads, writes=writes)

            def prep_shared(c, d):
                cs_ = slice(c * 128, (c + 1) * 128)
                vb_, bvb = vbR.next()
                kb_, bkb = vbR.next()
                kd_, bkd = kdR.next()
                pkk, bkk = psum()
                pqk, bqk = psum()
                for h in range(4):
                    S.op("pe", lambda h=h: nc.tensor.matmul(pkk[:, h * 128:(h + 1) * 128], lhsT=kT[:, h, cs_], rhs=kT[:, h, cs_],
                                                            start=True, stop=True), reads=[B_k[h][c]], writes=[bkk])
                for h in range(4):
                    S.op("pe", lambda h=h: nc.tensor.matmul(pqk[:, h * 128:(h + 1) * 128], lhsT=qT[:, h, cs_], rhs=kT[:, h, cs_],
                                                            start=True, stop=True), reads=[B_k[h][c], B_q[h][c]], writes=[bqk])
                return dict(vb=vb_, bvb=bvb, kb=kb_, bkb=bkb, kd=kd_, bkd=bkd, pkk=pkk, bkk=bkk, pqk=pqk, bqk=bqk)

            def gen_kv(c, d, sh):
                cs_ = slice(c * 128, (c + 1) * 128)
                kb_, bkb, kd_, bkd, vb_, bvb = sh["kb"], sh["bkb"], sh["kd"], sh["bkd"], sh["vb"], sh["bvb"]
                ptk, bptk = psum()
                ptkb = ptk.bitcast(BF16)
                for h in range(4):
                    S.op("pe", lambda h=h: nc.tensor.transpose(ptkb[:, h * 128:(h + 1) * 128], kT[:, h, cs_], ident_b),
                         reads=[B_k[h][c], B_const], writes=[bptk])
                for h in range(4):
                    hs = slice(h * 128, (h + 1) * 128)
                    k_ = d * 4 + h
                    S.op("act", lambda hs=hs, k_=k_: nc.scalar.activation(out=kb_[:, hs], in_=ptkb[:, hs], func=AF.Identity, scale=bege[:, c, k_:k_ + 1]),
                         reads=[bptk, B_coef], writes=[bkb])
                    S.op("act", lambda hs=hs, k_=k_: nc.scalar.activation(out=kd_[:, hs], in_=ptkb[:, hs], func=AF.Identity, scale=ekd[:, c, k_:k_ + 1]),
                         reads=[bptk, B_coef], writes=[bkd])
                yield
                ptv, bptv = psum()
                ptvb = ptv.bitcast(BF16)
                for h in range(4):
                    S.op("pe", lambda h=h: nc.tensor.transpose(ptvb[:, h * 128:(h + 1) * 128], vT[:, h, cs_], ident_b),
                         reads=[B_v[h][c], B_const], writes=[bptv])
                for h in range(4):
                    hs = slice(h * 128, (h + 1) * 128)
                    k_ = d * 4 + h
                    S.op("act", lambda hs=hs, k_=k_: nc.scalar.activation(out=vb_[:, hs], in_=ptvb[:, hs], func=AF.Identity, scale=beta[:, c, k_:k_ + 1]),
                         reads=[bptv, B_coef], writes=[bvb])
                yield

            def prep_dir(c, d, sh, st):
                ug, bug = f32R.next()
                per_head("pool", ug, lambda h: U[d], negg, c, d, "mult", [B_const, B_coef], [bug])
                gm, bgm = f32R.next()
                per_head("dve", gm, lambda h: MN[d], gcl, c, d, "add", [B_const, B_coef], [bgm])
                pd, bpd = psum()
                S.op("pe", lambda: nc.tensor.matmul(pd, lhsT=ones_f, rhs=ug, start=True, stop=False), reads=[bug, B_const], writes=[bpd])
                S.op("pe", lambda: nc.tensor.matmul(pd, lhsT=ident_f, rhs=gm, start=False, stop=True), reads=[bgm, B_const], writes=[bpd])
                E, bE = ER.next()
                S.op("act", lambda: nc.scalar.activation(out=E, in_=pd, func=AF.Exp), reads=[bpd], writes=[bE])
                yield
                M, bM = M0R.next()
                for h in range(4):
                    hs = slice(h * 128, (h + 1) * 128)
                    k_ = d * 4 + h
                    S.op("dve", lambda hs=hs, k_=k_: nc.vector.scalar_tensor_tensor(
                        out=M[:, hs], in0=sh["pkk"][:, hs], scalar=negb[:, c, k_:k_ + 1], in1=E[:, hs], op0=ALU.mult, op1=ALU.mult),
                        reads=[sh["bkk"], bE, B_coef], writes=[bM])
                qkm, bqkm = chR.next()
                S.op("dve", lambda: nc.vector.tensor_tensor(out=qkm, in0=sh["pqk"], in1=E, op=ALU.mult), reads=[sh["bqk"], bE], writes=[bqkm])
                yield
                pt, bpt = psum()
                ptb = pt.bitcast(BF16)
                for h in range(4):
                    S.op("pe", lambda h=h: nc.tensor.transpose(ptb[:, h * 128:(h + 1) * 128], M[:, h * 128:(h + 1) * 128], ident_b),
                         reads=[bM, B_const], writes=[bpt])
                MT, bMT = MT0R.next()
                S.op("act", lambda: nc.scalar.copy(out=MT, in_=ptb[:, 0:512]), reads=[bpt], writes=[bMT])
                pt2, bpt2 = psum()
                ptb2 = pt2.bitcast(BF16)
                for h in range(4):
                    S.op("pe", lambda h=h: nc.tensor.transpose(ptb2[:, h * 128:(h + 1) * 128], qkm[:, h * 128:(h + 1) * 128], ident_b),
                         reads=[bqkm, B_const], writes=[bpt2])
                qkT_, bqkT = qkTR.next()
                S.op("act", lambda: nc.scalar.copy(out=qkT_, in_=ptb2[:, 0:512]), reads=[bpt2], writes=[bqkT])
                st.update(M0=M, bM0=bM, MT0=MT, bMT0=bMT, qkT=qkT_, bqkT=bqkT, d=d)

            def masked(src, bsrc, m, engname):
                o, bo = chR.next()
                mk = MASKS[m].unsqueeze(1).to_broadcast([128, 4, 128])
                if engname == "pool":
                    S.op("pool", lambda: nc.gpsimd.tensor_tensor(out=v3(o), in0=v3(src), in1=mk, op=ALU.mult), reads=[bsrc, B_const], writes=[bo])
                else:
                    S.op("dve", lambda: nc.vector.tensor_tensor(out=v3(o), in0=v3(src), in1=mk, op=ALU.mult), reads=[bsrc, B_const], writes=[bo])
                return o, bo

            def mm4(lhs, blhs, rhs, brhs, acc=None, bacc=None):
                pp, bp = psum()
                if acc is not None:
                    S.op("pe", lambda: nc.tensor.matmul(pp, lhsT=ident_b, rhs=acc, start=True, stop=False), reads=[bacc, B_const], writes=[bp])
                for h in range(4):
                    hs = slice(h * 128, (h + 1) * 128)
                    S.op("pe", lambda hs=hs, h=h: nc.tensor.matmul(pp[:, hs], lhsT=lhs[:, hs], rhs=rhs[:, hs], start=(acc is None),
                                                                   stop=(h == 3 or acc is None)), reads=[blhs, brhs], writes=[bp])
                return pp, bp

            def evac(pp, bp, engname):
                o, bo = chR.next()
                if engname == "act":
                    S.op("act", lambda: nc.scalar.copy(out=o, in_=pp), reads=[bp], writes=[bo])
                else:
                    S.op("dve", lambda: nc.vector.tensor_copy(out=o, in_=pp), reads=[bp], writes=[bo])
                return o, bo

            def evac_add(pp, bp, acc, bacc):
                o, bo = chR.next()
                S.op("dve", lambda: nc.vector.tensor_tensor(out=o, in0=pp, in1=acc, op=ALU.add), reads=[bp, bacc], writes=[bo])
                return o, bo

            def transp(src, bsrc):
                pt, bpt = psum()
                ptb = pt.bitcast(BF16)
                for h in range(4):
                    S.op("pe", lambda h=h: nc.tensor.transpose(ptb[:, h * 128:(h + 1) * 128], src[:, h * 128:(h + 1) * 128], ident_b),
                         reads=[bsrc, B_const], writes=[bpt])
                o, bo = chR.next()
                S.op("act", lambda: nc.scalar.copy(out=o, in_=ptb[:, 0:512]), reads=[bpt], writes=[bo])
                return o, bo

            def chain_stages(st):
                M0, bM0, MT0, bMT0 = st["M0"], st["bM0"], st["MT0"], st["bMT0"]
                Mk, bMk = masked(M0, bM0, 0, "pool")
                MTk, bMTk = masked(MT0, bMT0, 0, "dve")
                PT, bPT = chR.next()
                S.op("dve", lambda PT=PT, MTk=MTk: nc.vector.tensor_tensor(
                    out=v3(PT), in0=v3(MTk), in1=ident_b.unsqueeze(1).to_broadcast([128, 4, 128]), op=ALU.add),
                    reads=[bMTk, B_const], writes=[bPT])
                yield
                for lev in range(3):
                    pM, bpM = mm4(MTk, bMTk, Mk, bMk)
                    nM, bnM = evac(pM, bpM, "act")
                    if lev < 2:
                        pMT, bpMT = mm4(Mk, bMk, MTk, bMTk)
                        nMT, bnMT = evac(pMT, bpMT, "act")
                    yield
                    pP, bpP = mm4(nM, bnM, PT, bPT)
                    PT, bPT = evac_add(pP, bpP, PT, bPT)
                    Mk, bMk = nM, bnM
                    if lev < 2:
                        MTk, bMTk = nMT, bnMT
                    yield
                Tt, bTt = PT, bPT
                pt, bpt = psum()
                ptb = pt.bitcast(BF16)
                for h in range(4):
                    S.op("pe", lambda h=h, Tt=Tt, ptb=ptb: nc.tensor.transpose(ptb[:, h * 128:(h + 1) * 128], Tt[:, h * 128:(h + 1) * 128], ident_b),
                         reads=[bTt, B_const], writes=[bpt])
                T, bT = chR.next()
                S.op("act", lambda T=T, ptb=ptb: nc.scalar.copy(out=T, in_=ptb[:, 0:512]), reads=[bpt], writes=[bT])
                Moff, bMoff = masked(M0, bM0, 1, "pool")
                yield
                for li in range(3):
                    last = (li == 2)
                    pX, bpX = mm4(Moff, bMoff, Tt, bTt)
                    X, bX = evac(pX, bpX, "act")
                    if li > 0:
                        T, bT = transp(Tt, bTt)
                    yield
                    pT2, bpT2 = mm4(T, bT, X, bX)
                    Tt, bTt = evac_add(pT2, bpT2, Tt, bTt)
                    if not last:
                        Moff, bMoff = masked(M0, bM0, 2 + li, "pool")
                    yield
                st["PT"], st["bPT"] = Tt, bTt

            def prep_finish(c, d, sh, st):
                Tt, bTt = st["PT"], st["bPT"]
                vb_, bvb, kb_, bkb, kd_, bkd = sh["vb"], sh["bvb"], sh["kb"], sh["bkb"], sh["kd"], sh["bkd"]
                pu, bpu = psum()
                pw, bpw = psum()
                for h in range(4):
                    hs = slice(h * 128, (h + 1) * 128)
                    S.op("pe", lambda hs=hs: nc.tensor.matmul(pu[:, hs], lhsT=Tt[:, hs], rhs=vb_[:, hs], start=True, stop=True),
                         reads=[bTt, bvb], writes=[bpu])
                for h in range(4):
                    hs = slice(h * 128, (h + 1) * 128)
                    S.op("pe", lambda hs=hs: nc.tensor.matmul(pw[:, hs], lhsT=kb_[:, hs], rhs=Tt[:, hs], start=True, stop=True),
                         reads=[bTt, bkb], writes=[bpw])
                u_, bu = uR.next()
                wT_, bwT = wTR.next()
                S.op("act", lambda: nc.scalar.copy(out=u_, in_=pu), reads=[bpu], writes=[bu])
                S.op("act", lambda: nc.scalar.copy(out=wT_, in_=pw), reads=[bpw], writes=[bwT])
                st.update(u=u_, bu=bu, wT=wT_, bwT=bwT, kd=kd_, bkd=bkd)
                if c in (0, 1) and (c == d):
                    dbg(f"Tt{d}", Tt, [128, 512], BF16, [bTt])
                    dbg(f"u{d}", u_, [128, 512], BF16, [bu])
                    dbg(f"wT{d}", wT_, [128, 512], BF16, [bwT])
                    dbg(f"kd{d}", kd_, [128, 512], BF16, [bkd])
                    dbg(f"qkT{d}", st['qkT'], [128, 512], BF16, [st['bqkT']])

            def scan_step(c, d, st, first_dir_for_chunk):
                cs_ = slice(c * 128, (c + 1) * 128)
                pv, bpv = psum()
                for h in range(4):
                    hs = slice(h * 128, (h + 1) * 128)
                    S.op("pe", lambda hs=hs: nc.tensor.matmul(pv[:, hs], lhsT=st["wT"][:, hs], rhs=Sbf[d][:, hs], start=True, stop=True),
                         reads=[st["bwT"], B_Sb[d]], writes=[bpv])
                vn, bvn = vnR.next()
                S.op("dve", lambda: nc.vector.tensor_tensor(out=vn, in0=st["u"], in1=pv, op=ALU.subtract), reads=[st["bu"], bpv], writes=[bvn])
                yield
                po1, bpo1 = psum()
                po2, bpo2 = psum()
                pS, bpS = psum()
                for h in range(4):
                    hs = slice(h * 128, (h + 1) * 128)
                    S.op("pe", lambda hs=hs, h=h: nc.tensor.matmul(po1[:, hs], lhsT=qT[:, h, cs_], rhs=Sbf[d][:, hs], start=True, stop=True),
                         reads=[B_q[h][c], B_Sb[d]], writes=[bpo1])
                for h in range(4):
                    hs = slice(h * 128, (h + 1) * 128)
                    S.op("pe", lambda hs=hs: nc.tensor.matmul(po2[:, hs], lhsT=st["qkT"][:, hs], rhs=vn[:, hs], start=True, stop=True),
                         reads=[st["bqkT"], bvn], writes=[bpo2])
                for h in range(4):
                    hs = slice(h * 128, (h + 1) * 128)
                    S.op("pe", lambda hs=hs: nc.tensor.matmul(pS[:, hs], lhsT=st["kd"][:, hs], rhs=vn[:, hs], start=True, stop=True),
                         reads=[st["bkd"], bvn], writes=[bpS])
                ot, bot = otR.next()
                per_head("dve", ot, lambda h: po1[:, h * 128:(h + 1) * 128], egc, c, d, "mult", [bpo1, B_coef], [bot])
                if first_dir_for_chunk:
                    S.op("dve", lambda: nc.vector.tensor_tensor(out=O[:, c, :], in0=ot, in1=po2, op=ALU.add),
                         reads=[bot, bpo2], writes=[B_O[c]])
                else:
                    S.op("dve", lambda: nc.vector.tensor_tensor(out=ot, in0=ot, in1=po2, op=ALU.add), reads=[bot, bpo2], writes=[bot])
                    S.op("pool", lambda: nc.gpsimd.tensor_tensor(out=ot, in0=ot, in1=O[:, c, :], op=ALU.add),
                         reads=[bot, B_O[c]], writes=[bot])
                for h in range(4):
                    hs = slice(h * 128, (h + 1) * 128)
                    S.op("dve", lambda hs=hs, h=h: nc.vector.scalar_tensor_tensor(
                        out=Sst[d][:, hs], in0=Sst[d][:, hs], scalar=egl[:, c, d * 4 + h:d * 4 + h + 1], in1=pS[:, hs],
                        op0=ALU.mult, op1=ALU.add), reads=[B_S[d], bpS, B_coef], writes=[B_S[d]])
                S.op("act", lambda: nc.scalar.copy(out=Sbf[d], in_=Sst[d]), reads=[B_S[d]], writes=[B_Sb[d]])
                yield
                if c in (0, 1) and (c == d):
                    dbg(f"vn{d}", vn, [128, 512], BF16, [bvn])
                    dbg(f"S{d}", Sst[d], [128, 512], F32, [B_S[d]])
                    dbg(f"O{d}", O[:, c, :], [128, 512], BF16, [B_O[c]])
                if not first_dir_for_chunk:
                    gate_chunk(c, ot, bot)

            gtmp = [sb(f"gtmp{i}", [128, 512], stack=stB) for i in range(1)]
            gR = Ring(S, [t[:] for t in gtmp], "gtmp")
            gsm = [sb(f"gsm{i}", [128, 8], stack=stB) for i in range(2)]
            gsR = Ring(S, [t[:] for t in gsm], "gsm")
            yab = [sb(f"yab{i}", [128, 512], BF16, stack=stB) for i in range(1)]
            yaR = Ring(S, [t[:] for t in yab], "yab")

            def gate_chunk(c, osum, bos):
                t1, b1 = gR.next()
                S.op("pool", lambda: nc.gpsimd.tensor_tensor(out=t1, in0=osum, in1=osum, op=ALU.mult), reads=[bos], writes=[b1])
                sm, bsm = gsR.next()
                S.op("dve", lambda: nc.vector.tensor_reduce(out=sm[:, 0:4], in_=v3(t1), axis=AX.X, op=ALU.add), reads=[b1], writes=[bsm])
                S.op("pool", lambda: nc.gpsimd.tensor_scalar(out=sm[:, 0:4], in0=sm[:, 0:4], scalar1=1.0 / 128, scalar2=EPS,
                                                             op0=ALU.mult, op1=ALU.add), reads=[bsm], writes=[bsm])
                S.op("pool", lambda: nc.gpsimd.tensor_tensor(out=sm[:, 4:8], in0=sm[:, 0:4], in1=mh4[:], op=ALU.pow),
                     reads=[bsm, B_const], writes=[bsm])
                for h in range(4):
                    S.op("dve", lambda h=h: nc.vector.tensor_scalar(out=t1[:, h * 128:(h + 1) * 128], in0=osum[:, h * 128:(h + 1) * 128],
                                                                    scalar1=sm[:, 4 + h:5 + h], scalar2=None, op0=ALU.mult),
                         reads=[bos, bsm], writes=[b1])
                S.op("pool", lambda: nc.gpsimd.tensor_tensor(out=t1, in0=t1, in1=gn_b[:], op=ALU.mult), reads=[b1, B_gn], writes=[b1])
                ya, bya = yaR.next()
                S.op("dve", lambda: nc.vector.tensor_tensor(out=ya, in0=t1, in1=zs[:, c, :], op=ALU.mult), reads=[b1, B_z[c]], writes=[bya])
                pt, bpt = psum()
                ptb = pt.bitcast(BF16)
                for h in range(4):
                    S.op("pe", lambda h=h: nc.tensor.transpose(ptb[:, h * 128:(h + 1) * 128], ya[:, h * 128:(h + 1) * 128], ident_b),
                         reads=[bya, B_const], writes=[bpt])
                S.op("act", lambda: nc.scalar.copy(out=qT[:, :, c * 128:(c + 1) * 128], in_=v3(ptb[:, 0:512])),
                     reads=[bpt], writes=[B_q[h][c] for h in range(4)])

            for si, (c0, n, r) in enumerate(SEQS):
                for d in range(2):
                    if si == 2:
                        S.dma("sp", lambda d=d: nc.sync.dma_start(out=Sst[d][:], in_=s0_d[d]), writes=[B_S[d]], stream=f"s0{d}")
                    else:
                        S.op("pool", lambda d=d: nc.gpsimd.memset(Sst[d][:], 0.0), writes=[B_S[d]])
                    S.op("act", lambda d=d: nc.scalar.copy(out=Sbf[d][:], in_=Sst[d][:]), reads=[B_S[d]], writes=[B_Sb[d]])
                pending = []

                def run_rr(gens):
                    alive = [True] * len(gens)
                    while any(alive):
                        for gi in range(len(gens)):
                            if alive[gi]:
                                try:
                                    next(gens[gi])
                                except StopIteration:
                                    alive[gi] = False

                for t in range(n):
                    cf, cb = c0 + t, c0 + n - 1 - t
                    assert cf != cb
                    issue_wc(2)
                    shs = {cf: prep_shared(cf, 0)}
                    shs[cb] = prep_shared(cb, 1)
                    sts = [{}, {}]
                    run_rr([prep_dir(cf, 0, shs[cf], sts[0]), prep_dir(cb, 1, shs[cb], sts[1])])
                    run_rr([chain_stages(sts[0]), chain_stages(sts[1])] + pending + [gen_kv(cf, 0, shs[cf]), gen_kv(cb, 1, shs[cb])])
                    prep_finish(cf, 0, shs[cf], sts[0])
                    prep_finish(cb, 1, shs[cb], sts[1])
                    pending = [scan_step(cf, 0, sts[0], (cf < cb)), scan_step(cb, 1, sts[1], (cf < cb))]
                run_rr(pending)
                if si < 2:
                    for d, od in [(0, sf_d), (1, sb_d)]:
                        S.dma("sp", lambda d=d, od=od, si=si: nc.sync.dma_start(
                            out=od[si].rearrange("h k v -> k h v"), in_=Sst[d][:].rearrange("p (h v) -> p h v", h=4)),
                            reads=[B_S[d]], stream="so")
            issue_wc(100)
            S.barrier()
        S.barrier()
    S.barrier()

    S.mark('b')
    if debug:
        dd = dout("dbg_yaT", [128, 4, NT], BF16)
        S.dma("sp", lambda: nc.sync.dma_start(out=dd, in_=qT[:]), stream="dbg")
        S.barrier()

    with contextlib.ExitStack() as stC:
        wo = sb("wo", [128, 8, D], BF16, stack=stC)
        wd = sb("wd", [128, NF, D], BF16, stack=stC)
        B_wo = S.buf("wo")
        B_wd = S.buf("wd")
        wgu = [sb(f"wgu{i}", [128, 2, D], BF16, stack=stC) for i in range(2)]
        wguR = Ring(S, [t[:] for t in wgu], "wgu")
        wgu_b2 = [S.buf("wgu2_0"), S.buf("wgu2_1")]
        h2T = sb("h2T", [128, 8, 512], BF16, stack=stC)
        B_h2 = [S.buf(f"h2_{i}") for i in range(4)]
        actT = sb("actT", [128, NF, 512], BF16, stack=stC)
        B_act = [S.buf(f"act{f}") for f in range(NF)]
        x1s = [sb(f"x1_{j}", [128, 4, D], stack=stC) for j in range(2)]
        B_x1s = [[S.buf(f"x1_{j}_{i}") for i in range(4)] for j in range(2)]
        gb = sb("gb", [128, 2, D], stack=stC)
        B_gb = S.buf("gb")
        fgb = sb("fgb", [128, D], stack=stC)
        B_fg = S.buf("fgb")
        xh2 = [sb(f"xh2{i}", [128, D], BF16, stack=stC) for i in range(2)]
        xhR2 = Ring(S, [t[:] for t in xh2], "xh2")
        sqR2 = xhR2
        ss2 = [sb(f"ss2{i}", [128, 4], stack=stC) for i in range(4)]
        ssR2 = Ring(S, [t[:] for t in ss2], "ss2")
        sg = [sb(f"sg{i}", [128, 512], stack=stC) for i in range(2)]
        sgR = Ring(S, [t[:] for t in sg], "sg")
        sqJ = Ring(S, [], "sqJ")
        sqJ.items = [(it[0].bitcast(BF16), it[1]) for it in sgR.items]
        tc_ = [sb(f"tc{i}", [128, 512], stack=stC) for i in range(2)]
        tcR = Ring(S, [t[:] for t in tc_], "tc")

        S.dma("sp", lambda: nc.sync.dma_start(out=wo[:], in_=wo_s.rearrange("(kc p) n -> p kc n", p=128)),
              reads=B_wscr["wo"], writes=[B_wo], stream="wo")
        S.dma("sp", lambda: nc.sync.dma_start(out=fgb[:], in_=fg_d.partition_broadcast(128)[:, 0, :]), writes=[B_fg], stream="wo")
        S.dma("sp", lambda: nc.sync.dma_start(out=wd[:], in_=wd_s.rearrange("(fc p) n -> p fc n", p=128)),
              reads=B_wscr["wd"], writes=[B_wd], stream="wd")

        for st5 in range(5):
            r = 0 if st5 == 0 else 1
            x1 = x1s[st5 % 2]
            B_x1 = B_x1s[st5 % 2]
            if st5 in (0, 1):
                for gi, seg in enumerate([2, 5]):
                    S.dma("sp", lambda gi=gi, seg=seg, r=r: nc.sync.dma_start(
                        out=gb[:, gi, :], in_=mod_d[r:r + 1, seg * D:(seg + 1) * D].partition_broadcast(128)[:, 0, :]),
                        reads=B_mblk[seg * 4:seg * 4 + 4], writes=[B_gb], stream="gb")
            for sub in range(4):
                c = st5 * 4 + sub
                S.dma("sp", lambda sub=sub, c=c, x1=x1: nc.sync.dma_start(out=x1[:, sub, :], in_=x_d[c * 128:(c + 1) * 128, :]),
                      writes=[B_x1[sub]], stream=f"x1_{sub}")
            def c_stage_a(sub, x1=x1, B_x1=B_x1, st5=st5):
                c = st5 * 4 + sub
                for half in range(2):
                    pp, bp = psum()
                    for kc in range(8):
                        src = qT if kc < 4 else ybT
                        bsrc = B_q[kc][c] if kc < 4 else B_yb[c]
                        S.op("pe", lambda pp=pp, kc=kc, c=c, half=half, src=src: nc.tensor.matmul(
                            pp, lhsT=src[:, kc % 4, c * 128:(c + 1) * 128], rhs=wo[:, kc, half * 512:(half + 1) * 512],
                            start=(kc == 0), stop=(kc == 7)), reads=[bsrc, B_wo], writes=[bp])
                    tt, btt = tcR.next()
                    S.op("dve", lambda pp=pp, tt=tt, half=half: nc.vector.tensor_tensor(
                        out=tt, in0=pp, in1=gb[:, 0, half * 512:(half + 1) * 512], op=ALU.mult), reads=[bp, B_gb], writes=[btt])
                    S.op("pool", lambda tt=tt, sub=sub, half=half, x1=x1: nc.gpsimd.tensor_tensor(
                        out=x1[:, sub, half * 512:(half + 1) * 512], in0=x1[:, sub, half * 512:(half + 1) * 512], in1=tt, op=ALU.add),
                        reads=[btt, B_x1[sub]], writes=[B_x1[sub]])
                return norm_stage_a(x1[:, sub, :], B_x1[sub], sqJ, xhR2, ssR2)

            cur = c_stage_a(0)
            for sub in range(4):
                nxt = c_stage_a(sub + 1) if sub + 1 < 4 else None
                norm_stage_b(cur[0], cur[1], 3, 2, r, h2T, sub * 128, B_h2[sub], B_mod2)
                cur = nxt
            for f in range(NF):
                wt, bw = wguR.next()
                bw2 = wgu_b2[(wguR.i - 1) % 2]
                S.dma("sp", lambda wt=wt, f=f: nc.sync.dma_start(out=wt[:, 0, :], in_=wg_s[f * 128:(f + 1) * 128, :]),
                      reads=B_wscr["wg"], writes=[bw], stream=f"wgu{f % 3}")
                S.dma("sp", lambda wt=wt, f=f: nc.sync.dma_start(out=wt[:, 1, :], in_=wu_s[f * 128:(f + 1) * 128, :]),
                      reads=B_wscr["wu"], writes=[bw2], stream=f"wgu{f % 3}")
                pg_, bg_ = psum()
                pu_, bu_ = psum()
                for (pp, bp, wi) in [(pg_, bg_, 0), (pu_, bu_, 1)]:
                    for kc in range(8):
                        S.op("pe", lambda pp=pp, kc=kc, wt=wt, wi=wi: nc.tensor.matmul(
                            pp, lhsT=wt[:, wi, kc * 128:(kc + 1) * 128], rhs=h2T[:, kc, :], start=(kc == 0), stop=(kc == 7)),
                            reads=[bw if wi == 0 else bw2] + B_h2, writes=[bp])
                sgt, bsg = sgR.next()
                S.op("act", lambda pg_=pg_, sgt=sgt: nc.scalar.activation(out=sgt, in_=pg_, func=AF.Silu), reads=[bg_], writes=[bsg])
                S.op("dve", lambda pu_=pu_, sgt=sgt, f=f: nc.vector.tensor_tensor(out=actT[:, f, :], in0=pu_, in1=sgt, op=ALU.mult),
                     reads=[bu_, bsg], writes=[B_act[f]])
            for sub in range(4):
                c = st5 * 4 + sub
                for half in range(2):
                    pp, bp = psum()
                    for f in range(NF):
                        S.op("pe", lambda pp=pp, f=f, sub=sub, half=half: nc.tensor.matmul(
                            pp, lhsT=actT[:, f, sub * 128:(sub + 1) * 128], rhs=wd[:, f, half * 512:(half + 1) * 512],
                            start=(f == 0), stop=(f == NF - 1)), reads=[B_act[f], B_wd], writes=[bp])
                    tt, btt = tcR.next()
                    S.op("dve", lambda pp=pp, tt=tt, half=half: nc.vector.tensor_tensor(
                        out=tt, in0=pp, in1=gb[:, 1, half * 512:(half + 1) * 512], op=ALU.mult), reads=[bp, B_gb], writes=[btt])
                    S.op("pool", lambda tt=tt, sub=sub, half=half, x1=x1: nc.gpsimd.tensor_tensor(
                        out=x1[:, sub, half * 512:(half + 1) * 512], in0=x1[:, sub, half * 512:(half + 1) * 512], in1=tt, op=ALU.add),
                        reads=[btt, B_x1[sub]], writes=[B_x1[sub]])
                sq, bsq = sqR2.next()
                ss, bss = ssR2.next()
                S.op("act", lambda sq=sq, ss=ss, sub=sub, x1=x1: nc.scalar.activation(out=sq, in_=x1[:, sub, :], func=AF.Square,
                                                                              accum_out=ss[:, 0:1]), reads=[B_x1[sub]], writes=[bsq, bss])
                S.op("pool", lambda ss=ss: nc.gpsimd.tensor_scalar(out=ss[:, 1:2], in0=ss[:, 0:1], scalar1=1.0 / D, scalar2=EPS,
                                                                   op0=ALU.mult, op1=ALU.add), reads=[bss], writes=[bss])
                S.op("pool", lambda ss=ss: nc.gpsimd.tensor_tensor(out=ss[:, 2:3], in0=ss[:, 1:2], in1=cols[:, 1:2], op=ALU.pow),
                     reads=[bss, B_const], writes=[bss])
                S.op("dve", lambda ss=ss, sub=sub, x1=x1: nc.vector.scalar_tensor_tensor(
                    out=x1[:, sub, :], in0=x1[:, sub, :], scalar=ss[:, 2:3], in1=fgb[:], op0=ALU.mult, op1=ALU.mult),
                    reads=[B_x1[sub], bss, B_fg], writes=[B_x1[sub]])
                S.dma("sp", lambda sub=sub, c=c, x1=x1: nc.sync.dma_start(out=y_d[c * 128:(c + 1) * 128, :], in_=x1[:, sub, :]),
                      reads=[B_x1[sub]], stream=f"yo{sub}")
        S.barrier()

    if stop is not None:
        S.ops = S.ops[:S.marks[stop]]
    S.emit()
    es.close()
    return nc, S


def _consts():
    i = np.arange(128)
    ident = np.eye(128, dtype=np.float32)
    Uf = (i[:, None] <= i[None, :]).astype(np.float32)
    Ub = (i[:, None] >= i[None, :]).astype(np.float32)
    MNf = np.where(i[:, None] >= i[None, :], 0.0, NEG).astype(np.float32)
    MNb = np.where(i[:, None] <= i[None, :], 0.0, NEG).astype(np.float32)
    offd = (1.0 - ident).astype(np.float32)
    ones = np.ones((128, 128), np.float32)
    blk = lambda sz: (i[:, None] // sz == i[None, :] // sz)
    masks = [blk(16) & (i[:, None] != i[None, :])] + [(blk(2 * sz) & ~blk(sz)) for sz in (16, 32, 64)]
    consts = np.concatenate([ident, Uf, Ub, MNf, MNb, offd, ones] + [m.astype(np.float32) for m in masks], axis=1)
    sel = np.zeros((2, 2, 128), np.float32)
    sel[0, 0, :] = 1.0
    sel[1, 1, :] = 1.0
    return np.ascontiguousarray(consts), sel


_CACHE = {}


def make_in_maps(x_prompt, x_sample, state_fwd, state_bwd, c, c_ctx, w_ada, b_ada, norm1_g, norm2_g, w_in, conv_w,
                 a_log, dt_bias, dn_norm_g, sgu_ln_g, sgu_ln_b, sgu_w, sgu_b, w_out, w_gate, w_up, w_down, final_g):
    f = lambda a: np.ascontiguousarray(np.asarray(a, dtype=np.float32))
    consts, sel = _consts()
    shared = {
        "w_ada": f(w_ada[0]),
        "b_ada2": f(np.stack([b_ada[0], b_ada[0]], 0)),
        "w_in": f(w_in[0]),
        "w_out": f(w_out[0]),
        "wg_t": f(np.asarray(w_gate[0]).reshape(8, 128, NF, 128).transpose(2, 1, 0, 3).reshape(NF * 128, D)),
        "wu_t": f(np.asarray(w_up[0]).reshape(8, 128, NF, 128).transpose(2, 1, 0, 3).reshape(NF * 128, D)),
        "w_down": f(w_down[0]),
        "conv_wT": f(np.asarray(conv_w[0]).reshape(5, 12, 128).transpose(2, 1, 0)),
        "n1g": f(np.asarray(norm1_g[0]).reshape(8, 128).T),
        "n2g": f(np.asarray(norm2_g[0]).reshape(8, 128).T),
        "adt": f(np.concatenate([np.asarray(a_log[0]).reshape(-1), np.asarray(dt_bias[0]).reshape(-1)])[None, :]),
        "dn_g": f(np.asarray(dn_norm_g[0])[None, :]),
        "lngT": f(np.asarray(sgu_ln_g[0]).reshape(4, 128).T),
        "ln_b": f(np.asarray(sgu_ln_b[0])[None, :]),
        "sgu_wT": f(np.asarray(sgu_w[0]).transpose(2, 0, 1)),
        "sgu_b": f(np.asarray(sgu_b[0]).reshape(1, 512)),
        "final_g": f(np.asarray(final_g)[None, :]),
        "consts": consts,
        "sel": sel,
    }
    xp = np.asarray(x_prompt, np.float32)
    xs = np.asarray(x_sample, np.float32)
    maps = []
    for i in range(8):
        m = dict(shared)
        m["x"] = f(np.concatenate([xp[2 * i], xp[2 * i + 1], xs[i]], axis=0))
        cv = np.stack([np.asarray(c_ctx, np.float32), np.asarray(c[i], np.float32)], 0)
        m["cT"] = f(cv.reshape(2, 8, 128).transpose(2, 1, 0))
        s0 = np.stack([np.asarray(state_fwd[i, 0]), np.asarray(state_bwd[i, 0])], 0)
        m["s0"] = f(s0.transpose(0, 2, 1, 3).reshape(2, 128, 512))
        maps.append(m)
    return maps


def kernel(**inputs):
    if "nc" not in _CACHE:
        _CACHE["nc"] = build_nc(debug=False)[0]
    nc = _CACHE["nc"]
    maps = make_in_maps(**inputs)
    res = run_bass_kernel_spmd(nc, maps, core_ids=list(range(8)))
    y_prompt = np.zeros((16, 256, D), np.float32)
    y_sample = np.zeros((8, 2048, D), np.float32)
    nsf = np.zeros((16, 1, 4, 128, 128), np.float32)
    nsb = np.zeros((16, 1, 4, 128, 128), np.float32)
    for i in range(8):
        r = res.results[i]
        y = np.asarray(r["y"])
        y_prompt[2 * i] = y[0:256]
        y_prompt[2 * i + 1] = y[256:512]
        y_sample[i] = y[512:]
        sf = np.asarray(r["sf"])
        sbb = np.asarray(r["sb"])
        nsf[2 * i, 0] = sf[0]
        nsf[2 * i + 1, 0] = sf[1]
        nsb[2 * i, 0] = sbb[0]
        nsb[2 * i + 1, 0] = sbb[1]
    return (y_prompt, y_sample, nsf, nsb)
```

```python
import contextlib
import numpy as np
import concourse.bass as bass
import concourse.mybir as mybir
from concourse.bass_utils import run_bass_kernel_spmd

F32 = mybir.dt.float32
BF16 = mybir.dt.bfloat16
AF = mybir.ActivationFunctionType
ALU = mybir.AluOpType
AX = mybir.AxisListType

NT = 2560
NCH = 20
D = 1024
DFF = 2816
NF = 22
EPS = 1e-6
NEG = -30000.0
SEQS = [(0, 2, 0), (2, 2, 0), (4, 16, 1)]
CT = [(0, 256, 2), (256, 256, 260)] + [(512 + 512 * k, 512, 518 + 512 * k) for k in range(4)]
PCW = 2568

DEBUG = {}


class Buf:
    __slots__ = ("name", "w", "rs")

    def __init__(self, name):
        self.name = name
        self.w = None
        self.rs = []


class Sched:
    EP = 16000

    def __init__(self, nc, es):
        self.nc = nc
        self.es = es
        self.ops = []
        self.eng = {"pe": nc.tensor, "act": nc.scalar, "dve": nc.vector, "pool": nc.gpsimd, "sp": nc.sync}
        self.nb = 0

    def buf(self, name=None):
        self.nb += 1
        return Buf(name or f"b{self.nb}")

    def op(self, eng, fn, reads=(), writes=()):
        self.ops.append(("c", eng, fn, tuple(reads), tuple(writes), None))

    def dma(self, q, fn, reads=(), writes=(), stream="d"):
        self.ops.append(("d", q, fn, tuple(reads), tuple(writes), stream))

    def barrier(self):
        self.ops.append(("b", None, None, (), (), None))

    def mark(self, name):
        self.marks = getattr(self, 'marks', {})
        self.marks[name] = len(self.ops)

    def emit(self):
        ops = self.ops
        n = len(ops)
        deps = [None] * n
        need_inc = [False] * n
        last_eng = {}
        last_stream = {}
        pending_barrier = {}
        dma_since = []
        for i, (kind, eng, fn, reads, writes, stream) in enumerate(ops):
            if kind == "b":
                snap = set(last_eng.values()) | set(dma_since)
                dma_since = []
                for e in self.eng:
                    pending_barrier[e] = pending_barrier.get(e, set()) | snap
                continue
            ds = {}
            for b in reads:
                if b.w is not None:
                    ds[b.w] = "raw"
            for b in writes:
                if b.w is not None and b.w not in ds:
                    ds[b.w] = "waw"
                lastr = {}
                for r in b.rs:
                    key = (ops[r][0], ops[r][1], ops[r][5])
                    if ops[r][0] == "c":
                        lastr[key] = max(lastr.get(key, -1), r)
                    else:
                        lastr[(key, r)] = r
                for r in lastr.values():
                    if r not in ds:
                        ds[r] = "war"
            if eng in pending_barrier and pending_barrier[eng]:
                for j in pending_barrier[eng]:
                    ds[j] = "raw"
                pending_barrier[eng] = set()
            for b in reads:
                b.rs.append(i)
            for b in writes:
                b.w = i
                b.rs = []
            keep = []
            for j, typ in ds.items():
                if j == i:
                    continue
                pk, pe = ops[j][0], ops[j][1]
                if pk == "c" and kind == "c" and pe == eng and typ != "raw":
                    continue
                if pk == "c" and kind == "c" and pe == eng and eng == "pe":
                    continue
                keep.append(j)
                if pk == "c":
                    need_inc[j] = True
            deps[i] = keep
            if kind == "c":
                last_eng[eng] = i
            else:
                last_stream[stream] = i
                if stream not in ("wc", "dbg"):
                    dma_since.append(i)
        sems = {}
        ord_of = {}
        cnt = {e: 0 for e in self.eng}
        known = {e: {} for e in self.eng}
        POOLN = {"sp": 10, "pool": 4, "act": 4}
        pool_next = {q: 0 for q in POOLN}
        pool_cnt = {}
        dma_ev = {}

        def sem(key):
            if key not in sems:
                sems[key] = self.es.enter_context(self.nc.semaphore("s_" + "_".join(str(k) for k in key)))
            return sems[key]

        nwaits = 0
        for i, (kind, eng, fn, reads, writes, stream) in enumerate(ops):
            if kind == "b":
                continue
            h = self.eng[eng]
            want = {}
            for j in deps[i]:
                pk, pe = ops[j][0], ops[j][1]
                if pk == "c":
                    key = ("e", pe)
                    want[key] = max(want.get(key, 0), ord_of[j])
                else:
                    key, v = dma_ev[j]
                    want[key] = max(want.get(key, 0), v)
            if kind == "d":
                slot = pool_next[eng] % POOLN[eng]
                pool_next[eng] += 1
                key = ("d", eng, slot)
                prev = pool_cnt.get(key, 0)
                if prev:
                    want[key] = max(want.get(key, 0), prev)
                pool_cnt[key] = prev + 16
                assert pool_cnt[key] < 60000
                dma_ev[i] = (key, prev + 16)
            for key, o in want.items():
                if known[eng].get(key, 0) >= o:
                    continue
                known[eng][key] = o
                if key[0] == "e":
                    ep, v = (o - 1) // self.EP, (o - 1) % self.EP + 1
                    h.wait_ge(sem((key[1], ep)), v)
                else:
                    h.wait_ge(sem(key), o)
                nwaits += 1
            inst = fn()
            if kind == "c":
                if need_inc[i]:
                    cnt[eng] += 1
                    ord_of[i] = cnt[eng]
                    ep = (cnt[eng] - 1) // self.EP
                    inst.then_inc(sem((eng, ep)), 1)
            else:
                inst.then_inc(sem(dma_ev[i][0]), 16)
        for key, c in pool_cnt.items():
            self.nc.sync.wait_ge(sem(key), c)
        self.stats = dict(nops=n, nwaits=nwaits, nsems=len(sems), cnt=cnt)


class Ring:
    def __init__(self, S, aps, name):
        self.items = [(ap, S.buf(f"{name}{i}")) for i, ap in enumerate(aps)]
        self.i = 0

    def next(self):
        it = self.items[self.i % len(self.items)]
        self.i += 1
        return it


def build_nc(debug=False, stop=None):
    nc = bass.Bass("TRN2", target_bir_lowering=False)
    es = contextlib.ExitStack()
    S = Sched(nc, es)

    def din(name, shape, dt=F32):
        return nc.dram_tensor(name, list(shape), dt, kind="ExternalInput").ap()

    def dout(name, shape, dt=F32):
        return nc.dram_tensor(name, list(shape), dt, kind="ExternalOutput").ap()

    def dint(name, shape, dt=F32):
        return nc.dram_tensor(name, list(shape), dt, kind="Internal").ap()

    x_d = din("x", [NT, D])
    cT_d = din("cT", [128, 8, 2])
    s0_d = din("s0", [2, 128, 512])
    w_ada_d = din("w_ada", [D, 6 * D])
    b_ada_d = din("b_ada2", [2, 6 * D])
    w_in_d = din("w_in", [D, 3088])
    w_out_d = din("w_out", [D, D])
    wg_d = din("wg_t", [NF * 128, D])
    wu_d = din("wu_t", [NF * 128, D])
    wd_d = din("w_down", [DFF, D])
    convw_d = din("conv_wT", [128, 12, 5])
    n1g_d = din("n1g", [128, 8])
    n2g_d = din("n2g", [128, 8])
    adt_d = din("adt", [1, 16])
    dng_d = din("dn_g", [1, 128])
    lngT_d = din("lngT", [128, 4])
    lnb_d = din("ln_b", [1, 512])
    sguw_d = din("sgu_wT", [128, 4, 128])
    sgub_d = din("sgu_b", [1, 512])
    fg_d = din("final_g", [1, D])
    consts_d = din("consts", [128, 11 * 128])
    sel_d = din("sel", [2, 2, 128])

    y_d = dout("y", [NT, D])
    sf_d = dout("sf", [2, 4, 128, 128])
    sb_d = dout("sb", [2, 4, 128, 128])
    dbg_d = {}

    mod_d = dint("mod_scr", [2, 6 * D])
    wo_s = dint("wo_scr", [D, D], BF16)
    wd_s = dint("wd_scr", [DFF, D], BF16)
    wg_s = dint("wg_scr", [NF * 128, D], BF16)
    wu_s = dint("wu_scr", [NF * 128, D], BF16)

    def sb(name, shape, dt=F32, stack=None):
        return (stack or es).enter_context(nc.sbuf_tensor("sb_" + name, list(shape), dt))

    banks = [es.enter_context(nc.psum_tensor(f"bank{i}", [128, 512], F32)) for i in range(8)]
    PS = Ring(S, [b[:] for b in banks], "ps")

    def psum():
        return PS.next()

    consts = sb("consts", [128, 11 * 128])
    ident_f = consts[:, 0:128]
    U = [consts[:, 128:256], consts[:, 256:384]]
    MN = [consts[:, 384:512], consts[:, 512:640]]
    offd_f = consts[:, 640:768]
    ones_f = consts[:, 768:896]
    cbf = sb("cbf", [128, 7 * 128], BF16)
    ident_b = cbf[:, 0:128]
    ones_b = cbf[:, 128:256]
    offd_b = cbf[:, 256:384]
    MASKS = [cbf[:, 384 + 128 * m:512 + 128 * m] for m in range(4)]
    sel = sb("sel", [2, 2, 128])
    B_const = S.buf("consts")
    cols = sb("cols", [128, 8])
    mh4 = sb("mh4", [128, 4])
    sc_sh = sb("sc_sh", [128, 4, 8, 2])
    B_mod1 = S.buf("mod1")
    B_mod2 = S.buf("mod2")
    B_mblk = [S.buf(f"mblk{i}") for i in range(24)]
    n1g = sb("n1g", [128, 8])
    n2g = sb("n2g", [128, 8])
    convw = sb("convw", [128, 12, 5])
    adt_b = sb("adt_b", [128, 16])
    nea_b = sb("nea_b", [128, 8])
    lngT = sb("lngT", [128, 4])
    BB = sb("BB", [128, 512])
    sguw = sb("sguw", [128, 4, 128], BF16)
    B_par = S.buf("params")
    qT = sb("qT", [128, 4, NT], BF16)
    ybT = sb("ybT", [128, 4, NT], BF16)
    B_q = [[S.buf(f"q{h}_{c}") for c in range(NCH)] for h in range(4)]
    B_yb = [S.buf(f"yb{c}") for c in range(NCH)]

    eng = S.eng

    def dbg(name, ap, shape, dt, reads):
        if not debug:
            return
        dd = dout("dbg_" + name, shape, dt)
        S.dma("sp", lambda: nc.sync.dma_start(out=dd, in_=ap), reads=reads, stream="dbg")

    S.dma("sp", lambda: nc.sync.dma_start(out=consts[:], in_=consts_d), writes=[B_const], stream="c0")
    S.dma("sp", lambda: nc.sync.dma_start(out=sel[:], in_=sel_d), writes=[B_const], stream="c0")
    for (t, d_) in [(n1g, n1g_d), (n2g, n2g_d), (convw, convw_d), (lngT, lngT_d)]:
        S.dma("sp", lambda t=t, d_=d_: nc.sync.dma_start(out=t[:], in_=d_), writes=[B_par], stream="c0")
    for (t, d_, w) in [(adt_b, adt_d, 16)]:
        S.dma("sp", lambda t=t, d_=d_: nc.sync.dma_start(out=t[:], in_=d_.partition_broadcast(128)[:, 0, :]),
              writes=[B_par], stream="c0")
    S.op("dve", lambda: nc.vector.tensor_copy(out=cbf[:, 0:128], in_=ident_f), reads=[B_const], writes=[B_const])
    S.op("dve", lambda: nc.vector.tensor_copy(out=cbf[:, 128:256], in_=ones_f), reads=[B_const], writes=[B_const])
    S.op("dve", lambda: nc.vector.tensor_copy(out=cbf[:, 256:384], in_=offd_f), reads=[B_const], writes=[B_const])
    S.op("dve", lambda: nc.vector.tensor_copy(out=cbf[:, 384:896], in_=consts[:, 896:1408]), reads=[B_const], writes=[B_const])
    for k, v in enumerate([EPS, -0.5, -0.5 * float(np.log(128.0)), 1.0, 4 * EPS, 0.0]):
        S.op("pool", lambda k=k, v=v: nc.gpsimd.memset(cols[:, k:k + 1], v), writes=[B_const])
    S.op("pool", lambda: nc.gpsimd.memset(mh4[:], -0.5), writes=[B_const])
    S.op("act", lambda: nc.scalar.activation(out=nea_b[:], in_=adt_b[:, 0:8], func=AF.Exp), reads=[B_par], writes=[B_par])
    S.op("dve", lambda: nc.vector.tensor_scalar(out=nea_b[:], in0=nea_b[:], scalar1=-1.0, scalar2=None, op0=ALU.mult),
         reads=[B_par], writes=[B_par])
    S.mark('p0b')
    B_wscr = {"wo": [], "wd": [], "wg": [], "wu": []}

    with contextlib.ExitStack() as st0:
        lnb_b = sb("lnb_b", [128, 512], stack=st0)
        sgub_b = sb("sgub_b", [128, 512], stack=st0)
        sguw_f = sb("sguw_f", [128, 4, 128], stack=st0)
        S.dma("sp", lambda: nc.sync.dma_start(out=sguw_f[:], in_=sguw_d), writes=[B_par], stream="c0")
        for (t, d_) in [(lnb_b, lnb_d), (sgub_b, sgub_d)]:
            S.dma("sp", lambda t=t, d_=d_: nc.sync.dma_start(out=t[:], in_=d_.partition_broadcast(128)[:, 0, :]),
                  writes=[B_par], stream="c0")
        S.op("dve", lambda: nc.vector.tensor_copy(out=sguw[:], in_=sguw_f[:]), reads=[B_par], writes=[B_par])
        S.mark('p0a')
        pB, bB = psum()
        for g in range(4):
            S.op("pe", lambda g=g: nc.tensor.matmul(pB[:, g * 128:(g + 1) * 128], lhsT=lnb_b[:, g * 128:(g + 1) * 128],
                                                      rhs=sguw_f[:, g, :], start=True, stop=True),
                 reads=[B_par], writes=[bB])
        S.op("dve", lambda: nc.vector.tensor_tensor(out=BB[:], in0=pB, in1=sgub_b[:], op=ALU.add),
             reads=[bB, B_par], writes=[B_par])

        S.barrier()
    S.barrier()

    def norm_stage_a(xt, bx, ring_sq, ring_xh, tmp_small):
        sq, bsq = ring_sq.next()
        ss, bss = tmp_small.next()
        S.op("act", lambda: nc.scalar.activation(out=sq, in_=xt, func=AF.Square, accum_out=ss[:, 0:1]),
             reads=[bx], writes=[bsq, bss])
        S.op("pool", lambda: nc.gpsimd.tensor_scalar(out=ss[:, 1:2], in0=ss[:, 0:1], scalar1=1.0 / D, scalar2=EPS,
                                                     op0=ALU.mult, op1=ALU.add), reads=[bss], writes=[bss])
        S.op("pool", lambda: nc.gpsimd.tensor_tensor(out=ss[:, 2:3], in0=ss[:, 1:2], in1=cols[:, 1:2], op=ALU.pow),
             reads=[bss, B_const], writes=[bss])
        xh, bxh = ring_xh.next()
        S.op("dve", lambda: nc.vector.tensor_scalar(out=xh, in0=xt, scalar1=ss[:, 2:3], scalar2=None, op0=ALU.mult),
             reads=[bx, bss], writes=[bxh])
        return xh, bxh

    def norm_stage_b(xh, bxh, which_scale, which_shift, r, dstT, tok0, bdst, bmod):
        pt, bpt = psum()
        ptb = pt.bitcast(BF16)
        for kc in range(8):
            S.op("pe", lambda kc=kc: nc.tensor.transpose(ptb[:, kc * 128:(kc + 1) * 128], xh[:, kc * 128:(kc + 1) * 128], ident_b),
                 reads=[bxh, B_const], writes=[bpt])
        for kc in range(8):
            S.op("act", lambda kc=kc: nc.scalar.activation(out=dstT[:, kc, tok0:tok0 + 128], in_=ptb[:, kc * 128:(kc + 1) * 128],
                                                           func=AF.Identity, scale=sc_sh[:, which_scale, kc, r:r + 1],
                                                           bias=sc_sh[:, which_shift, kc, r:r + 1]),
                 reads=[bpt, bmod], writes=[bdst])

    def norm_to_featmajor(xt, bx, which_scale, which_shift, r, dstT, tok0, bdst, ring_sq, ring_xh, tmp_small, bmod):
        xh, bxh = norm_stage_a(xt, bx, ring_sq, ring_xh, tmp_small)
        norm_stage_b(xh, bxh, which_scale, which_shift, r, dstT, tok0, bdst, bmod)

    with contextlib.ExitStack() as stAB:
        hT = sb("hT", [128, 8, NT], BF16, stack=stAB)
        kT = sb("kT", [128, 4, NT], BF16, stack=stAB)
        vT = sb("vT", [128, 4, NT], BF16, stack=stAB)
        zs = sb("zs", [128, NCH, 512], BF16, stack=stAB)
        lg = sb("lg", [128, NCH, 16], stack=stAB)
        beta = sb("beta", [128, NCH, 8], stack=stAB)
        negb = sb("negb", [128, NCH, 8], stack=stAB)
        gg = sb("gg", [128, NCH, 8], stack=stAB)
        negg = sb("negg", [128, NCH, 8], stack=stAB)
        gcl = sb("gcl", [128, NCH, 16], stack=stAB)
        egc = sb("egc", [128, NCH, 8], stack=stAB)
        egl = sb("egl", [128, NCH, 8], stack=stAB)
        ekd = sb("ekd", [128, NCH, 8], stack=stAB)
        bege = sb("bege", [128, NCH, 8], stack=stAB)
        B_h = [S.buf(f"h{c}") for c in range(NCH)]
        B_k = [[S.buf(f"k{h}_{c}") for c in range(NCH)] for h in range(4)]
        B_v = [[S.buf(f"v{h}_{c}") for c in range(NCH)] for h in range(4)]
        B_z = [S.buf(f"z{c}") for c in range(NCH)]
        B_lg = S.buf("lg")
        B_coef = S.buf("coef")

        def chunks_of(t0, n):
            return range(t0 // 128, (t0 + n + 127) // 128)

        with contextlib.ExitStack() as stA:
            xr = [sb(f"xr{i}", [128, D], stack=stA) for i in range(3)]
            xR = Ring(S, [t[:] for t in xr], "xr")
            sqj = sb("sqj", [128, D], BF16, stack=stA)
            sqR = Ring(S, [sqj[:]], "sqj")
            xhr = [sb(f"xh{i}", [128, D], BF16, stack=stA) for i in range(2)]
            xhR = Ring(S, [t[:] for t in xhr], "xh")
            ssr = [sb(f"ss{i}", [128, 4], stack=stA) for i in range(4)]
            ssR = Ring(S, [t[:] for t in ssr], "ss")

            cT = sb("cT", [128, 8, 2], stack=stA)
            cs = sb("cs", [128, 8, 2], stack=stA)
            B_c = S.buf("c")
            waR = Ring(S, [sb(f"wa{i}", [128, 4, 512], stack=stA)[:] for i in range(2)], "wa")
            bbR = Ring(S, [sb(f"bbt{i}", [2, 512], stack=stA)[:] for i in range(2)], "bbt")
            mrR = Ring(S, [sb(f"mrw{i}", [2, 512], stack=stA)[:] for i in range(2)], "mrw")
            S.dma("sp", lambda: nc.sync.dma_start(out=cT[:], in_=cT_d), writes=[B_c], stream="c0")
            S.op("act", lambda: nc.scalar.activation(out=cs[:], in_=cT[:], func=AF.Silu), reads=[B_c], writes=[B_c])
            wav = w_ada_d.rearrange("(kc p) n -> p kc n", p=128)
            mod_state = {}

            def mod_block(hb):
                cb_, kh = hb // 2, hb % 2
                cs0 = cb_ * 512
                wt, bw = waR.next()
                S.dma("sp", lambda: nc.sync.dma_start(out=wt, in_=wav[:, kh * 4:(kh + 1) * 4, cs0:cs0 + 512]), writes=[bw], stream="wa")
                if kh == 0:
                    bt, bb_ = bbR.next()
                    S.dma("sp", lambda: nc.sync.dma_start(out=bt, in_=b_ada_d[:, cs0:cs0 + 512]), writes=[bb_], stream="wa")
                    pm, bm = psum()
                    mod_state[cb_] = (bt, bb_, pm, bm)
                bt, bb_, pm, bm = mod_state[cb_]
                for k in range(4):
                    S.op("pe", lambda k=k: nc.tensor.matmul(pm[0:2, :], lhsT=cs[:, kh * 4 + k, :], rhs=wt[:, k, :],
                                                            start=(kh == 0 and k == 0), stop=(kh == 1 and k == 3)),
                         reads=[B_c, bw], writes=[bm])
                if kh == 1:
                    mr, bmr = mrR.next()
                    S.op("dve", lambda: nc.vector.tensor_tensor(out=mr, in0=pm[0:2, :], in1=bt, op=ALU.add), reads=[bm, bb_], writes=[bmr])
                    S.dma("pool", lambda: nc.gpsimd.dma_start(out=mod_d[:, cs0:cs0 + 512], in_=mr), reads=[bmr], writes=[B_mblk[cb_]], stream="md")

            def load_scsh(wis, bmodx):
                for wi in wis:
                    seg = [0, 1, 3, 4][wi]
                    for r in range(2):
                        S.dma("sp", lambda wi=wi, seg=seg, r=r: nc.sync.dma_start(
                            out=sc_sh[:, wi, :, r], in_=mod_d[r, seg * D:(seg + 1) * D].rearrange("(kc p) -> p kc", p=128),
                            allow_slow_non_contiguous=True), reads=B_mblk[seg * 2:seg * 2 + 2], writes=[bmodx], stream="md")
                wi, gt = (1, n1g) if 1 in wis else (3, n2g)
                for r in range(2):
                    S.op("dve", lambda wi=wi, gt=gt, r=r: nc.vector.scalar_tensor_tensor(
                        out=sc_sh[:, wi, :, r], in0=sc_sh[:, wi, :, r], scalar=1.0, in1=gt[:], op0=ALU.add, op1=ALU.mult),
                        reads=[bmodx, B_par], writes=[bmodx])

            for nb in range(8):
                mod_block(nb)
            load_scsh([0, 1], B_mod1)
            S.mark('p0')

            def a1_stage_a(c):
                xt, bx = xR.next()
                S.dma("sp", lambda xt=xt, c=c: nc.sync.dma_start(out=xt, in_=x_d[c * 128:(c + 1) * 128, :]),
                      writes=[bx], stream=f"x{c % 3}")
                return norm_stage_a(xt, bx, sqR, xhR, ssR)

            cur = a1_stage_a(0)
            for c in range(NCH):
                if c < 16:
                    mod_block(8 + c)
                nxt = a1_stage_a(c + 1) if c + 1 < NCH else None
                r = 0 if c < 4 else 1
                norm_stage_b(cur[0], cur[1], 1, 0, r, hT, c * 128, B_h[c], B_mod1)
                cur = nxt
            load_scsh([2, 3], B_mod2)

            S.barrier()
        S.barrier()
        with contextlib.ExitStack() as stA:
            wblk = [sb(f"wblk{i}", [128, 8, 512], BF16, stack=stA) for i in range(2)]
            wR = Ring(S, [t[:] for t in wblk], "wblk")
            pre = [sb(f"pre{i}", [128, PCW], BF16, stack=stA) for i in range(2)]
            preR = Ring(S, [t[:] for t in pre], "pre")
            dg = [sb(f"dg{i}", [128, 5, 128], BF16, stack=stA) for i in range(2)]
            dgR = Ring(S, [t[:] for t in dg], "dg")
            tmpA = [sb(f"tmpA{i}", [128, 512], stack=stA) for i in range(3)]
            tAR = Ring(S, [t[:] for t in tmpA], "tmpA")
            tmpB = [sb(f"tmpB{i}", [128, 512], BF16, stack=stA) for i in range(4)]
            tBR = Ring(S, [t[:] for t in tmpB], "tmpB")
            us = sb("us", [128, 4, 512], BF16, stack=stA)
            B_us = S.buf("us")
            stt = [sb(f"stt{i}", [128, 8], stack=stA) for i in range(2)]
            stR = Ring(S, [t[:] for t in stt], "stt")

            S.mark('a1')
            w_in_v = w_in_d.rearrange("(kc p) n -> p kc n", p=128)

            def load_wblk(col0, ncol):
                wt, bw = wR.next()
                S.dma("pool", lambda: nc.gpsimd.dma_start(out=wt[:, :, 0:ncol], in_=w_in_v[:, :, col0:col0 + ncol]),
                      writes=[bw], stream=f"wi{(wR.i) % 2}")
                return wt, bw

            for blk in range(3):
                wt, bw = load_wblk(blk * 512, 512)
                for ch in range(4):
                    cch = blk * 4 + ch
                    pc, bpc = preR.next()
                    S.op("pool", lambda pc=pc: nc.gpsimd.memset(pc, 0.0), writes=[bpc])
                    for st5 in range(5):
                        pp, bp = psum()
                        for kc in range(8):
                            S.op("pe", lambda pp=pp, kc=kc, ch=ch, st5=st5, wt=wt: nc.tensor.matmul(
                                pp, lhsT=wt[:, kc, ch * 128:(ch + 1) * 128], rhs=hT[:, kc, st5 * 512:(st5 + 1) * 512],
                                start=(kc == 0), stop=(kc == 7)),
                                reads=[bw] + B_h[st5 * 4:st5 * 4 + 4], writes=[bp])
                        if st5 == 0:
                            S.op("act", lambda pp=pp, pc=pc: nc.scalar.copy(out=pc[:, 2:258], in_=pp[:, 0:256]),
                                 reads=[bp], writes=[bpc])
                            S.op("act", lambda pp=pp, pc=pc: nc.scalar.copy(out=pc[:, 260:516], in_=pp[:, 256:512]),
                                 reads=[bp], writes=[bpc])
                        else:
                            c0 = 518 + 512 * (st5 - 1)
                            S.op("act", lambda pp=pp, pc=pc, c0=c0: nc.scalar.copy(out=pc[:, c0:c0 + 512], in_=pp),
                                 reads=[bp], writes=[bpc])
                    dgt, bdg = dgR.next()
                    for j in range(5):
                        S.op("dve", lambda dgt=dgt, j=j, cch=cch: nc.vector.tensor_scalar(
                            out=dgt[:, j, :], in0=ident_f, scalar1=convw[:, cch, j:j + 1], scalar2=None, op0=ALU.mult),
                            reads=[B_const, B_par], writes=[bdg])
                    dst, Bd = [(qT, B_q), (kT, B_k), (vT, B_v)][blk]
                    for (t0, n_, c0) in CT:
                        pp, bp = psum()
                        for j in range(5):
                            S.op("pe", lambda pp=pp, j=j, dgt=dgt, pc=pc, c0=c0, n_=n_: nc.tensor.matmul(
                                pp[:, 0:n_], lhsT=dgt[:, j, :], rhs=pc[:, c0 + j - 2:c0 + j - 2 + n_],
                                start=(j == 0), stop=(j == 4)), reads=[bdg, bpc], writes=[bp])
                        S.op("act", lambda pp=pp, dst=dst, ch=ch, t0=t0, n_=n_: nc.scalar.activation(
                            out=dst[:, ch, t0:t0 + n_], in_=pp[:, 0:n_], func=AF.Silu),
                            reads=[bp], writes=[Bd[ch][c] for c in chunks_of(t0, n_)])

            wt, bw = load_wblk(3 * 512, 512)
            for c in range(NCH):
                pp, bp = psum()
                for kc in range(8):
                    S.op("pe", lambda pp=pp, kc=kc, c=c, wt=wt: nc.tensor.matmul(
                        pp, lhsT=hT[:, kc, c * 128:(c + 1) * 128], rhs=wt[:, kc, :], start=(kc == 0), stop=(kc == 7)),
                        reads=[bw, B_h[c]], writes=[bp])
                S.op("act", lambda pp=pp, c=c: nc.scalar.activation(out=zs[:, c, :], in_=pp, func=AF.Silu),
                     reads=[bp], writes=[B_z[c]])

            wt16, bw16 = load_wblk(3072, 16)
            pl, bpl = psum()
            for c in range(NCH):
                for kc in range(8):
                    S.op("pe", lambda kc=kc, c=c: nc.tensor.matmul(
                        pl[:, c * 16:(c + 1) * 16], lhsT=hT[:, kc, c * 128:(c + 1) * 128], rhs=wt16[:, kc, 0:16],
                        start=(kc == 0), stop=(kc == 7)), reads=[bw16, B_h[c]], writes=[bpl])
            S.op("dve", lambda: nc.vector.tensor_copy(out=lg[:].rearrange("p c k -> p (c k)"), in_=pl[:, 0:NCH * 16]),
                 reads=[bpl], writes=[B_lg])

            C1 = 0.7978845608028654
            C2 = 0.044715

            def gelu2(pp, bp, out_ap, bout, n_=512):
                S.op("act", lambda: nc.scalar.activation(out=out_ap, in_=pp[:, 0:n_], func=AF.Gelu_apprx_tanh), reads=[bp], writes=[bout])

            wtu, bwu = load_wblk(4 * 512, 512)
            wtv, bwv = load_wblk(5 * 512, 512)
            for st5 in range(5):
                for ch in range(4):
                    pp, bp = psum()
                    for kc in range(8):
                        S.op("pe", lambda pp=pp, kc=kc, ch=ch, st5=st5: nc.tensor.matmul(
                            pp, lhsT=wtu[:, kc, ch * 128:(ch + 1) * 128], rhs=hT[:, kc, st5 * 512:(st5 + 1) * 512],
                            start=(kc == 0), stop=(kc == 7)), reads=[bwu] + B_h[st5 * 4:st5 * 4 + 4], writes=[bp])
                    gelu2(pp, bp, us[:, ch, :], B_us)
                def sgu_a(c):
                    pp, bp = psum()
                    for kc in range(8):
                        S.op("pe", lambda pp=pp, kc=kc, c=c: nc.tensor.matmul(
                            pp, lhsT=hT[:, kc, c * 128:(c + 1) * 128], rhs=wtv[:, kc, :], start=(kc == 0), stop=(kc == 7)),
                            reads=[bwv, B_h[c]], writes=[bp])
                    vg, bvg = tAR.next()
                    gelu2(pp, bp, vg, bvg)
                    return vg, bvg

                cur_sgu = sgu_a(st5 * 4)
                for sub in range(4):
                    c = st5 * 4 + sub
                    nxt_sgu = sgu_a(c + 1) if sub + 1 < 4 else None
                    vg, bvg = cur_sgu
                    cur_sgu = nxt_sgu
                    stt_, bst = stR.next()
                    S.op("dve", lambda vg=vg, stt_=stt_: nc.vector.bn_stats(out=stt_[:, 0:6], in_=vg), reads=[bvg], writes=[bst])
                    S.op("dve", lambda stt_=stt_: nc.vector.bn_aggr(out=stt_[:, 6:8], in_=stt_[:, 0:6]), reads=[bst], writes=[bst])
                    S.op("pool", lambda stt_=stt_: nc.gpsimd.tensor_scalar(out=stt_[:, 0:1], in0=stt_[:, 7:8], scalar1=EPS,
                                                                         scalar2=None, op0=ALU.add), reads=[bst], writes=[bst])
                    S.op("pool", lambda stt_=stt_: nc.gpsimd.tensor_tensor(out=stt_[:, 1:2], in0=stt_[:, 0:1], in1=cols[:, 1:2],
                                                                         op=ALU.pow), reads=[bst, B_const], writes=[bst])
                    vn, bvn = tBR.next()
                    S.op("dve", lambda vg=vg, vn=vn, stt_=stt_: nc.vector.tensor_scalar(
                        out=vn, in0=vg, scalar1=stt_[:, 6:7], scalar2=stt_[:, 1:2], op0=ALU.subtract, op1=ALU.mult),
                        reads=[bvg, bst], writes=[bvn])
                    pm, bm = psum()
                    for g in range(4):
                        S.op("pe", lambda pm=pm, g=g, vn=vn: nc.tensor.matmul(
                            pm[:, g * 128:(g + 1) * 128], lhsT=vn[:, g * 128:(g + 1) * 128], rhs=sguw[:, g, :],
                            start=True, stop=True), reads=[bvn, B_par], writes=[bm])
                    mx, bmx = tAR.next()
                    for g in range(4):
                        S.op("dve", lambda pm=pm, g=g, mx=mx: nc.vector.scalar_tensor_tensor(
                            out=mx[:, g * 128:(g + 1) * 128], in0=pm[:, g * 128:(g + 1) * 128], scalar=lngT[:, g:g + 1],
                            in1=BB[:, g * 128:(g + 1) * 128], op0=ALU.mult, op1=ALU.add),
                            reads=[bm, B_par], writes=[bmx])
                    S.op("dve", lambda mx=mx, c=c, sub=sub: nc.vector.scalar_tensor_tensor(
                        out=ybT[:, :, c * 128:(c + 1) * 128], in0=mx.rearrange("p (g t) -> p g t", g=4), scalar=1.0,
                        in1=us[:, :, sub * 128:(sub + 1) * 128], op0=ALU.mult, op1=ALU.mult),
                        reads=[bmx, B_us], writes=[B_yb[c]])

            S.op("act", lambda: nc.scalar.activation(out=beta[:], in_=lg[:, :, 0:8], func=AF.Tanh, scale=0.5),
                 reads=[B_lg], writes=[B_coef])
            S.op("dve", lambda: nc.vector.tensor_scalar(out=beta[:], in0=beta[:], scalar1=0.5, scalar2=0.5, op0=ALU.mult,
                                                        op1=ALU.add), reads=[B_coef], writes=[B_coef])
            S.op("dve", lambda: nc.vector.tensor_scalar(out=negb[:], in0=beta[:], scalar1=-1.0, scalar2=None, op0=ALU.mult),
                 reads=[B_coef], writes=[B_coef])
            S.op("dve", lambda: nc.vector.tensor_tensor(out=gg[:], in0=lg[:, :, 8:16],
                                                        in1=adt_b[:, 8:16].unsqueeze(1).to_broadcast([128, NCH, 8]), op=ALU.add),
                 reads=[B_lg, B_par], writes=[B_coef])
            S.op("dve", lambda: nc.vector.tensor_scalar(out=negg[:], in0=gg[:], scalar1=-1.0, scalar2=None, op0=ALU.mult),
                 reads=[B_coef], writes=[B_coef])
            S.op("dve", lambda: nc.vector.tensor_tensor(out=negg[:], in0=negg[:], in1=gg[:], op=ALU.max),
                 reads=[B_coef], writes=[B_coef])
            S.op("act", lambda: nc.scalar.activation(out=negg[:], in_=negg[:], func=AF.Exp, scale=-1.0),
                 reads=[B_coef], writes=[B_coef])
            S.op("act", lambda: nc.scalar.activation(out=negg[:], in_=negg[:], func=AF.Ln, bias=cols[:, 3:4], scale=1.0),
                 reads=[B_coef, B_const], writes=[B_coef])
            S.op("dve", lambda: nc.vector.tensor_scalar(out=gg[:], in0=gg[:], scalar1=0.0, scalar2=None, op0=ALU.max),
                 reads=[B_coef], writes=[B_coef])
            S.op("dve", lambda: nc.vector.tensor_tensor(out=gg[:], in0=gg[:], in1=negg[:], op=ALU.add),
                 reads=[B_coef], writes=[B_coef])
            S.op("dve", lambda: nc.vector.tensor_tensor(out=gg[:], in0=gg[:],
                                                        in1=nea_b[:].unsqueeze(1).to_broadcast([128, NCH, 8]), op=ALU.mult),
                 reads=[B_coef, B_par], writes=[B_coef])
            S.op("dve", lambda: nc.vector.tensor_scalar(out=negg[:], in0=gg[:], scalar1=-1.0, scalar2=None, op0=ALU.mult),
                 reads=[B_coef], writes=[B_coef])

            for (dst, Bd, bias_col) in [(qT, B_q, 2), (kT, B_k, 5)]:
                for h in range(4):
                    for st5 in range(5):
                        bsl = Bd[h][st5 * 4:st5 * 4 + 4]
                        sl = dst[:, h, st5 * 512:(st5 + 1) * 512]
                        sq_, bsq = tBR.next()
                        S.op("pool", lambda sq_=sq_, sl=sl: nc.gpsimd.tensor_tensor(out=sq_, in0=sl, in1=sl, op=ALU.mult),
                             reads=bsl, writes=[bsq])
                        pp, bp = psum()
                        S.op("pe", lambda pp=pp, sq_=sq_: nc.tensor.matmul(pp, lhsT=ones_b, rhs=sq_, start=True, stop=True),
                             reads=[bsq, B_const], writes=[bp])
                        rs_, brs = tAR.next()
                        S.op("act", lambda pp=pp, rs_=rs_: nc.scalar.activation(out=rs_, in_=pp, func=AF.Ln, bias=cols[:, 0:1],
                                                                                scale=1.0), reads=[bp, B_const], writes=[brs])
                        S.op("act", lambda rs_=rs_, bias_col=bias_col: nc.scalar.activation(
                            out=rs_, in_=rs_, func=AF.Exp, bias=cols[:, bias_col:bias_col + 1], scale=-0.5),
                            reads=[brs, B_const], writes=[brs])
                        S.op("dve", lambda sl=sl, rs_=rs_: nc.vector.tensor_tensor(out=sl, in0=sl, in1=rs_, op=ALU.mult),
                             reads=[brs] + bsl, writes=bsl)
            S.barrier()
        S.barrier()

        S.mark('a2')
        if debug:
            for nm, t_, shp in [("qT", qT, [128, 4, NT]), ("kT", kT, [128, 4, NT]), ("vT", vT, [128, 4, NT]),
                                ("ybT", ybT, [128, 4, NT]), ("zs", zs, [128, NCH, 512]), ("hT", hT, [128, 8, NT])]:
                dd = dout("dbg_" + nm, shp, BF16)
                S.dma("sp", lambda dd=dd, t_=t_: nc.sync.dma_start(out=dd, in_=t_[:]), stream="dbg")
            for nm, t_, shp in [("beta", beta, [128, NCH, 8]), ("gg", gg, [128, NCH, 8])]:
                dd = dout("dbg_" + nm, shp, F32)
                S.dma("sp", lambda dd=dd, t_=t_: nc.sync.dma_start(out=dd, in_=t_[:]), stream="dbg")
            S.barrier()

        wc_jobs = []
        for (nm, src, dst, rows) in [("wo", w_out_d, wo_s, D), ("wd", wd_d, wd_s, DFF), ("wg", wg_d, wg_s, NF * 128), ("wu", wu_d, wu_s, NF * 128)]:
            r0 = 0
            while r0 < rows:
                r1 = min(rows, r0 + 512)
                bq = S.buf(f"wscr_{nm}_{r0}")
                B_wscr[nm].append(bq)
                wc_jobs.append((src, dst, r0, r1, bq))
                r0 = r1

        def issue_wc(k):
            for _ in range(k):
                if wc_jobs:
                    src, dst, r0, r1, bq = wc_jobs.pop(0)
                    S.dma("pool", lambda src=src, dst=dst, r0=r0, r1=r1: nc.gpsimd.dma_start(out=dst[r0:r1, :], in_=src[r0:r1, :]),
                          writes=[bq], stream="wc")

        with contextlib.ExitStack() as stB:
            gn_b = sb("gn_b", [128, 512], stack=stB)
            B_gn = S.buf("gn")
            for h in range(4):
                S.dma("sp", lambda h=h: nc.sync.dma_start(out=gn_b[:, h * 128:(h + 1) * 128],
                                                          in_=dng_d.partition_broadcast(128)[:, 0, :]),
                      writes=[B_gn], stream="c0")
            hflat = hT[:].rearrange("p a b -> p (a b)")
            O = hflat[:, 0:NCH * 512].rearrange("p (c f) -> p c f", c=NCH)
            B_O = [S.buf(f"O{c}") for c in range(NCH)]
            spare = [NCH * 512]

            def carve(name, n):
                aps = []
                for i in range(n):
                    aps.append(hflat[:, spare[0]:spare[0] + 512])
                    spare[0] += 512
                assert spare[0] <= 8 * NT
                return Ring(S, aps, name)

            def t512(name, dt, n):
                ts = [sb(f"{name}{i}", [128, 512], dt, stack=stB) for i in range(n)]
                return Ring(S, [t[:] for t in ts], name)

            _car = carve("chainc", 10)
            _all = t512("chaina", BF16, 11)
            chR = Ring(S, [it[0] for it in _car.items] + [it[0] for it in _all.items], "chain")
            M0R = t512("M0", BF16, 2)
            MT0R = t512("MT0", BF16, 2)
            vbR = carve("vbeta", 4)
            def mix(name, ncar, nall):
                a = carve(name + "c", ncar)
                b = t512(name + "a", BF16, nall)
                return Ring(S, [it[0] for it in a.items] + [it[0] for it in b.items], name)
            wTR = mix("wT", 2, 2)
            qkTR = mix("qkT", 2, 2)
            kdR = mix("kdec", 2, 2)
            f32R = t512("f32t", F32, 2)
            ER = t512("E", BF16, 2)
            uR = t512("u", BF16, 4)
            vnR = t512("vn", BF16, 2)
            otR = t512("ot", F32, 2)
            Sst = [sb(f"Sst{d}", [128, 512], stack=stB)[:] for d in range(2)]
            Sbf = [sb(f"Sbf{d}", [128, 512], BF16, stack=stB)[:] for d in range(2)]
            B_S = [S.buf("S0"), S.buf("S1")]
            B_Sb = [S.buf("Sb0"), S.buf("Sb1")]

            pg, bpg = psum()
            pgv = pg[:, 0:NCH * 16].rearrange("p (c k) -> p c k", k=16)
            for d in range(2):
                S.op("pe", lambda d=d: nc.tensor.matmul(pgv[:, :, d * 4:d * 4 + 4], lhsT=U[d], rhs=gg[:, :, d * 4:d * 4 + 4],
                                                        start=True, stop=True), reads=[B_coef, B_const], writes=[bpg])
            S.op("pe", lambda: nc.tensor.matmul(pgv[:, :, 8:16], lhsT=ones_f, rhs=gg[:, :, :], start=True, stop=True),
                 reads=[B_coef, B_const], writes=[bpg])
            S.op("dve", lambda: nc.vector.tensor_copy(out=gcl[:].rearrange("p c k -> p (c k)"), in_=pg[:, 0:NCH * 16]),
                 reads=[bpg], writes=[B_coef])
            S.op("act", lambda: nc.scalar.activation(out=egc[:], in_=gcl[:, :, 0:8], func=AF.Exp), reads=[B_coef], writes=[B_coef])
            S.op("act", lambda: nc.scalar.activation(out=egl[:], in_=gcl[:, :, 8:16], func=AF.Exp), reads=[B_coef], writes=[B_coef])
            S.op("dve", lambda: nc.vector.tensor_tensor(out=ekd[:], in0=gcl[:, :, 8:16], in1=gcl[:, :, 0:8], op=ALU.subtract),
                 reads=[B_coef], writes=[B_coef])
            S.op("act", lambda: nc.scalar.activation(out=ekd[:], in_=ekd[:], func=AF.Exp), reads=[B_coef], writes=[B_coef])
            S.op("dve", lambda: nc.vector.tensor_tensor(out=bege[:], in0=beta[:], in1=egc[:], op=ALU.mult),
                 reads=[B_coef], writes=[B_coef])

            S.mark('b0')
            def bc(t, c, d):
                return t[:, c, d * 4:d * 4 + 4].unsqueeze(2).to_broadcast([128, 4, 128])

            def v3(ap):
                return ap.rearrange("p (h j) -> p h j", h=4)

            def per_head(engname, out512, in_fn, coef, c, d, op, reads, writes):
                for h in range(4):
                    hs = slice(h * 128, (h + 1) * 128)
                    sc = coef[:, c, d * 4 + h:d * 4 + h + 1]
                    if engname == "pool":
                        if op == "mult":
                            S.op("pool", lambda hs=hs, sc=sc, h=h: nc.gpsimd.tensor_scalar(
                                out=out512[:, hs], in0=in_fn(h), scalar1=sc, scalar2=0.0, op0=ALU.mult, op1=ALU.add),
                                reads=reads, writes=writes)
                        else:
                            S.op("pool", lambda hs=hs, sc=sc, h=h: nc.gpsimd.tensor_scalar(
                                out=out512[:, hs], in0=in_fn(h), scalar1=sc, scalar2=1.0, op0=ALU.add, op1=ALU.mult),
                                reads=reads, writes=writes)
                    else:
                        S.op("dve", lambda hs=hs, sc=sc, h=h: nc.vector.tensor_scalar(
                            out=out512[:, hs], in0=in_fn(h), scalar1=sc, scalar2=None,
                            op0=(ALU.mult if op == "mult" else ALU.add)), reads=reads, writes=writes)

            def prep_shared(c, d):
                cs_ = slice(c * 128, (c + 1) * 128)
                vb_, bvb = vbR.next()
                kb_, bkb = vbR.next()
                kd_, bkd = kdR.next()
                pkk, bkk = psum()
                pqk, bqk = psum()
                for h in range(4):
                    S.op("pe", lambda h=h: nc.tensor.matmul(pkk[:, h * 128:(h + 1) * 128], lhsT=kT[:, h, cs_], rhs=kT[:, h, cs_],
                                                            start=True, stop=True), reads=[B_k[h][c]], writes=[bkk])
                for h in range(4):
                    S.op("pe", lambda h=h: nc.tensor.matmul(pqk[:, h * 128:(h + 1) * 128], lhsT=qT[:, h, cs_], rhs=kT[:, h, cs_],
                                                            start=True, stop=True), reads=[B_k[h][c], B_q[h][c]], writes=[bqk])
                return dict(vb=vb_, bvb=bvb, kb=kb_, bkb=bkb, kd=kd_, bkd=bkd, pkk=pkk, bkk=bkk, pqk=pqk, bqk=bqk)

            def gen_kv(c, d, sh):
                cs_ = slice(c * 128, (c + 1) * 128)
                kb_, bkb, kd_, bkd, vb_, bvb = sh["kb"], sh["bkb"], sh["kd"], sh["bkd"], sh["vb"], sh["bvb"]
                ptk, bptk = psum()
                ptkb = ptk.bitcast(BF16)
                for h in range(4):
                    S.op("pe", lambda h=h: nc.tensor.transpose(ptkb[:, h * 128:(h + 1) * 128], kT[:, h, cs_], ident_b),
                         reads=[B_k[h][c], B_const], writes=[bptk])
                for h in range(4):
                    hs = slice(h * 128, (h + 1) * 128)
                    k_ = d * 4 + h
                    S.op("act", lambda hs=hs, k_=k_: nc.scalar.activation(out=kb_[:, hs], in_=ptkb[:, hs], func=AF.Identity, scale=bege[:, c, k_:k_ + 1]),
                         reads=[bptk, B_coef], writes=[bkb])
                    S.op("act", lambda hs=hs, k_=k_: nc.scalar.activation(out=kd_[:, hs], in_=ptkb[:, hs], func=AF.Identity, scale=ekd[:, c, k_:k_ + 1]),
                         reads=[bptk, B_coef], writes=[bkd])
                yield
                ptv, bptv = psum()
                ptvb = ptv.bitcast(BF16)
                for h in range(4):
                    S.op("pe", lambda h=h: nc.tensor.transpose(ptvb[:, h * 128:(h + 1) * 128], vT[:, h, cs_], ident_b),
                         reads=[B_v[h][c], B_const], writes=[bptv])
                for h in range(4):
                    hs = slice(h * 128, (h + 1) * 128)
                    k_ = d * 4 + h
                    S.op("act", lambda hs=hs, k_=k_: nc.scalar.activation(out=vb_[:, hs], in_=ptvb[:, hs], func=AF.Identity, scale=beta[:, c, k_:k_ + 1]),
                         reads=[bptv, B_coef], writes=[bvb])
                yield

            def prep_dir(c, d, sh, st):
                ug, bug = f32R.next()
                per_head("pool", ug, lambda h: U[d], negg, c, d, "mult", [B_const, B_coef], [bug])
                gm, bgm = f32R.next()
                per_head("dve", gm, lambda h: MN[d], gcl, c, d, "add", [B_const, B_coef], [bgm])
                pd, bpd = psum()
                S.op("pe", lambda: nc.tensor.matmul(pd, lhsT=ones_f, rhs=ug, start=True, stop=False), reads=[bug, B_const], writes=[bpd])
                S.op("pe", lambda: nc.tensor.matmul(pd, lhsT=ident_f, rhs=gm, start=False, stop=True), reads=[bgm, B_const], writes=[bpd])
                E, bE = ER.next()
                S.op("act", lambda: nc.scalar.activation(out=E, in_=pd, func=AF.Exp), reads=[bpd], writes=[bE])
                yield
                M, bM = M0R.next()
                for h in range(4):
                    hs = slice(h * 128, (h + 1) * 128)
                    k_ = d * 4 + h
                    S.op("dve", lambda hs=hs, k_=k_: nc.vector.scalar_tensor_tensor(
                        out=M[:, hs], in0=sh["pkk"][:, hs], scalar=negb[:, c, k_:k_ + 1], in1=E[:, hs], op0=ALU.mult, op1=ALU.mult),
                        reads=[sh["bkk"], bE, B_coef], writes=[bM])
                qkm, bqkm = chR.next()
                S.op("dve", lambda: nc.vector.tensor_tensor(out=qkm, in0=sh["pqk"], in1=E, op=ALU.mult), reads=[sh["bqk"], bE], writes=[bqkm])
                yield
                pt, bpt = psum()
                ptb = pt.bitcast(BF16)
                for h in range(4):
                    S.op("pe", lambda h=h: nc.tensor.transpose(ptb[:, h * 128:(h + 1) * 128], M[:, h * 128:(h + 1) * 128], ident_b),
                         reads=[bM, B_const], writes=[bpt])
                MT, bMT = MT0R.next()
                S.op("act", lambda: nc.scalar.copy(out=MT, in_=ptb[:, 0:512]), reads=[bpt], writes=[bMT])
                pt2, bpt2 = psum()
                ptb2 = pt2.bitcast(BF16)
                for h in range(4):
                    S.op("pe", lambda h=h: nc.tensor.transpose(ptb2[:, h * 128:(h + 1) * 128], qkm[:, h * 128:(h + 1) * 128], ident_b),
                         reads=[bqkm, B_const], writes=[bpt2])
                qkT_, bqkT = qkTR.next()
                S.op("act", lambda: nc.scalar.copy(out=qkT_, in_=ptb2[:, 0:512]), reads=[bpt2], writes=[bqkT])
                st.update(M0=M, bM0=bM, MT0=MT, bMT0=bMT, qkT=qkT_, bqkT=bqkT, d=d)

            def masked(src, bsrc, m, engname):
                o, bo = chR.next()
                mk = MASKS[m].unsqueeze(1).to_broadcast([128, 4, 128])
                if engname == "pool":
                    S.op("pool", lambda: nc.gpsimd.tensor_tensor(out=v3(o), in0=v3(src), in1=mk, op=ALU.mult), reads=[bsrc, B_const], writes=[bo])
                else:
                    S.op("dve", lambda: nc.vector.tensor_tensor(out=v3(o), in0=v3(src), in1=mk, op=ALU.mult), reads=[bsrc, B_const], writes=[bo])
                return o, bo

            def mm4(lhs, blhs, rhs, brhs, acc=None, bacc=None):
                pp, bp = psum()
                if acc is not None:
                    S.op("pe", lambda: nc.tensor.matmul(pp, lhsT=ident_b, rhs=acc, start=True, stop=False), reads=[bacc, B_const], writes=[bp])
                for h in range(4):
                    hs = slice(h * 128, (h + 1) * 128)
                    S.op("pe", lambda hs=hs, h=h: nc.tensor.matmul(pp[:, hs], lhsT=lhs[:, hs], rhs=rhs[:, hs], start=(acc is None),
                                                                   stop=(h == 3 or acc is None)), reads=[blhs, brhs], writes=[bp])
                return pp, bp

            def evac(pp, bp, engname):
                o, bo = chR.next()
                if engname == "act":
                    S.op("act", lambda: nc.scalar.copy(out=o, in_=pp), reads=[bp], writes=[bo])
                else:
                    S.op("dve", lambda: nc.vector.tensor_copy(out=o, in_=pp), reads=[bp], writes=[bo])
                return o, bo

            def evac_add(pp, bp, acc, bacc):
                o, bo = chR.next()
                S.op("dve", lambda: nc.vector.tensor_tensor(out=o, in0=pp, in1=acc, op=ALU.add), reads=[bp, bacc], writes=[bo])
                return o, bo

            def transp(src, bsrc):
                pt, bpt = psum()
                ptb = pt.bitcast(BF16)
                for h in range(4):
                    S.op("pe", lambda h=h: nc.tensor.transpose(ptb[:, h * 128:(h + 1) * 128], src[:, h * 128:(h + 1) * 128], ident_b),
                         reads=[bsrc, B_const], writes=[bpt])
                o, bo = chR.next()
                S.op("act", lambda: nc.scalar.copy(out=o, in_=ptb[:, 0:512]), reads=[bpt], writes=[bo])
                return o, bo

            def chain_stages(st):
                M0, bM0, MT0, bMT0 = st["M0"], st["bM0"], st["MT0"], st["bMT0"]
                Mk, bMk = masked(M0, bM0, 0, "pool")
                MTk, bMTk = masked(MT0, bMT0, 0, "dve")
                PT, bPT = chR.next()
                S.op("dve", lambda PT=PT, MTk=MTk: nc.vector.tensor_tensor(
                    out=v3(PT), in0=v3(MTk), in1=ident_b.unsqueeze(1).to_broadcast([128, 4, 128]), op=ALU.add),
                    reads=[bMTk, B_const], writes=[bPT])
                yield
                for lev in range(3):
                    pM, bpM = mm4(MTk, bMTk, Mk, bMk)
                    nM, bnM = evac(pM, bpM, "act")
                    if lev < 2:
                        pMT, bpMT = mm4(Mk, bMk, MTk, bMTk)
                        nMT, bnMT = evac(pMT, bpMT, "act")
                    yield
                    pP, bpP = mm4(nM, bnM, PT, bPT)
                    PT, bPT = evac_add(pP, bpP, PT, bPT)
                    Mk, bMk = nM, bnM
                    if lev < 2:
                        MTk, bMTk = nMT, bnMT
                    yield
                Tt, bTt = PT, bPT
                pt, bpt = psum()
                ptb = pt.bitcast(BF16)
                for h in range(4):
                    S.op("pe", lambda h=h, Tt=Tt, ptb=ptb: nc.tensor.transpose(ptb[:, h * 128:(h + 1) * 128], Tt[:, h * 128:(h + 1) * 128], ident_b),
                         reads=[bTt, B_const], writes=[bpt])
                T, bT = chR.next()
                S.op("act", lambda T=T, ptb=ptb: nc.scalar.copy(out=T, in_=ptb[:, 0:512]), reads=[bpt], writes=[bT])
                Moff, bMoff = masked(M0, bM0, 1, "pool")
                yield
                for li in range(3):
                    last = (li == 2)
                    pX, bpX = mm4(Moff, bMoff, Tt, bTt)
                    X, bX = evac(pX, bpX, "act")
                    if li > 0:
                        T, bT = transp(Tt, bTt)
                    yield
                    pT2, bpT2 = mm4(T, bT, X, bX)
                    Tt, bTt = evac_add(pT2, bpT2, Tt, bTt)
                    if not last:
                        Moff, bMoff = masked(M0, bM0, 2 + li, "pool")
                    yield
                st["PT"], st["bPT"] = Tt, bTt

            def prep_finish(c, d, sh, st):
                Tt, bTt = st["PT"], st["bPT"]
                vb_, bvb, kb_, bkb, kd_, bkd = sh["vb"], sh["bvb"], sh["kb"], sh["bkb"], sh["kd"], sh["bkd"]
                pu, bpu = psum()
                pw, bpw = psum()
                for h in range(4):
                    hs = slice(h * 128, (h + 1) * 128)
                    S.op("pe", lambda hs=hs: nc.tensor.matmul(pu[:, hs], lhsT=Tt[:, hs], rhs=vb_[:, hs], start=True, stop=True),
                         reads=[bTt, bvb], writes=[bpu])
                for h in range(4):
                    hs = slice(h * 128, (h + 1) * 128)
                    S.op("pe", lambda hs=hs: nc.tensor.matmul(pw[:, hs], lhsT=kb_[:, hs], rhs=Tt[:, hs], start=True, stop=True),
                         reads=[bTt, bkb], writes=[bpw])
                u_, bu = uR.next()
                wT_, bwT = wTR.next()
                S.op("act", lambda: nc.scalar.copy(out=u_, in_=pu), reads=[bpu], writes=[bu])
                S.op("dve", lambda: nc.vector.tensor_copy(out=wT_, in_=pw), reads=[bpw], writes=[bwT])
                st.update(u=u_, bu=bu, wT=wT_, bwT=bwT, kd=kd_, bkd=bkd)
                if c in (0, 1) and (c == d):
                    dbg(f"Tt{d}", Tt, [128, 512], BF16, [bTt])
                    dbg(f"u{d}", u_, [128, 512], BF16, [bu])
                    dbg(f"wT{d}", wT_, [128, 512], BF16, [bwT])
                    dbg(f"kd{d}", kd_, [128, 512], BF16, [bkd])
                    dbg(f"qkT{d}", st['qkT'], [128, 512], BF16, [st['bqkT']])

            def scan_step(c, d, st, first_dir_for_chunk):
                cs_ = slice(c * 128, (c + 1) * 128)
                pv, bpv = psum()
                for h in range(4):
                    hs = slice(h * 128, (h + 1) * 128)
                    S.op("pe", lambda hs=hs: nc.tensor.matmul(pv[:, hs], lhsT=st["wT"][:, hs], rhs=Sbf[d][:, hs], start=True, stop=True),
                         reads=[st["bwT"], B_Sb[d]], writes=[bpv])
                vn, bvn = vnR.next()
                S.op("dve", lambda: nc.vector.tensor_tensor(out=vn, in0=st["u"], in1=pv, op=ALU.subtract), reads=[st["bu"], bpv], writes=[bvn])
                yield
                po1, bpo1 = psum()
                po2, bpo2 = psum()
                pS, bpS = psum()
                for h in range(4):
                    hs = slice(h * 128, (h + 1) * 128)
                    S.op("pe", lambda hs=hs, h=h: nc.tensor.matmul(po1[:, hs], lhsT=qT[:, h, cs_], rhs=Sbf[d][:, hs], start=True, stop=True),
                         reads=[B_q[h][c], B_Sb[d]], writes=[bpo1])
                for h in range(4):
                    hs = slice(h * 128, (h + 1) * 128)
                    S.op("pe", lambda hs=hs: nc.tensor.matmul(po2[:, hs], lhsT=st["qkT"][:, hs], rhs=vn[:, hs], start=True, stop=True),
                         reads=[st["bqkT"], bvn], writes=[bpo2])
                for h in range(4):
                    hs = slice(h * 128, (h + 1) * 128)
                    S.op("pe", lambda hs=hs: nc.tensor.matmul(pS[:, hs], lhsT=st["kd"][:, hs], rhs=vn[:, hs], start=True, stop=True),
                         reads=[st["bkd"], bvn], writes=[bpS])
                ot, bot = otR.next()
                per_head("dve", ot, lambda h: po1[:, h * 128:(h + 1) * 128], egc, c, d, "mult", [bpo1, B_coef], [bot])
                if first_dir_for_chunk:
                    S.op("dve", lambda: nc.vector.tensor_tensor(out=O[:, c, :], in0=ot, in1=po2, op=ALU.add),
                         reads=[bot, bpo2], writes=[B_O[c]])
                else:
                    S.op("dve", lambda: nc.vector.tensor_tensor(out=ot, in0=ot, in1=po2, op=ALU.add), reads=[bot, bpo2], writes=[bot])
                    S.op("pool", lambda: nc.gpsimd.tensor_tensor(out=ot, in0=ot, in1=O[:, c, :], op=ALU.add),
                         reads=[bot, B_O[c]], writes=[bot])
                for h in range(4):
                    hs = slice(h * 128, (h + 1) * 128)
                    S.op("dve", lambda hs=hs, h=h: nc.vector.scalar_tensor_tensor(
                        out=Sst[d][:, hs], in0=Sst[d][:, hs], scalar=egl[:, c, d * 4 + h:d * 4 + h + 1], in1=pS[:, hs],
                        op0=ALU.mult, op1=ALU.add), reads=[B_S[d], bpS, B_coef], writes=[B_S[d]])
                S.op("act", lambda: nc.scalar.copy(out=Sbf[d], in_=Sst[d]), reads=[B_S[d]], writes=[B_Sb[d]])
                yield
                if c in (0, 1) and (c == d):
                    dbg(f"vn{d}", vn, [128, 512], BF16, [bvn])
                    dbg(f"S{d}", Sst[d], [128, 512], F32, [B_S[d]])
                    dbg(f"O{d}", O[:, c, :], [128, 512], BF16, [B_O[c]])
                if not first_dir_for_chunk:
                    gate_chunk(c, ot, bot)

            gtmp = [sb(f"gtmp{i}", [128, 512], stack=stB) for i in range(1)]
            gR = Ring(S, [t[:] for t in gtmp], "gtmp")
            gsm = [sb(f"gsm{i}", [128, 8], stack=stB) for i in range(2)]
            gsR = Ring(S, [t[:] for t in gsm], "gsm")
            yab = [sb(f"yab{i}", [128, 512], BF16, stack=stB) for i in range(1)]
            yaR = Ring(S, [t[:] for t in yab], "yab")

            def gate_chunk(c, osum, bos):
                t1, b1 = gR.next()
                S.op("pool", lambda: nc.gpsimd.tensor_tensor(out=t1, in0=osum, in1=osum, op=ALU.mult), reads=[bos], writes=[b1])
                sm, bsm = gsR.next()
                S.op("dve", lambda: nc.vector.tensor_reduce(out=sm[:, 0:4], in_=v3(t1), axis=AX.X, op=ALU.add), reads=[b1], writes=[bsm])
                S.op("pool", lambda: nc.gpsimd.tensor_scalar(out=sm[:, 0:4], in0=sm[:, 0:4], scalar1=1.0 / 128, scalar2=EPS,
                                                             op0=ALU.mult, op1=ALU.add), reads=[bsm], writes=[bsm])
                S.op("pool", lambda: nc.gpsimd.tensor_tensor(out=sm[:, 4:8], in0=sm[:, 0:4], in1=mh4[:], op=ALU.pow),
                     reads=[bsm, B_const], writes=[bsm])
                for h in range(4):
                    S.op("dve", lambda h=h: nc.vector.tensor_scalar(out=t1[:, h * 128:(h + 1) * 128], in0=osum[:, h * 128:(h + 1) * 128],
                                                                    scalar1=sm[:, 4 + h:5 + h], scalar2=None, op0=ALU.mult),
                         reads=[bos, bsm], writes=[b1])
                S.op("pool", lambda: nc.gpsimd.tensor_tensor(out=t1, in0=t1, in1=gn_b[:], op=ALU.mult), reads=[b1, B_gn], writes=[b1])
                ya, bya = yaR.next()
                S.op("dve", lambda: nc.vector.tensor_tensor(out=ya, in0=t1, in1=zs[:, c, :], op=ALU.mult), reads=[b1, B_z[c]], writes=[bya])
                pt, bpt = psum()
                ptb = pt.bitcast(BF16)
                for h in range(4):
                    S.op("pe", lambda h=h: nc.tensor.transpose(ptb[:, h * 128:(h + 1) * 128], ya[:, h * 128:(h + 1) * 128], ident_b),
                         reads=[bya, B_const], writes=[bpt])
                S.op("act", lambda: nc.scalar.copy(out=qT[:, :, c * 128:(c + 1) * 128], in_=v3(ptb[:, 0:512])),
                     reads=[bpt], writes=[B_q[h][c] for h in range(4)])

            for si, (c0, n, r) in enumerate(SEQS):
                for d in range(2):
                    if si == 2:
                        S.dma("sp", lambda d=d: nc.sync.dma_start(out=Sst[d][:], in_=s0_d[d]), writes=[B_S[d]], stream=f"s0{d}")
                    else:
                        S.op("pool", lambda d=d: nc.gpsimd.memset(Sst[d][:], 0.0), writes=[B_S[d]])
                    S.op("act", lambda d=d: nc.scalar.copy(out=Sbf[d][:], in_=Sst[d][:]), reads=[B_S[d]], writes=[B_Sb[d]])
                pending = []

                def run_rr(gens):
                    alive = [True] * len(gens)
                    while any(alive):
                        for gi in range(len(gens)):
                            if alive[gi]:
                                try:
                                    next(gens[gi])
                                except StopIteration:
                                    alive[gi] = False

                for t in range(n):
                    cf, cb = c0 + t, c0 + n - 1 - t
                    assert cf != cb
                    issue_wc(2)
                    shs = {cf: prep_shared(cf, 0)}
                    shs[cb] = prep_shared(cb, 1)
                    sts = [{}, {}]
                    run_rr([prep_dir(cf, 0, shs[cf], sts[0]), prep_dir(cb, 1, shs[cb], sts[1])])
                    run_rr([chain_stages(sts[0]), chain_stages(sts[1])] + pending + [gen_kv(cf, 0, shs[cf]), gen_kv(cb, 1, shs[cb])])
                    prep_finish(cf, 0, shs[cf], sts[0])
                    prep_finish(cb, 1, shs[cb], sts[1])
                    pending = [scan_step(cf, 0, sts[0], (cf < cb)), scan_step(cb, 1, sts[1], (cf < cb))]
                run_rr(pending)
                if si < 2:
                    for d, od in [(0, sf_d), (1, sb_d)]:
                        S.dma("sp", lambda d=d, od=od, si=si: nc.sync.dma_start(
                            out=od[si].rearrange("h k v -> k h v"), in_=Sst[d][:].rearrange("p (h v) -> p h v", h=4)),
                            reads=[B_S[d]], stream="so")
            issue_wc(100)
            S.barrier()
        S.barrier()
    S.barrier()

    S.mark('b')
    if debug:
        dd = dout("dbg_yaT", [128, 4, NT], BF16)
        S.dma("sp", lambda: nc.sync.dma_start(out=dd, in_=qT[:]), stream="dbg")
        S.barrier()

    with contextlib.ExitStack() as stC:
        wo = sb("wo", [128, 8, D], BF16, stack=stC)
        wd = sb("wd", [128, NF, D], BF16, stack=stC)
        B_wo = S.buf("wo")
        B_wd = S.buf("wd")
        wgu = [sb(f"wgu{i}", [128, 2, D], BF16, stack=stC) for i in range(2)]
        wguR = Ring(S, [t[:] for t in wgu], "wgu")
        wgu_b2 = [S.buf("wgu2_0"), S.buf("wgu2_1")]
        h2T = sb("h2T", [128, 8, 512], BF16, stack=stC)
        B_h2 = [S.buf(f"h2_{i}") for i in range(4)]
        actT = sb("actT", [128, NF, 512], BF16, stack=stC)
        B_act = [S.buf(f"act{f}") for f in range(NF)]
        x1s = [sb(f"x1_{j}", [128, 4, D], stack=stC) for j in range(2)]
        B_x1s = [[S.buf(f"x1_{j}_{i}") for i in range(4)] for j in range(2)]
        gb = sb("gb", [128, 2, D], stack=stC)
        B_gb = S.buf("gb")
        fgb = sb("fgb", [128, D], stack=stC)
        B_fg = S.buf("fgb")
        xh2 = [sb(f"xh2{i}", [128, D], BF16, stack=stC) for i in range(2)]
        xhR2 = Ring(S, [t[:] for t in xh2], "xh2")
        sqR2 = xhR2
        ss2 = [sb(f"ss2{i}", [128, 4], stack=stC) for i in range(4)]
        ssR2 = Ring(S, [t[:] for t in ss2], "ss2")
        sg = [sb(f"sg{i}", [128, 512], stack=stC) for i in range(2)]
        sgR = Ring(S, [t[:] for t in sg], "sg")
        sqJ = Ring(S, [], "sqJ")
        sqJ.items = [(it[0].bitcast(BF16), it[1]) for it in sgR.items]
        tc_ = [sb(f"tc{i}", [128, 512], stack=stC) for i in range(2)]
        tcR = Ring(S, [t[:] for t in tc_], "tc")

        S.dma("sp", lambda: nc.sync.dma_start(out=wo[:], in_=wo_s.rearrange("(kc p) n -> p kc n", p=128)),
              reads=B_wscr["wo"], writes=[B_wo], stream="wo")
        S.dma("sp", lambda: nc.sync.dma_start(out=fgb[:], in_=fg_d.partition_broadcast(128)[:, 0, :]), writes=[B_fg], stream="wo")
        S.dma("sp", lambda: nc.sync.dma_start(out=wd[:], in_=wd_s.rearrange("(fc p) n -> p fc n", p=128)),
              reads=B_wscr["wd"], writes=[B_wd], stream="wd")

        for st5 in range(5):
            r = 0 if st5 == 0 else 1
            x1 = x1s[st5 % 2]
            B_x1 = B_x1s[st5 % 2]
            if st5 in (0, 1):
                for gi, seg in enumerate([2, 5]):
                    S.dma("sp", lambda gi=gi, seg=seg, r=r: nc.sync.dma_start(
                        out=gb[:, gi, :], in_=mod_d[r:r + 1, seg * D:(seg + 1) * D].partition_broadcast(128)[:, 0, :]),
                        reads=B_mblk[seg * 2:seg * 2 + 2], writes=[B_gb], stream="gb")
            for sub in range(4):
                c = st5 * 4 + sub
                S.dma("sp", lambda sub=sub, c=c, x1=x1: nc.sync.dma_start(out=x1[:, sub, :], in_=x_d[c * 128:(c + 1) * 128, :]),
                      writes=[B_x1[sub]], stream=f"x1_{sub}")
            def c_stage_a(sub, x1=x1, B_x1=B_x1, st5=st5):
                c = st5 * 4 + sub
                for half in range(2):
                    pp, bp = psum()
                    for kc in range(8):
                        src = qT if kc < 4 else ybT
                        bsrc = B_q[kc][c] if kc < 4 else B_yb[c]
                        S.op("pe", lambda pp=pp, kc=kc, c=c, half=half, src=src: nc.tensor.matmul(
                            pp, lhsT=src[:, kc % 4, c * 128:(c + 1) * 128], rhs=wo[:, kc, half * 512:(half + 1) * 512],
                            start=(kc == 0), stop=(kc == 7)), reads=[bsrc, B_wo], writes=[bp])
                    tt, btt = tcR.next()
                    S.op("dve", lambda pp=pp, tt=tt, half=half: nc.vector.tensor_tensor(
                        out=tt, in0=pp, in1=gb[:, 0, half * 512:(half + 1) * 512], op=ALU.mult), reads=[bp, B_gb], writes=[btt])
                    S.op("pool", lambda tt=tt, sub=sub, half=half, x1=x1: nc.gpsimd.tensor_tensor(
                        out=x1[:, sub, half * 512:(half + 1) * 512], in0=x1[:, sub, half * 512:(half + 1) * 512], in1=tt, op=ALU.add),
                        reads=[btt, B_x1[sub]], writes=[B_x1[sub]])
                return norm_stage_a(x1[:, sub, :], B_x1[sub], sqJ, xhR2, ssR2)

            cur = c_stage_a(0)
            for sub in range(4):
                nxt = c_stage_a(sub + 1) if sub + 1 < 4 else None
                norm_stage_b(cur[0], cur[1], 3, 2, r, h2T, sub * 128, B_h2[sub], B_mod2)
                cur = nxt
            for f in range(NF):
                wt, bw = wguR.next()
                bw2 = wgu_b2[(wguR.i - 1) % 2]
                S.dma("sp", lambda wt=wt, f=f: nc.sync.dma_start(out=wt[:, 0, :], in_=wg_s[f * 128:(f + 1) * 128, :]),
                      reads=B_wscr["wg"], writes=[bw], stream=f"wgu{f % 3}")
                S.dma("sp", lambda wt=wt, f=f: nc.sync.dma_start(out=wt[:, 1, :], in_=wu_s[f * 128:(f + 1) * 128, :]),
                      reads=B_wscr["wu"], writes=[bw2], stream=f"wgu{f % 3}")
                pg_, bg_ = psum()
                pu_, bu_ = psum()
                for (pp, bp, wi) in [(pg_, bg_, 0), (pu_, bu_, 1)]:
                    for kc in range(8):
                        S.op("pe", lambda pp=pp, kc=kc, wt=wt, wi=wi: nc.tensor.matmul(
                            pp, lhsT=wt[:, wi, kc * 128:(kc + 1) * 128], rhs=h2T[:, kc, :], start=(kc == 0), stop=(kc == 7)),
                            reads=[bw if wi == 0 else bw2] + B_h2, writes=[bp])
                sgt, bsg = sgR.next()
                S.op("act", lambda pg_=pg_, sgt=sgt: nc.scalar.activation(out=sgt, in_=pg_, func=AF.Silu), reads=[bg_], writes=[bsg])
                S.op("dve", lambda pu_=pu_, sgt=sgt, f=f: nc.vector.tensor_tensor(out=actT[:, f, :], in0=pu_, in1=sgt, op=ALU.mult),
                     reads=[bu_, bsg], writes=[B_act[f]])
            for sub in range(4):
                c = st5 * 4 + sub
                for half in range(2):
                    pp, bp = psum()
                    for f in range(NF):
                        S.op("pe", lambda pp=pp, f=f, sub=sub, half=half: nc.tensor.matmul(
                            pp, lhsT=actT[:, f, sub * 128:(sub + 1) * 128], rhs=wd[:, f, half * 512:(half + 1) * 512],
                            start=(f == 0), stop=(f == NF - 1)), reads=[B_act[f], B_wd], writes=[bp])
                    tt, btt = tcR.next()
                    S.op("dve", lambda pp=pp, tt=tt, half=half: nc.vector.tensor_tensor(
                        out=tt, in0=pp, in1=gb[:, 1, half * 512:(half + 1) * 512], op=ALU.mult), reads=[bp, B_gb], writes=[btt])
                    S.op("pool", lambda tt=tt, sub=sub, half=half, x1=x1: nc.gpsimd.tensor_tensor(
                        out=x1[:, sub, half * 512:(half + 1) * 512], in0=x1[:, sub, half * 512:(half + 1) * 512], in1=tt, op=ALU.add),
                        reads=[btt, B_x1[sub]], writes=[B_x1[sub]])
                sq, bsq = sqR2.next()
                ss, bss = ssR2.next()
                S.op("act", lambda sq=sq, ss=ss, sub=sub, x1=x1: nc.scalar.activation(out=sq, in_=x1[:, sub, :], func=AF.Square,
                                                                              accum_out=ss[:, 0:1]), reads=[B_x1[sub]], writes=[bsq, bss])
                S.op("pool", lambda ss=ss: nc.gpsimd.tensor_scalar(out=ss[:, 1:2], in0=ss[:, 0:1], scalar1=1.0 / D, scalar2=EPS,
                                                                   op0=ALU.mult, op1=ALU.add), reads=[bss], writes=[bss])
                S.op("pool", lambda ss=ss: nc.gpsimd.tensor_tensor(out=ss[:, 2:3], in0=ss[:, 1:2], in1=cols[:, 1:2], op=ALU.pow),
                     reads=[bss, B_const], writes=[bss])
                S.op("dve", lambda ss=ss, sub=sub, x1=x1: nc.vector.scalar_tensor_tensor(
                    out=x1[:, sub, :], in0=x1[:, sub, :], scalar=ss[:, 2:3], in1=fgb[:], op0=ALU.mult, op1=ALU.mult),
                    reads=[B_x1[sub], bss, B_fg], writes=[B_x1[sub]])
                S.dma("sp", lambda sub=sub, c=c, x1=x1: nc.sync.dma_start(out=y_d[c * 128:(c + 1) * 128, :], in_=x1[:, sub, :]),
                      reads=[B_x1[sub]], stream=f"yo{sub}")
        S.barrier()

    if stop is not None:
        S.ops = S.ops[:S.marks[stop]]
    S.emit()
    es.close()
    return nc, S


def _consts():
    i = np.arange(128)
    ident = np.eye(128, dtype=np.float32)
    Uf = (i[:, None] <= i[None, :]).astype(np.float32)
    Ub = (i[:, None] >= i[None, :]).astype(np.float32)
    MNf = np.where(i[:, None] >= i[None, :], 0.0, NEG).astype(np.float32)
    MNb = np.where(i[:, None] <= i[None, :], 0.0, NEG).astype(np.float32)
    offd = (1.0 - ident).astype(np.float32)
    ones = np.ones((128, 128), np.float32)
    blk = lambda sz: (i[:, None] // sz == i[None, :] // sz)
    masks = [blk(16) & (i[:, None] != i[None, :])] + [(blk(2 * sz) & ~blk(sz)) for sz in (16, 32, 64)]
    consts = np.concatenate([ident, Uf, Ub, MNf, MNb, offd, ones] + [m.astype(np.float32) for m in masks], axis=1)
    sel = np.zeros((2, 2, 128), np.float32)
    sel[0, 0, :] = 1.0
    sel[1, 1, :] = 1.0
    return np.ascontiguousarray(consts), sel


_CACHE = {}


def make_in_maps(x_prompt, x_sample, state_fwd, state_bwd, c, c_ctx, w_ada, b_ada, norm1_g, norm2_g, w_in, conv_w,
                 a_log, dt_bias, dn_norm_g, sgu_ln_g, sgu_ln_b, sgu_w, sgu_b, w_out, w_gate, w_up, w_down, final_g):
    f = lambda a: np.ascontiguousarray(np.asarray(a, dtype=np.float32))
    consts, sel = _consts()
    shared = {
        "w_ada": f(w_ada[0]),
        "b_ada2": f(np.stack([b_ada[0], b_ada[0]], 0)),
        "w_in": f(w_in[0]),
        "w_out": f(w_out[0]),
        "wg_t": f(np.asarray(w_gate[0]).reshape(8, 128, NF, 128).transpose(2, 1, 0, 3).reshape(NF * 128, D)),
        "wu_t": f(np.asarray(w_up[0]).reshape(8, 128, NF, 128).transpose(2, 1, 0, 3).reshape(NF * 128, D)),
        "w_down": f(w_down[0]),
        "conv_wT": f(np.asarray(conv_w[0]).reshape(5, 12, 128).transpose(2, 1, 0)),
        "n1g": f(np.asarray(norm1_g[0]).reshape(8, 128).T),
        "n2g": f(np.asarray(norm2_g[0]).reshape(8, 128).T),
        "adt": f(np.concatenate([np.asarray(a_log[0]).reshape(-1), np.asarray(dt_bias[0]).reshape(-1)])[None, :]),
        "dn_g": f(np.asarray(dn_norm_g[0])[None, :]),
        "lngT": f(np.asarray(sgu_ln_g[0]).reshape(4, 128).T),
        "ln_b": f(np.asarray(sgu_ln_b[0])[None, :]),
        "sgu_wT": f(np.asarray(sgu_w[0]).transpose(2, 0, 1)),
        "sgu_b": f(np.asarray(sgu_b[0]).reshape(1, 512)),
        "final_g": f(np.asarray(final_g)[None, :]),
        "consts": consts,
        "sel": sel,
    }
    xp = np.asarray(x_prompt, np.float32)
    xs = np.asarray(x_sample, np.float32)
    maps = []
    for i in range(8):
        m = dict(shared)
        m["x"] = f(np.concatenate([xp[2 * i], xp[2 * i + 1], xs[i]], axis=0))
        cv = np.stack([np.asarray(c_ctx, np.float32), np.asarray(c[i], np.float32)], 0)
        m["cT"] = f(cv.reshape(2, 8, 128).transpose(2, 1, 0))
        s0 = np.stack([np.asarray(state_fwd[i, 0]), np.asarray(state_bwd[i, 0])], 0)
        m["s0"] = f(s0.transpose(0, 2, 1, 3).reshape(2, 128, 512))
        maps.append(m)
    return maps


def kernel(**inputs):
    if "nc" not in _CACHE:
        _CACHE["nc"] = build_nc(debug=False)[0]
    nc = _CACHE["nc"]
    maps = make_in_maps(**inputs)
    res = run_bass_kernel_spmd(nc, maps, core_ids=list(range(8)))
    y_prompt = np.zeros((16, 256, D), np.float32)
    y_sample = np.zeros((8, 2048, D), np.float32)
    nsf = np.zeros((16, 1, 4, 128, 128), np.float32)
    nsb = np.zeros((16, 1, 4, 128, 128), np.float32)
    for i in range(8):
        r = res.results[i]
        y = np.asarray(r["y"])
        y_prompt[2 * i] = y[0:256]
        y_prompt[2 * i + 1] = y[256:512]
        y_sample[i] = y[512:]
        sf = np.asarray(r["sf"])
        sbb = np.asarray(r["sb"])
        nsf[2 * i, 0] = sf[0]
        nsf[2 * i + 1, 0] = sf[1]
        nsb[2 * i, 0] = sbb[0]
        nsb[2 * i + 1, 0] = sbb[1]
    return (y_prompt, y_sample, nsf, nsb)
```

```python
import contextlib
import numpy as np
import concourse.bass as bass
import concourse.mybir as mybir
from concourse.bass_utils import run_bass_kernel_spmd

F32 = mybir.dt.float32
BF16 = mybir.dt.bfloat16
AF = mybir.ActivationFunctionType
ALU = mybir.AluOpType
AX = mybir.AxisListType

NT = 2560
NCH = 20
D = 1024
DFF = 2816
NF = 22
EPS = 1e-6
NEG = -30000.0
SEQS = [(0, 2, 0), (2, 2, 0), (4, 16, 1)]
CT = [(0, 256, 2), (256, 256, 260)] + [(512 + 512 * k, 512, 518 + 512 * k) for k in range(4)]
PCW = 2568

DEBUG = {}


class Buf:
    __slots__ = ("name", "w", "rs")

    def __init__(self, name):
        self.name = name
        self.w = None
        self.rs = []


class Sched:
    EP = 16000

    def __init__(self, nc, es):
        self.nc = nc
        self.es = es
        self.ops = []
        self.eng = {"pe": nc.tensor, "act": nc.scalar, "dve": nc.vector, "pool": nc.gpsimd, "sp": nc.sync}
        self.nb = 0

    def buf(self, name=None):
        self.nb += 1
        return Buf(name or f"b{self.nb}")

    def op(self, eng, fn, reads=(), writes=()):
        self.ops.append(("c", eng, fn, tuple(reads), tuple(writes), None))

    def dma(self, q, fn, reads=(), writes=(), stream="d"):
        self.ops.append(("d", q, fn, tuple(reads), tuple(writes), stream))

    def barrier(self):
        self.ops.append(("b", None, None, (), (), None))

    def mark(self, name):
        self.marks = getattr(self, 'marks', {})
        self.marks[name] = len(self.ops)

    def emit(self):
        ops = self.ops
        n = len(ops)
        deps = [None] * n
        need_inc = [False] * n
        last_eng = {}
        last_stream = {}
        pending_barrier = {}
        dma_since = []
        for i, (kind, eng, fn, reads, writes, stream) in enumerate(ops):
            if kind == "b":
                snap = set(last_eng.values()) | set(dma_since)
                dma_since = []
                for e in self.eng:
                    pending_barrier[e] = pending_barrier.get(e, set()) | snap
                continue
            ds = {}
            for b in reads:
                if b.w is not None:
                    ds[b.w] = "raw"
            for b in writes:
                if b.w is not None and b.w not in ds:
                    ds[b.w] = "waw"
                lastr = {}
                for r in b.rs:
                    key = (ops[r][0], ops[r][1], ops[r][5])
                    if ops[r][0] == "c":
                        lastr[key] = max(lastr.get(key, -1), r)
                    else:
                        lastr[(key, r)] = r
                for r in lastr.values():
                    if r not in ds:
                        ds[r] = "war"
            if eng in pending_barrier and pending_barrier[eng]:
                for j in pending_barrier[eng]:
                    ds[j] = "raw"
                pending_barrier[eng] = set()
            for b in reads:
                b.rs.append(i)
            for b in writes:
                b.w = i
                b.rs = []
            keep = []
            for j, typ in ds.items():
                if j == i:
                    continue
                pk, pe = ops[j][0], ops[j][1]
                if pk == "c" and kind == "c" and pe == eng and typ != "raw":
                    continue
                if pk == "c" and kind == "c" and pe == eng and eng == "pe":
                    continue
                keep.append(j)
                if pk == "c":
                    need_inc[j] = True
            deps[i] = keep
            if kind == "c":
                last_eng[eng] = i
            else:
                last_stream[stream] = i
                if stream not in ("wc", "dbg"):
                    dma_since.append(i)
        sems = {}
        ord_of = {}
        cnt = {e: 0 for e in self.eng}
        known = {e: {} for e in self.eng}
        POOLN = {"sp": 10, "pool": 4, "act": 4}
        pool_next = {q: 0 for q in POOLN}
        pool_cnt = {}
        dma_ev = {}

        def sem(key):
            if key not in sems:
                sems[key] = self.es.enter_context(self.nc.semaphore("s_" + "_".join(str(k) for k in key)))
            return sems[key]

        nwaits = 0
        for i, (kind, eng, fn, reads, writes, stream) in enumerate(ops):
            if kind == "b":
                continue
            h = self.eng[eng]
            want = {}
            for j in deps[i]:
                pk, pe = ops[j][0], ops[j][1]
                if pk == "c":
                    key = ("e", pe)
                    want[key] = max(want.get(key, 0), ord_of[j])
                else:
                    key, v = dma_ev[j]
                    want[key] = max(want.get(key, 0), v)
            if kind == "d":
                slot = pool_next[eng] % POOLN[eng]
                pool_next[eng] += 1
                key = ("d", eng, slot)
                prev = pool_cnt.get(key, 0)
                if prev:
                    want[key] = max(want.get(key, 0), prev)
                pool_cnt[key] = prev + 16
                assert pool_cnt[key] < 60000
                dma_ev[i] = (key, prev + 16)
            for key, o in want.items():
                if known[eng].get(key, 0) >= o:
                    continue
                known[eng][key] = o
                if key[0] == "e":
                    ep, v = (o - 1) // self.EP, (o - 1) % self.EP + 1
                    h.wait_ge(sem((key[1], ep)), v)
                else:
                    h.wait_ge(sem(key), o)
                nwaits += 1
            inst = fn()
            if kind == "c":
                if need_inc[i]:
                    cnt[eng] += 1
                    ord_of[i] = cnt[eng]
                    ep = (cnt[eng] - 1) // self.EP
                    inst.then_inc(sem((eng, ep)), 1)
            else:
                inst.then_inc(sem(dma_ev[i][0]), 16)
        for key, c in pool_cnt.items():
            self.nc.sync.wait_ge(sem(key), c)
        self.stats = dict(nops=n, nwaits=nwaits, nsems=len(sems), cnt=cnt)


class Ring:
    def __init__(self, S, aps, name):
        self.items = [(ap, S.buf(f"{name}{i}")) for i, ap in enumerate(aps)]
        self.i = 0

    def next(self):
        it = self.items[self.i % len(self.items)]
        self.i += 1
        return it


def build_nc(debug=False, stop=None):
    nc = bass.Bass("TRN2", target_bir_lowering=False)
    es = contextlib.ExitStack()
    S = Sched(nc, es)

    def din(name, shape, dt=F32):
        return nc.dram_tensor(name, list(shape), dt, kind="ExternalInput").ap()

    def dout(name, shape, dt=F32):
        return nc.dram_tensor(name, list(shape), dt, kind="ExternalOutput").ap()

    def dint(name, shape, dt=F32):
        return nc.dram_tensor(name, list(shape), dt, kind="Internal").ap()

    x_d = din("x", [NT, D])
    cT_d = din("cT", [128, 8, 2])
    s0_d = din("s0", [2, 128, 512])
    w_ada_d = din("w_ada", [D, 6 * D])
    b_ada_d = din("b_ada2", [2, 6 * D])
    w_in_d = din("w_in", [D, 3088])
    w_out_d = din("w_out", [D, D])
    wg_d = din("wg_t", [NF * 128, D])
    wu_d = din("wu_t", [NF * 128, D])
    wd_d = din("w_down", [DFF, D])
    convw_d = din("conv_wT", [128, 12, 5])
    n1g_d = din("n1g", [128, 8])
    n2g_d = din("n2g", [128, 8])
    adt_d = din("adt", [1, 16])
    dng_d = din("dn_g", [1, 128])
    lngT_d = din("lngT", [128, 4])
    lnb_d = din("ln_b", [1, 512])
    sguw_d = din("sgu_wT", [128, 4, 128])
    sgub_d = din("sgu_b", [1, 512])
    fg_d = din("final_g", [1, D])
    consts_d = din("consts", [128, 11 * 128])
    sel_d = din("sel", [2, 2, 128])

    y_d = dout("y", [NT, D])
    sf_d = dout("sf", [2, 4, 128, 128])
    sb_d = dout("sb", [2, 4, 128, 128])
    dbg_d = {}

    mod_d = dint("mod_scr", [2, 6 * D])
    wo_s = dint("wo_scr", [D, D], BF16)
    wd_s = dint("wd_scr", [DFF, D], BF16)
    wg_s = dint("wg_scr", [NF * 128, D], BF16)
    wu_s = dint("wu_scr", [NF * 128, D], BF16)

    def sb(name, shape, dt=F32, stack=None):
        return (stack or es).enter_context(nc.sbuf_tensor("sb_" + name, list(shape), dt))

    banks = [es.enter_context(nc.psum_tensor(f"bank{i}", [128, 512], F32)) for i in range(8)]
    PS = Ring(S, [b[:] for b in banks], "ps")

    def psum():
        return PS.next()

    consts = sb("consts", [128, 11 * 128])
    ident_f = consts[:, 0:128]
    U = [consts[:, 128:256], consts[:, 256:384]]
    MN = [consts[:, 384:512], consts[:, 512:640]]
    offd_f = consts[:, 640:768]
    ones_f = consts[:, 768:896]
    cbf = sb("cbf", [128, 7 * 128], BF16)
    ident_b = cbf[:, 0:128]
    ones_b = cbf[:, 128:256]
    offd_b = cbf[:, 256:384]
    MASKS = [cbf[:, 384 + 128 * m:512 + 128 * m] for m in range(4)]
    sel = sb("sel", [2, 2, 128])
    B_const = S.buf("consts")
    cols = sb("cols", [128, 8])
    mh4 = sb("mh4", [128, 4])
    sc_sh = sb("sc_sh", [128, 4, 8, 2])
    B_mod1 = S.buf("mod1")
    B_mod2 = S.buf("mod2")
    B_mblk = [S.buf(f"mblk{i}") for i in range(24)]
    n1g = sb("n1g", [128, 8])
    n2g = sb("n2g", [128, 8])
    convw = sb("convw", [128, 12, 5])
    adt_b = sb("adt_b", [128, 16])
    nea_b = sb("nea_b", [128, 8])
    lngT = sb("lngT", [128, 4])
    BB = sb("BB", [128, 512])
    sguw = sb("sguw", [128, 4, 128], BF16)
    B_par = S.buf("params")
    qT = sb("qT", [128, 4, NT], BF16)
    ybT = sb("ybT", [128, 4, NT], BF16)
    B_q = [[S.buf(f"q{h}_{c}") for c in range(NCH)] for h in range(4)]
    B_yb = [S.buf(f"yb{c}") for c in range(NCH)]

    eng = S.eng

    def dbg(name, ap, shape, dt, reads):
        if not debug:
            return
        dd = dout("dbg_" + name, shape, dt)
        S.dma("sp", lambda: nc.sync.dma_start(out=dd, in_=ap), reads=reads, stream="dbg")

    S.dma("sp", lambda: nc.sync.dma_start(out=consts[:], in_=consts_d), writes=[B_const], stream="c0")
    S.dma("sp", lambda: nc.sync.dma_start(out=sel[:], in_=sel_d), writes=[B_const], stream="c0")
    for (t, d_) in [(n1g, n1g_d), (n2g, n2g_d), (convw, convw_d), (lngT, lngT_d)]:
        S.dma("sp", lambda t=t, d_=d_: nc.sync.dma_start(out=t[:], in_=d_), writes=[B_par], stream="c0")
    for (t, d_, w) in [(adt_b, adt_d, 16)]:
        S.dma("sp", lambda t=t, d_=d_: nc.sync.dma_start(out=t[:], in_=d_.partition_broadcast(128)[:, 0, :]),
              writes=[B_par], stream="c0")
    S.op("dve", lambda: nc.vector.tensor_copy(out=cbf[:, 0:128], in_=ident_f), reads=[B_const], writes=[B_const])
    S.op("dve", lambda: nc.vector.tensor_copy(out=cbf[:, 128:256], in_=ones_f), reads=[B_const], writes=[B_const])
    S.op("dve", lambda: nc.vector.tensor_copy(out=cbf[:, 256:384], in_=offd_f), reads=[B_const], writes=[B_const])
    S.op("dve", lambda: nc.vector.tensor_copy(out=cbf[:, 384:896], in_=consts[:, 896:1408]), reads=[B_const], writes=[B_const])
    for k, v in enumerate([EPS, -0.5, -0.5 * float(np.log(128.0)), 1.0, 4 * EPS, 0.0]):
        S.op("pool", lambda k=k, v=v: nc.gpsimd.memset(cols[:, k:k + 1], v), writes=[B_const])
    S.op("pool", lambda: nc.gpsimd.memset(mh4[:], -0.5), writes=[B_const])
    S.op("act", lambda: nc.scalar.activation(out=nea_b[:], in_=adt_b[:, 0:8], func=AF.Exp), reads=[B_par], writes=[B_par])
    S.op("dve", lambda: nc.vector.tensor_scalar(out=nea_b[:], in0=nea_b[:], scalar1=-1.0, scalar2=None, op0=ALU.mult),
         reads=[B_par], writes=[B_par])
    S.mark('p0b')
    B_wscr = {"wo": [], "wd": [], "wg": [], "wu": []}

    with contextlib.ExitStack() as st0:
        lnb_b = sb("lnb_b", [128, 512], stack=st0)
        sgub_b = sb("sgub_b", [128, 512], stack=st0)
        sguw_f = sb("sguw_f", [128, 4, 128], stack=st0)
        S.dma("sp", lambda: nc.sync.dma_start(out=sguw_f[:], in_=sguw_d), writes=[B_par], stream="c0")
        for (t, d_) in [(lnb_b, lnb_d), (sgub_b, sgub_d)]:
            S.dma("sp", lambda t=t, d_=d_: nc.sync.dma_start(out=t[:], in_=d_.partition_broadcast(128)[:, 0, :]),
                  writes=[B_par], stream="c0")
        S.op("dve", lambda: nc.vector.tensor_copy(out=sguw[:], in_=sguw_f[:]), reads=[B_par], writes=[B_par])
        S.mark('p0a')
        pB, bB = psum()
        for g in range(4):
            S.op("pe", lambda g=g: nc.tensor.matmul(pB[:, g * 128:(g + 1) * 128], lhsT=lnb_b[:, g * 128:(g + 1) * 128],
                                                      rhs=sguw_f[:, g, :], start=True, stop=True),
                 reads=[B_par], writes=[bB])
        S.op("dve", lambda: nc.vector.tensor_tensor(out=BB[:], in0=pB, in1=sgub_b[:], op=ALU.add),
             reads=[bB, B_par], writes=[B_par])

        S.barrier()
    S.barrier()

    def norm_stage_a(xt, bx, ring_sq, ring_xh, tmp_small):
        sq, bsq = ring_sq.next()
        ss, bss = tmp_small.next()
        S.op("act", lambda: nc.scalar.activation(out=sq, in_=xt, func=AF.Square, accum_out=ss[:, 0:1]),
             reads=[bx], writes=[bsq, bss])
        S.op("pool", lambda: nc.gpsimd.tensor_scalar(out=ss[:, 1:2], in0=ss[:, 0:1], scalar1=1.0 / D, scalar2=EPS,
                                                     op0=ALU.mult, op1=ALU.add), reads=[bss], writes=[bss])
        S.op("pool", lambda: nc.gpsimd.tensor_tensor(out=ss[:, 2:3], in0=ss[:, 1:2], in1=cols[:, 1:2], op=ALU.pow),
             reads=[bss, B_const], writes=[bss])
        xh, bxh = ring_xh.next()
        S.op("dve", lambda: nc.vector.tensor_scalar(out=xh, in0=xt, scalar1=ss[:, 2:3], scalar2=None, op0=ALU.mult),
             reads=[bx, bss], writes=[bxh])
        return xh, bxh

    def norm_stage_b(xh, bxh, which_scale, which_shift, r, dstT, tok0, bdst, bmod):
        pt, bpt = psum()
        ptb = pt.bitcast(BF16)
        for kc in range(8):
            S.op("pe", lambda kc=kc: nc.tensor.transpose(ptb[:, kc * 128:(kc + 1) * 128], xh[:, kc * 128:(kc + 1) * 128], ident_b),
                 reads=[bxh, B_const], writes=[bpt])
        for kc in range(8):
            S.op("act", lambda kc=kc: nc.scalar.activation(out=dstT[:, kc, tok0:tok0 + 128], in_=ptb[:, kc * 128:(kc + 1) * 128],
                                                           func=AF.Identity, scale=sc_sh[:, which_scale, kc, r:r + 1],
                                                           bias=sc_sh[:, which_shift, kc, r:r + 1]),
                 reads=[bpt, bmod], writes=[bdst])

    def norm_to_featmajor(xt, bx, which_scale, which_shift, r, dstT, tok0, bdst, ring_sq, ring_xh, tmp_small, bmod):
        xh, bxh = norm_stage_a(xt, bx, ring_sq, ring_xh, tmp_small)
        norm_stage_b(xh, bxh, which_scale, which_shift, r, dstT, tok0, bdst, bmod)

    with contextlib.ExitStack() as stAB:
        hT = sb("hT", [128, 8, NT], BF16, stack=stAB)
        kT = sb("kT", [128, 4, NT], BF16, stack=stAB)
        vT = sb("vT", [128, 4, NT], BF16, stack=stAB)
        zs = sb("zs", [128, NCH, 512], BF16, stack=stAB)
        lg = sb("lg", [128, NCH, 16], stack=stAB)
        beta = sb("beta", [128, NCH, 8], stack=stAB)
        negb = sb("negb", [128, NCH, 8], stack=stAB)
        gg = sb("gg", [128, NCH, 8], stack=stAB)
        negg = sb("negg", [128, NCH, 8], stack=stAB)
        gcl = sb("gcl", [128, NCH, 16], stack=stAB)
        egc = sb("egc", [128, NCH, 8], stack=stAB)
        egl = sb("egl", [128, NCH, 8], stack=stAB)
        ekd = sb("ekd", [128, NCH, 8], stack=stAB)
        bege = sb("bege", [128, NCH, 8], stack=stAB)
        B_h = [S.buf(f"h{c}") for c in range(NCH)]
        B_k = [[S.buf(f"k{h}_{c}") for c in range(NCH)] for h in range(4)]
        B_v = [[S.buf(f"v{h}_{c}") for c in range(NCH)] for h in range(4)]
        B_z = [S.buf(f"z{c}") for c in range(NCH)]
        B_lg = S.buf("lg")
        B_coef = S.buf("coef")

        def chunks_of(t0, n):
            return range(t0 // 128, (t0 + n + 127) // 128)

        with contextlib.ExitStack() as stA:
            xr = [sb(f"xr{i}", [128, D], stack=stA) for i in range(3)]
            xR = Ring(S, [t[:] for t in xr], "xr")
            sqj = sb("sqj", [128, D], BF16, stack=stA)
            sqR = Ring(S, [sqj[:]], "sqj")
            xhr = [sb(f"xh{i}", [128, D], BF16, stack=stA) for i in range(2)]
            xhR = Ring(S, [t[:] for t in xhr], "xh")
            ssr = [sb(f"ss{i}", [128, 4], stack=stA) for i in range(4)]
            ssR = Ring(S, [t[:] for t in ssr], "ss")

            cT = sb("cT", [128, 8, 2], stack=stA)
            cs = sb("cs", [128, 8, 2], stack=stA)
            B_c = S.buf("c")
            waR = Ring(S, [sb(f"wa{i}", [128, 4, 512], stack=stA)[:] for i in range(2)], "wa")
            bbR = Ring(S, [sb(f"bbt{i}", [2, 512], stack=stA)[:] for i in range(2)], "bbt")
            mrR = Ring(S, [sb(f"mrw{i}", [2, 512], stack=stA)[:] for i in range(2)], "mrw")
            S.dma("sp", lambda: nc.sync.dma_start(out=cT[:], in_=cT_d), writes=[B_c], stream="c0")
            S.op("act", lambda: nc.scalar.activation(out=cs[:], in_=cT[:], func=AF.Silu), reads=[B_c], writes=[B_c])
            wav = w_ada_d.rearrange("(kc p) n -> p kc n", p=128)
            mod_state = {}

            def mod_block(hb):
                cb_, kh = hb // 2, hb % 2
                cs0 = cb_ * 512
                wt, bw = waR.next()
                S.dma("sp", lambda: nc.sync.dma_start(out=wt, in_=wav[:, kh * 4:(kh + 1) * 4, cs0:cs0 + 512]), writes=[bw], stream="wa")
                if kh == 0:
                    bt, bb_ = bbR.next()
                    S.dma("sp", lambda: nc.sync.dma_start(out=bt, in_=b_ada_d[:, cs0:cs0 + 512]), writes=[bb_], stream="wa")
                    pm, bm = psum()
                    mod_state[cb_] = (bt, bb_, pm, bm)
                bt, bb_, pm, bm = mod_state[cb_]
                for k in range(4):
                    S.op("pe", lambda k=k: nc.tensor.matmul(pm[0:2, :], lhsT=cs[:, kh * 4 + k, :], rhs=wt[:, k, :],
                                                            start=(kh == 0 and k == 0), stop=(kh == 1 and k == 3)),
                         reads=[B_c, bw], writes=[bm])
                if kh == 1:
                    mr, bmr = mrR.next()
                    S.op("dve", lambda: nc.vector.tensor_tensor(out=mr, in0=pm[0:2, :], in1=bt, op=ALU.add), reads=[bm, bb_], writes=[bmr])
                    S.dma("pool", lambda: nc.gpsimd.dma_start(out=mod_d[:, cs0:cs0 + 512], in_=mr), reads=[bmr], writes=[B_mblk[cb_]], stream="md")

            def load_scsh(wis, bmodx):
                for wi in wis:
                    seg = [0, 1, 3, 4][wi]
                    for r in range(2):
                        S.dma("sp", lambda wi=wi, seg=seg, r=r: nc.sync.dma_start(
                            out=sc_sh[:, wi, :, r], in_=mod_d[r, seg * D:(seg + 1) * D].rearrange("(kc p) -> p kc", p=128),
                            allow_slow_non_contiguous=True), reads=B_mblk[seg * 2:seg * 2 + 2], writes=[bmodx], stream="md")
                wi, gt = (1, n1g) if 1 in wis else (3, n2g)
                for r in range(2):
                    S.op("dve", lambda wi=wi, gt=gt, r=r: nc.vector.scalar_tensor_tensor(
                        out=sc_sh[:, wi, :, r], in0=sc_sh[:, wi, :, r], scalar=1.0, in1=gt[:], op0=ALU.add, op1=ALU.mult),
                        reads=[bmodx, B_par], writes=[bmodx])

            for nb in range(8):
                mod_block(nb)
            load_scsh([0, 1], B_mod1)
            S.mark('p0')

            def a1_stage_a(c):
                xt, bx = xR.next()
                S.dma("sp", lambda xt=xt, c=c: nc.sync.dma_start(out=xt, in_=x_d[c * 128:(c + 1) * 128, :]),
                      writes=[bx], stream=f"x{c % 3}")
                return norm_stage_a(xt, bx, sqR, xhR, ssR)

            cur = a1_stage_a(0)
            for c in range(NCH):
                if c < 16:
                    mod_block(8 + c)
                nxt = a1_stage_a(c + 1) if c + 1 < NCH else None
                r = 0 if c < 4 else 1
                norm_stage_b(cur[0], cur[1], 1, 0, r, hT, c * 128, B_h[c], B_mod1)
                cur = nxt
            load_scsh([2, 3], B_mod2)

            S.barrier()
        S.barrier()
        with contextlib.ExitStack() as stA:
            wblk = [sb(f"wblk{i}", [128, 8, 512], BF16, stack=stA) for i in range(2)]
            wR = Ring(S, [t[:] for t in wblk], "wblk")
            pre = [sb(f"pre{i}", [128, PCW], BF16, stack=stA) for i in range(2)]
            preR = Ring(S, [t[:] for t in pre], "pre")
            dg = [sb(f"dg{i}", [128, 5, 128], BF16, stack=stA) for i in range(2)]
            dgR = Ring(S, [t[:] for t in dg], "dg")
            tmpA = [sb(f"tmpA{i}", [128, 512], stack=stA) for i in range(3)]
            tAR = Ring(S, [t[:] for t in tmpA], "tmpA")
            tmpB = [sb(f"tmpB{i}", [128, 512], BF16, stack=stA) for i in range(4)]
            tBR = Ring(S, [t[:] for t in tmpB], "tmpB")
            us = sb("us", [128, 4, 512], BF16, stack=stA)
            B_us = S.buf("us")
            stt = [sb(f"stt{i}", [128, 8], stack=stA) for i in range(2)]
            stR = Ring(S, [t[:] for t in stt], "stt")

            S.mark('a1')
            w_in_v = w_in_d.rearrange("(kc p) n -> p kc n", p=128)

            def load_wblk(col0, ncol):
                wt, bw = wR.next()
                S.dma("pool", lambda: nc.gpsimd.dma_start(out=wt[:, :, 0:ncol], in_=w_in_v[:, :, col0:col0 + ncol]),
                      writes=[bw], stream=f"wi{(wR.i) % 2}")
                return wt, bw

            for blk in range(3):
                wt, bw = load_wblk(blk * 512, 512)
                for ch in range(4):
                    cch = blk * 4 + ch
                    pc, bpc = preR.next()
                    S.op("pool", lambda pc=pc: nc.gpsimd.memset(pc, 0.0), writes=[bpc])
                    for st5 in range(5):
                        pp, bp = psum()
                        for kc in range(8):
                            S.op("pe", lambda pp=pp, kc=kc, ch=ch, st5=st5, wt=wt: nc.tensor.matmul(
                                pp, lhsT=wt[:, kc, ch * 128:(ch + 1) * 128], rhs=hT[:, kc, st5 * 512:(st5 + 1) * 512],
                                start=(kc == 0), stop=(kc == 7)),
                                reads=[bw] + B_h[st5 * 4:st5 * 4 + 4], writes=[bp])
                        if st5 == 0:
                            S.op("act", lambda pp=pp, pc=pc: nc.scalar.copy(out=pc[:, 2:258], in_=pp[:, 0:256]),
                                 reads=[bp], writes=[bpc])
                            S.op("act", lambda pp=pp, pc=pc: nc.scalar.copy(out=pc[:, 260:516], in_=pp[:, 256:512]),
                                 reads=[bp], writes=[bpc])
                        else:
                            c0 = 518 + 512 * (st5 - 1)
                            S.op("act", lambda pp=pp, pc=pc, c0=c0: nc.scalar.copy(out=pc[:, c0:c0 + 512], in_=pp),
                                 reads=[bp], writes=[bpc])
                    dgt, bdg = dgR.next()
                    for j in range(5):
                        S.op("dve", lambda dgt=dgt, j=j, cch=cch: nc.vector.tensor_scalar(
                            out=dgt[:, j, :], in0=ident_f, scalar1=convw[:, cch, j:j + 1], scalar2=None, op0=ALU.mult),
                            reads=[B_const, B_par], writes=[bdg])
                    dst, Bd = [(qT, B_q), (kT, B_k), (vT, B_v)][blk]
                    for (t0, n_, c0) in CT:
                        pp, bp = psum()
                        for j in range(5):
                            S.op("pe", lambda pp=pp, j=j, dgt=dgt, pc=pc, c0=c0, n_=n_: nc.tensor.matmul(
                                pp[:, 0:n_], lhsT=dgt[:, j, :], rhs=pc[:, c0 + j - 2:c0 + j - 2 + n_],
                                start=(j == 0), stop=(j == 4)), reads=[bdg, bpc], writes=[bp])
                        S.op("act", lambda pp=pp, dst=dst, ch=ch, t0=t0, n_=n_: nc.scalar.activation(
                            out=dst[:, ch, t0:t0 + n_], in_=pp[:, 0:n_], func=AF.Silu),
                            reads=[bp], writes=[Bd[ch][c] for c in chunks_of(t0, n_)])

            wt, bw = load_wblk(3 * 512, 512)
            for c in range(NCH):
                pp, bp = psum()
                for kc in range(8):
                    S.op("pe", lambda pp=pp, kc=kc, c=c, wt=wt: nc.tensor.matmul(
                        pp, lhsT=hT[:, kc, c * 128:(c + 1) * 128], rhs=wt[:, kc, :], start=(kc == 0), stop=(kc == 7)),
                        reads=[bw, B_h[c]], writes=[bp])
                S.op("act", lambda pp=pp, c=c: nc.scalar.activation(out=zs[:, c, :], in_=pp, func=AF.Silu),
                     reads=[bp], writes=[B_z[c]])

            wt16, bw16 = load_wblk(3072, 16)
            pl, bpl = psum()
            for c in range(NCH):
                for kc in range(8):
                    S.op("pe", lambda kc=kc, c=c: nc.tensor.matmul(
                        pl[:, c * 16:(c + 1) * 16], lhsT=hT[:, kc, c * 128:(c + 1) * 128], rhs=wt16[:, kc, 0:16],
                        start=(kc == 0), stop=(kc == 7)), reads=[bw16, B_h[c]], writes=[bpl])
            S.op("dve", lambda: nc.vector.tensor_copy(out=lg[:].rearrange("p c k -> p (c k)"), in_=pl[:, 0:NCH * 16]),
                 reads=[bpl], writes=[B_lg])

            C1 = 0.7978845608028654
            C2 = 0.044715

            def gelu2(pp, bp, out_ap, bout, n_=512):
                S.op("act", lambda: nc.scalar.activation(out=out_ap, in_=pp[:, 0:n_], func=AF.Gelu_apprx_tanh), reads=[bp], writes=[bout])

            wtu, bwu = load_wblk(4 * 512, 512)
            wtv, bwv = load_wblk(5 * 512, 512)
            for st5 in range(5):
                for ch in range(4):
                    pp, bp = psum()
                    for kc in range(8):
                        S.op("pe", lambda pp=pp, kc=kc, ch=ch, st5=st5: nc.tensor.matmul(
                            pp, lhsT=wtu[:, kc, ch * 128:(ch + 1) * 128], rhs=hT[:, kc, st5 * 512:(st5 + 1) * 512],
                            start=(kc == 0), stop=(kc == 7)), reads=[bwu] + B_h[st5 * 4:st5 * 4 + 4], writes=[bp])
                    gelu2(pp, bp, us[:, ch, :], B_us)
                def sgu_a(c):
                    pp, bp = psum()
                    for kc in range(8):
                        S.op("pe", lambda pp=pp, kc=kc, c=c: nc.tensor.matmul(
                            pp, lhsT=hT[:, kc, c * 128:(c + 1) * 128], rhs=wtv[:, kc, :], start=(kc == 0), stop=(kc == 7)),
                            reads=[bwv, B_h[c]], writes=[bp])
                    vg, bvg = tAR.next()
                    gelu2(pp, bp, vg, bvg)
                    return vg, bvg

                cur_sgu = sgu_a(st5 * 4)
                for sub in range(4):
                    c = st5 * 4 + sub
                    nxt_sgu = sgu_a(c + 1) if sub + 1 < 4 else None
                    vg, bvg = cur_sgu
                    cur_sgu = nxt_sgu
                    stt_, bst = stR.next()
                    S.op("dve", lambda vg=vg, stt_=stt_: nc.vector.bn_stats(out=stt_[:, 0:6], in_=vg), reads=[bvg], writes=[bst])
                    S.op("dve", lambda stt_=stt_: nc.vector.bn_aggr(out=stt_[:, 6:8], in_=stt_[:, 0:6]), reads=[bst], writes=[bst])
                    S.op("pool", lambda stt_=stt_: nc.gpsimd.tensor_scalar(out=stt_[:, 0:1], in0=stt_[:, 7:8], scalar1=EPS,
                                                                         scalar2=None, op0=ALU.add), reads=[bst], writes=[bst])
                    S.op("pool", lambda stt_=stt_: nc.gpsimd.tensor_tensor(out=stt_[:, 1:2], in0=stt_[:, 0:1], in1=cols[:, 1:2],
                                                                         op=ALU.pow), reads=[bst, B_const], writes=[bst])
                    vn, bvn = tBR.next()
                    S.op("dve", lambda vg=vg, vn=vn, stt_=stt_: nc.vector.tensor_scalar(
                        out=vn, in0=vg, scalar1=stt_[:, 6:7], scalar2=stt_[:, 1:2], op0=ALU.subtract, op1=ALU.mult),
                        reads=[bvg, bst], writes=[bvn])
                    pm, bm = psum()
                    for g in range(4):
                        S.op("pe", lambda pm=pm, g=g, vn=vn: nc.tensor.matmul(
                            pm[:, g * 128:(g + 1) * 128], lhsT=vn[:, g * 128:(g + 1) * 128], rhs=sguw[:, g, :],
                            start=True, stop=True), reads=[bvn, B_par], writes=[bm])
                    mx, bmx = tAR.next()
                    for g in range(4):
                        S.op("dve", lambda pm=pm, g=g, mx=mx: nc.vector.scalar_tensor_tensor(
                            out=mx[:, g * 128:(g + 1) * 128], in0=pm[:, g * 128:(g + 1) * 128], scalar=lngT[:, g:g + 1],
                            in1=BB[:, g * 128:(g + 1) * 128], op0=ALU.mult, op1=ALU.add),
                            reads=[bm, B_par], writes=[bmx])
                    S.op("dve", lambda mx=mx, c=c, sub=sub: nc.vector.scalar_tensor_tensor(
                        out=ybT[:, :, c * 128:(c + 1) * 128], in0=mx.rearrange("p (g t) -> p g t", g=4), scalar=1.0,
                        in1=us[:, :, sub * 128:(sub + 1) * 128], op0=ALU.mult, op1=ALU.mult),
                        reads=[bmx, B_us], writes=[B_yb[c]])

            S.op("act", lambda: nc.scalar.activation(out=beta[:], in_=lg[:, :, 0:8], func=AF.Tanh, scale=0.5),
                 reads=[B_lg], writes=[B_coef])
            S.op("dve", lambda: nc.vector.tensor_scalar(out=beta[:], in0=beta[:], scalar1=0.5, scalar2=0.5, op0=ALU.mult,
                                                        op1=ALU.add), reads=[B_coef], writes=[B_coef])
            S.op("dve", lambda: nc.vector.tensor_scalar(out=negb[:], in0=beta[:], scalar1=-1.0, scalar2=None, op0=ALU.mult),
                 reads=[B_coef], writes=[B_coef])
            S.op("dve", lambda: nc.vector.tensor_tensor(out=gg[:], in0=lg[:, :, 8:16],
                                                        in1=adt_b[:, 8:16].unsqueeze(1).to_broadcast([128, NCH, 8]), op=ALU.add),
                 reads=[B_lg, B_par], writes=[B_coef])
            S.op("dve", lambda: nc.vector.tensor_scalar(out=negg[:], in0=gg[:], scalar1=-1.0, scalar2=None, op0=ALU.mult),
                 reads=[B_coef], writes=[B_coef])
            S.op("dve", lambda: nc.vector.tensor_tensor(out=negg[:], in0=negg[:], in1=gg[:], op=ALU.max),
                 reads=[B_coef], writes=[B_coef])
            S.op("act", lambda: nc.scalar.activation(out=negg[:], in_=negg[:], func=AF.Exp, scale=-1.0),
                 reads=[B_coef], writes=[B_coef])
            S.op("act", lambda: nc.scalar.activation(out=negg[:], in_=negg[:], func=AF.Ln, bias=cols[:, 3:4], scale=1.0),
                 reads=[B_coef, B_const], writes=[B_coef])
            S.op("dve", lambda: nc.vector.tensor_scalar(out=gg[:], in0=gg[:], scalar1=0.0, scalar2=None, op0=ALU.max),
                 reads=[B_coef], writes=[B_coef])
            S.op("dve", lambda: nc.vector.tensor_tensor(out=gg[:], in0=gg[:], in1=negg[:], op=ALU.add),
                 reads=[B_coef], writes=[B_coef])
            S.op("dve", lambda: nc.vector.tensor_tensor(out=gg[:], in0=gg[:],
                                                        in1=nea_b[:].unsqueeze(1).to_broadcast([128, NCH, 8]), op=ALU.mult),
                 reads=[B_coef, B_par], writes=[B_coef])
            S.op("dve", lambda: nc.vector.tensor_scalar(out=negg[:], in0=gg[:], scalar1=-1.0, scalar2=None, op0=ALU.mult),
                 reads=[B_coef], writes=[B_coef])

            for (dst, Bd, bias_col) in [(qT, B_q, 2), (kT, B_k, 5)]:
                for h in range(4):
                    for st5 in range(5):
                        bsl = Bd[h][st5 * 4:st5 * 4 + 4]
                        sl = dst[:, h, st5 * 512:(st5 + 1) * 512]
                        sq_, bsq = tBR.next()
                        S.op("pool", lambda sq_=sq_, sl=sl: nc.gpsimd.tensor_tensor(out=sq_, in0=sl, in1=sl, op=ALU.mult),
                             reads=bsl, writes=[bsq])
                        pp, bp = psum()
                        S.op("pe", lambda pp=pp, sq_=sq_: nc.tensor.matmul(pp, lhsT=ones_b, rhs=sq_, start=True, stop=True),
                             reads=[bsq, B_const], writes=[bp])
                        rs_, brs = tAR.next()
                        S.op("act", lambda pp=pp, rs_=rs_: nc.scalar.activation(out=rs_, in_=pp, func=AF.Ln, bias=cols[:, 0:1],
                                                                                scale=1.0), reads=[bp, B_const], writes=[brs])
                        S.op("act", lambda rs_=rs_, bias_col=bias_col: nc.scalar.activation(
                            out=rs_, in_=rs_, func=AF.Exp, bias=cols[:, bias_col:bias_col + 1], scale=-0.5),
                            reads=[brs, B_const], writes=[brs])
                        S.op("dve", lambda sl=sl, rs_=rs_: nc.vector.tensor_tensor(out=sl, in0=sl, in1=rs_, op=ALU.mult),
                             reads=[brs] + bsl, writes=bsl)
            S.barrier()
        S.barrier()

        S.mark('a2')
        if debug:
            for nm, t_, shp in [("qT", qT, [128, 4, NT]), ("kT", kT, [128, 4, NT]), ("vT", vT, [128, 4, NT]),
                                ("ybT", ybT, [128, 4, NT]), ("zs", zs, [128, NCH, 512]), ("hT", hT, [128, 8, NT])]:
                dd = dout("dbg_" + nm, shp, BF16)
                S.dma("sp", lambda dd=dd, t_=t_: nc.sync.dma_start(out=dd, in_=t_[:]), stream="dbg")
            for nm, t_, shp in [("beta", beta, [128, NCH, 8]), ("gg", gg, [128, NCH, 8])]:
                dd = dout("dbg_" + nm, shp, F32)
                S.dma("sp", lambda dd=dd, t_=t_: nc.sync.dma_start(out=dd, in_=t_[:]), stream="dbg")
            S.barrier()

        wc_jobs = []
        for (nm, src, dst, rows) in [("wo", w_out_d, wo_s, D), ("wd", wd_d, wd_s, DFF), ("wg", wg_d, wg_s, NF * 128), ("wu", wu_d, wu_s, NF * 128)]:
            r0 = 0
            while r0 < rows:
                r1 = min(rows, r0 + 512)
                bq = S.buf(f"wscr_{nm}_{r0}")
                B_wscr[nm].append(bq)
                wc_jobs.append((src, dst, r0, r1, bq))
                r0 = r1

        def issue_wc(k):
            for _ in range(k):
                if wc_jobs:
                    src, dst, r0, r1, bq = wc_jobs.pop(0)
                    S.dma("pool", lambda src=src, dst=dst, r0=r0, r1=r1: nc.gpsimd.dma_start(out=dst[r0:r1, :], in_=src[r0:r1, :]),
                          writes=[bq], stream="wc")

        with contextlib.ExitStack() as stB:
            gn_b = sb("gn_b", [128, 512], stack=stB)
            B_gn = S.buf("gn")
            for h in range(4):
                S.dma("sp", lambda h=h: nc.sync.dma_start(out=gn_b[:, h * 128:(h + 1) * 128],
                                                          in_=dng_d.partition_broadcast(128)[:, 0, :]),
                      writes=[B_gn], stream="c0")
            hflat = hT[:].rearrange("p a b -> p (a b)")
            O = hflat[:, 0:NCH * 512].rearrange("p (c f) -> p c f", c=NCH)
            B_O = [S.buf(f"O{c}") for c in range(NCH)]
            spare = [NCH * 512]

            def carve(name, n):
                aps = []
                for i in range(n):
                    aps.append(hflat[:, spare[0]:spare[0] + 512])
                    spare[0] += 512
                assert spare[0] <= 8 * NT
                return Ring(S, aps, name)

            def t512(name, dt, n):
                ts = [sb(f"{name}{i}", [128, 512], dt, stack=stB) for i in range(n)]
                return Ring(S, [t[:] for t in ts], name)

            _car = carve("chainc", 10)
            _all = t512("chaina", BF16, 11)
            chR = Ring(S, [it[0] for it in _car.items] + [it[0] for it in _all.items], "chain")
            M0R = t512("M0", BF16, 2)
            MT0R = t512("MT0", BF16, 2)
            vbR = carve("vbeta", 4)
            def mix(name, ncar, nall):
                a = carve(name + "c", ncar)
                b = t512(name + "a", BF16, nall)
                return Ring(S, [it[0] for it in a.items] + [it[0] for it in b.items], name)
            wTR = mix("wT", 2, 2)
            qkTR = mix("qkT", 2, 2)
            kdR = mix("kdec", 2, 2)
            f32R = t512("f32t", F32, 2)
            ER = t512("E", BF16, 2)
            uR = t512("u", BF16, 4)
            vnR = t512("vn", BF16, 2)
            otR = t512("ot", F32, 2)
            Sst = [sb(f"Sst{d}", [128, 512], stack=stB)[:] for d in range(2)]
            Sbf = [sb(f"Sbf{d}", [128, 512], BF16, stack=stB)[:] for d in range(2)]
            B_S = [S.buf("S0"), S.buf("S1")]
            B_Sb = [S.buf("Sb0"), S.buf("Sb1")]

            pg, bpg = psum()
            pgv = pg[:, 0:NCH * 16].rearrange("p (c k) -> p c k", k=16)
            for d in range(2):
                S.op("pe", lambda d=d: nc.tensor.matmul(pgv[:, :, d * 4:d * 4 + 4], lhsT=U[d], rhs=gg[:, :, d * 4:d * 4 + 4],
                                                        start=True, stop=True), reads=[B_coef, B_const], writes=[bpg])
            S.op("pe", lambda: nc.tensor.matmul(pgv[:, :, 8:16], lhsT=ones_f, rhs=gg[:, :, :], start=True, stop=True),
                 reads=[B_coef, B_const], writes=[bpg])
            S.op("dve", lambda: nc.vector.tensor_copy(out=gcl[:].rearrange("p c k -> p (c k)"), in_=pg[:, 0:NCH * 16]),
                 reads=[bpg], writes=[B_coef])
            S.op("act", lambda: nc.scalar.activation(out=egc[:], in_=gcl[:, :, 0:8], func=AF.Exp), reads=[B_coef], writes=[B_coef])
            S.op("act", lambda: nc.scalar.activation(out=egl[:], in_=gcl[:, :, 8:16], func=AF.Exp), reads=[B_coef], writes=[B_coef])
            S.op("dve", lambda: nc.vector.tensor_tensor(out=ekd[:], in0=gcl[:, :, 8:16], in1=gcl[:, :, 0:8], op=ALU.subtract),
                 reads=[B_coef], writes=[B_coef])
            S.op("act", lambda: nc.scalar.activation(out=ekd[:], in_=ekd[:], func=AF.Exp), reads=[B_coef], writes=[B_coef])
            S.op("dve", lambda: nc.vector.tensor_tensor(out=bege[:], in0=beta[:], in1=egc[:], op=ALU.mult),
                 reads=[B_coef], writes=[B_coef])

            S.mark('b0')
            def bc(t, c, d):
                return t[:, c, d * 4:d * 4 + 4].unsqueeze(2).to_broadcast([128, 4, 128])

            def v3(ap):
                return ap.rearrange("p (h j) -> p h j", h=4)

            def per_head(engname, out512, in_fn, coef, c, d, op, reads, writes):
                for h in range(4):
                    hs = slice(h * 128, (h + 1) * 128)
                    sc = coef[:, c, d * 4 + h:d * 4 + h + 1]
                    if engname == "pool":
                        if op == "mult":
                            S.op("pool", lambda hs=hs, sc=sc, h=h: nc.gpsimd.tensor_scalar(
                                out=out512[:, hs], in0=in_fn(h), scalar1=sc, scalar2=0.0, op0=ALU.mult, op1=ALU.add),
                                reads=reads, writes=writes)
                        else:
                            S.op("pool", lambda hs=hs, sc=sc, h=h: nc.gpsimd.tensor_scalar(
                                out=out512[:, hs], in0=in_fn(h), scalar1=sc, scalar2=1.0, op0=ALU.add, op1=ALU.mult),
                                reads=reads, writes=writes)
                    else:
                        S.op("dve", lambda hs=hs, sc=sc, h=h: nc.vector.tensor_scalar(
                            out=out512[:, hs], in0=in_fn(h), scalar1=sc, scalar2=None,
                            op0=(ALU.mult if op == "mult" else ALU.add)), reads=reads, writes=writes)

            def prep_shared(c, d):
                cs_ = slice(c * 128, (c + 1) * 128)
                vb_, bvb = vbR.next()
                kb_, bkb = vbR.next()
                kd_, bkd = kdR.next()
                pkk, bkk = psum()
                pqk, bqk = psum()
                for h in range(4):
                    S.op("pe", lambda h=h: nc.tensor.matmul(pkk[:, h * 128:(h + 1) * 128], lhsT=kT[:, h, cs_], rhs=kT[:, h, cs_],
                                                            start=True, stop=True), reads=[B_k[h][c]], writes=[bkk])
                for h in range(4):
                    S.op("pe", lambda h=h: nc.tensor.matmul(pqk[:, h * 128:(h + 1) * 128], lhsT=qT[:, h, cs_], rhs=kT[:, h, cs_],
                                                            start=True, stop=True), reads=[B_k[h][c], B_q[h][c]], writes=[bqk])
                return dict(vb=vb_, bvb=bvb, kb=kb_, bkb=bkb, kd=kd_, bkd=bkd, pkk=pkk, bkk=bkk, pqk=pqk, bqk=bqk)

            def gen_kv(c, d, sh):
                cs_ = slice(c * 128, (c + 1) * 128)
                kb_, bkb, kd_, bkd, vb_, bvb = sh["kb"], sh["bkb"], sh["kd"], sh["bkd"], sh["vb"], sh["bvb"]
                ptk, bptk = psum()
                ptkb = ptk.bitcast(BF16)
                for h in range(4):
                    S.op("pe", lambda h=h: nc.tensor.transpose(ptkb[:, h * 128:(h + 1) * 128], kT[:, h, cs_], ident_b),
                         reads=[B_k[h][c], B_const], writes=[bptk])
                for h in range(4):
                    hs = slice(h * 128, (h + 1) * 128)
                    k_ = d * 4 + h
                    S.op("act", lambda hs=hs, k_=k_: nc.scalar.activation(out=kb_[:, hs], in_=ptkb[:, hs], func=AF.Identity, scale=bege[:, c, k_:k_ + 1]),
                         reads=[bptk, B_coef], writes=[bkb])
                    S.op("act", lambda hs=hs, k_=k_: nc.scalar.activation(out=kd_[:, hs], in_=ptkb[:, hs], func=AF.Identity, scale=ekd[:, c, k_:k_ + 1]),
                         reads=[bptk, B_coef], writes=[bkd])
                yield
                ptv, bptv = psum()
                ptvb = ptv.bitcast(BF16)
                for h in range(4):
                    S.op("pe", lambda h=h: nc.tensor.transpose(ptvb[:, h * 128:(h + 1) * 128], vT[:, h, cs_], ident_b),
                         reads=[B_v[h][c], B_const], writes=[bptv])
                for h in range(4):
                    hs = slice(h * 128, (h + 1) * 128)
                    k_ = d * 4 + h
                    S.op("act", lambda hs=hs, k_=k_: nc.scalar.activation(out=vb_[:, hs], in_=ptvb[:, hs], func=AF.Identity, scale=beta[:, c, k_:k_ + 1]),
                         reads=[bptv, B_coef], writes=[bvb])
                yield

            def prep_dir(c, d, sh, st):
                ug, bug = f32R.next()
                per_head("pool", ug, lambda h: U[d], negg, c, d, "mult", [B_const, B_coef], [bug])
                gm, bgm = f32R.next()
                per_head("dve", gm, lambda h: MN[d], gcl, c, d, "add", [B_const, B_coef], [bgm])
                pd, bpd = psum()
                S.op("pe", lambda: nc.tensor.matmul(pd, lhsT=ones_f, rhs=ug, start=True, stop=False), reads=[bug, B_const], writes=[bpd])
                S.op("pe", lambda: nc.tensor.matmul(pd, lhsT=ident_f, rhs=gm, start=False, stop=True), reads=[bgm, B_const], writes=[bpd])
                E, bE = ER.next()
                S.op("act", lambda: nc.scalar.activation(out=E, in_=pd, func=AF.Exp), reads=[bpd], writes=[bE])
                yield
                M, bM = M0R.next()
                for h in range(4):
                    hs = slice(h * 128, (h + 1) * 128)
                    k_ = d * 4 + h
                    S.op("dve", lambda hs=hs, k_=k_: nc.vector.scalar_tensor_tensor(
                        out=M[:, hs], in0=sh["pkk"][:, hs], scalar=negb[:, c, k_:k_ + 1], in1=E[:, hs], op0=ALU.mult, op1=ALU.mult),
                        reads=[sh["bkk"], bE, B_coef], writes=[bM])
                qkm, bqkm = chR.next()
                S.op("dve", lambda: nc.vector.tensor_tensor(out=qkm, in0=sh["pqk"], in1=E, op=ALU.mult), reads=[sh["bqk"], bE], writes=[bqkm])
                yield
                pt, bpt = psum()
                ptb = pt.bitcast(BF16)
                for h in range(4):
                    S.op("pe", lambda h=h: nc.tensor.transpose(ptb[:, h * 128:(h + 1) * 128], M[:, h * 128:(h + 1) * 128], ident_b),
                         reads=[bM, B_const], writes=[bpt])
                MT, bMT = MT0R.next()
                S.op("act", lambda: nc.scalar.copy(out=MT, in_=ptb[:, 0:512]), reads=[bpt], writes=[bMT])
                pt2, bpt2 = psum()
                ptb2 = pt2.bitcast(BF16)
                for h in range(4):
                    S.op("pe", lambda h=h: nc.tensor.transpose(ptb2[:, h * 128:(h + 1) * 128], qkm[:, h * 128:(h + 1) * 128], ident_b),
                         reads=[bqkm, B_const], writes=[bpt2])
                qkT_, bqkT = qkTR.next()
                S.op("act", lambda: nc.scalar.copy(out=qkT_, in_=ptb2[:, 0:512]), reads=[bpt2], writes=[bqkT])
                st.update(M0=M, bM0=bM, MT0=MT, bMT0=bMT, qkT=qkT_, bqkT=bqkT, d=d)

            def masked(src, bsrc, m, engname):
                o, bo = chR.next()
                mk = MASKS[m].unsqueeze(1).to_broadcast([128, 4, 128])
                if engname == "pool":
                    S.op("pool", lambda: nc.gpsimd.tensor_tensor(out=v3(o), in0=v3(src), in1=mk, op=ALU.mult), reads=[bsrc, B_const], writes=[bo])
                else:
                    S.op("dve", lambda: nc.vector.tensor_tensor(out=v3(o), in0=v3(src), in1=mk, op=ALU.mult), reads=[bsrc, B_const], writes=[bo])
                return o, bo

            def mm4(lhs, blhs, rhs, brhs, acc=None, bacc=None):
                pp, bp = psum()
                if acc is not None:
                    S.op("pe", lambda: nc.tensor.matmul(pp, lhsT=ident_b, rhs=acc, start=True, stop=False), reads=[bacc, B_const], writes=[bp])
                for h in range(4):
                    hs = slice(h * 128, (h + 1) * 128)
                    S.op("pe", lambda hs=hs, h=h: nc.tensor.matmul(pp[:, hs], lhsT=lhs[:, hs], rhs=rhs[:, hs], start=(acc is None),
                                                                   stop=(h == 3 or acc is None)), reads=[blhs, brhs], writes=[bp])
                return pp, bp

            def evac(pp, bp, engname):
                o, bo = chR.next()
                if engname == "act":
                    S.op("act", lambda: nc.scalar.copy(out=o, in_=pp), reads=[bp], writes=[bo])
                else:
                    S.op("dve", lambda: nc.vector.tensor_copy(out=o, in_=pp), reads=[bp], writes=[bo])
                return o, bo

            def evac_add(pp, bp, acc, bacc):
                o, bo = chR.next()
                S.op("dve", lambda: nc.vector.tensor_tensor(out=o, in0=pp, in1=acc, op=ALU.add), reads=[bp, bacc], writes=[bo])
                return o, bo

            def transp(src, bsrc):
                pt, bpt = psum()
                ptb = pt.bitcast(BF16)
                for h in range(4):
                    S.op("pe", lambda h=h: nc.tensor.transpose(ptb[:, h * 128:(h + 1) * 128], src[:, h * 128:(h + 1) * 128], ident_b),
                         reads=[bsrc, B_const], writes=[bpt])
                o, bo = chR.next()
                S.op("act", lambda: nc.scalar.copy(out=o, in_=ptb[:, 0:512]), reads=[bpt], writes=[bo])
                return o, bo

            def chain_stages(st):
                M0, bM0, MT0, bMT0 = st["M0"], st["bM0"], st["MT0"], st["bMT0"]
                Mk, bMk = masked(M0, bM0, 0, "pool")
                MTk, bMTk = masked(MT0, bMT0, 0, "dve")
                PT, bPT = chR.next()
                S.op("dve", lambda PT=PT, MTk=MTk: nc.vector.tensor_tensor(
                    out=v3(PT), in0=v3(MTk), in1=ident_b.unsqueeze(1).to_broadcast([128, 4, 128]), op=ALU.add),
                    reads=[bMTk, B_const], writes=[bPT])
                yield
                for lev in range(3):
                    pM, bpM = mm4(MTk, bMTk, Mk, bMk)
                    nM, bnM = evac(pM, bpM, "act")
                    if lev < 2:
                        pMT, bpMT = mm4(Mk, bMk, MTk, bMTk)
                        nMT, bnMT = evac(pMT, bpMT, "act")
                    yield
                    pP, bpP = mm4(nM, bnM, PT, bPT)
                    PT, bPT = evac_add(pP, bpP, PT, bPT)
                    Mk, bMk = nM, bnM
                    if lev < 2:
                        MTk, bMTk = nMT, bnMT
                    yield
                Tt, bTt = PT, bPT
                pt, bpt = psum()
                ptb = pt.bitcast(BF16)
                for h in range(4):
                    S.op("pe", lambda h=h, Tt=Tt, ptb=ptb: nc.tensor.transpose(ptb[:, h * 128:(h + 1) * 128], Tt[:, h * 128:(h + 1) * 128], ident_b),
                         reads=[bTt, B_const], writes=[bpt])
                T, bT = chR.next()
                S.op("act", lambda T=T, ptb=ptb: nc.scalar.copy(out=T, in_=ptb[:, 0:512]), reads=[bpt], writes=[bT])
                Moff, bMoff = masked(M0, bM0, 1, "pool")
                yield
                for li in range(3):
                    last = (li == 2)
                    pX, bpX = mm4(Moff, bMoff, Tt, bTt)
                    X, bX = evac(pX, bpX, "act")
                    if li > 0:
                        T, bT = transp(Tt, bTt)
                    yield
                    pT2, bpT2 = mm4(T, bT, X, bX)
                    Tt, bTt = evac_add(pT2, bpT2, Tt, bTt)
                    if not last:
                        Moff, bMoff = masked(M0, bM0, 2 + li, "pool")
                    yield
                st["PT"], st["bPT"] = Tt, bTt

            def prep_finish(c, d, sh, st):
                Tt, bTt = st["PT"], st["bPT"]
                vb_, bvb, kb_, bkb, kd_, bkd = sh["vb"], sh["bvb"], sh["kb"], sh["bkb"], sh["kd"], sh["bkd"]
                pu, bpu = psum()
                pw, bpw = psum()
                for h in range(4):
                    hs = slice(h * 128, (h + 1) * 128)
                    S.op("pe", lambda hs=hs: nc.tensor.matmul(pu[:, hs], lhsT=Tt[:, hs], rhs=vb_[:, hs], start=True, stop=True),
                         reads=[bTt, bvb], writes=[bpu])
                for h in range(4):
                    hs = slice(h * 128, (h + 1) * 128)
                    S.op("pe", lambda hs=hs: nc.tensor.matmul(pw[:, hs], lhsT=kb_[:, hs], rhs=Tt[:, hs], start=True, stop=True),
                         reads=[bTt, bkb], writes=[bpw])
                u_, bu = uR.next()
                wT_, bwT = wTR.next()
                S.op("act", lambda: nc.scalar.copy(out=u_, in_=pu), reads=[bpu], writes=[bu])
                S.op("dve", lambda: nc.vector.tensor_copy(out=wT_, in_=pw), reads=[bpw], writes=[bwT])
                st.update(u=u_, bu=bu, wT=wT_, bwT=bwT, kd=kd_, bkd=bkd)
                if c in (0, 1) and (c == d):
                    dbg(f"Tt{d}", Tt, [128, 512], BF16, [bTt])
                    dbg(f"u{d}", u_, [128, 512], BF16, [bu])
                    dbg(f"wT{d}", wT_, [128, 512], BF16, [bwT])
                    dbg(f"kd{d}", kd_, [128, 512], BF16, [bkd])
                    dbg(f"qkT{d}", st['qkT'], [128, 512], BF16, [st['bqkT']])

            def scan_step(c, d, st, first_dir_for_chunk):
                cs_ = slice(c * 128, (c + 1) * 128)
                pv, bpv = psum()
                for h in range(4):
                    hs = slice(h * 128, (h + 1) * 128)
                    S.op("pe", lambda hs=hs: nc.tensor.matmul(pv[:, hs], lhsT=st["wT"][:, hs], rhs=Sbf[d][:, hs], start=True, stop=True),
                         reads=[st["bwT"], B_Sb[d]], writes=[bpv])
                vn, bvn = vnR.next()
                S.op("dve", lambda: nc.vector.tensor_tensor(out=vn, in0=st["u"], in1=pv, op=ALU.subtract), reads=[st["bu"], bpv], writes=[bvn])
                yield
                po1, bpo1 = psum()
                po2, bpo2 = psum()
                pS, bpS = psum()
                for h in range(4):
                    hs = slice(h * 128, (h + 1) * 128)
                    S.op("pe", lambda hs=hs, h=h: nc.tensor.matmul(po1[:, hs], lhsT=qT[:, h, cs_], rhs=Sbf[d][:, hs], start=True, stop=True),
                         reads=[B_q[h][c], B_Sb[d]], writes=[bpo1])
                for h in range(4):
                    hs = slice(h * 128, (h + 1) * 128)
                    S.op("pe", lambda hs=hs: nc.tensor.matmul(po2[:, hs], lhsT=st["qkT"][:, hs], rhs=vn[:, hs], start=True, stop=True),
                         reads=[st["bqkT"], bvn], writes=[bpo2])
                for h in range(4):
                    hs = slice(h * 128, (h + 1) * 128)
                    S.op("pe", lambda hs=hs: nc.tensor.matmul(pS[:, hs], lhsT=st["kd"][:, hs], rhs=vn[:, hs], start=True, stop=True),
                         reads=[st["bkd"], bvn], writes=[bpS])
                ot, bot = otR.next()
                per_head("dve", ot, lambda h: po1[:, h * 128:(h + 1) * 128], egc, c, d, "mult", [bpo1, B_coef], [bot])
                if first_dir_for_chunk:
                    S.op("dve", lambda: nc.vector.tensor_tensor(out=O[:, c, :], in0=ot, in1=po2, op=ALU.add),
                         reads=[bot, bpo2], writes=[B_O[c]])
                else:
                    S.op("dve", lambda: nc.vector.tensor_tensor(out=ot, in0=ot, in1=po2, op=ALU.add), reads=[bot, bpo2], writes=[bot])
                    S.op("pool", lambda: nc.gpsimd.tensor_tensor(out=ot, in0=ot, in1=O[:, c, :], op=ALU.add),
                         reads=[bot, B_O[c]], writes=[bot])
                for h in range(4):
                    hs = slice(h * 128, (h + 1) * 128)
                    S.op("dve", lambda hs=hs, h=h: nc.vector.scalar_tensor_tensor(
                        out=Sst[d][:, hs], in0=Sst[d][:, hs], scalar=egl[:, c, d * 4 + h:d * 4 + h + 1], in1=pS[:, hs],
                        op0=ALU.mult, op1=ALU.add), reads=[B_S[d], bpS, B_coef], writes=[B_S[d]])
                S.op("act", lambda: nc.scalar.copy(out=Sbf[d], in_=Sst[d]), reads=[B_S[d]], writes=[B_Sb[d]])
                yield
                if c in (0, 1) and (c == d):
                    dbg(f"vn{d}", vn, [128, 512], BF16, [bvn])
                    dbg(f"S{d}", Sst[d], [128, 512], F32, [B_S[d]])
                    dbg(f"O{d}", O[:, c, :], [128, 512], BF16, [B_O[c]])
                if not first_dir_for_chunk:
                    gate_chunk(c, ot, bot)

            gtmp = [sb(f"gtmp{i}", [128, 512], stack=stB) for i in range(1)]
            gR = Ring(S, [t[:] for t in gtmp], "gtmp")
            gsm = [sb(f"gsm{i}", [128, 8], stack=stB) for i in range(2)]
            gsR = Ring(S, [t[:] for t in gsm], "gsm")
            yab = [sb(f"yab{i}", [128, 512], BF16, stack=stB) for i in range(1)]
            yaR = Ring(S, [t[:] for t in yab], "yab")

            def gate_chunk(c, osum, bos):
                t1, b1 = gR.next()
                S.op("pool", lambda: nc.gpsimd.tensor_tensor(out=t1, in0=osum, in1=osum, op=ALU.mult), reads=[bos], writes=[b1])
                sm, bsm = gsR.next()
                S.op("dve", lambda: nc.vector.tensor_reduce(out=sm[:, 0:4], in_=v3(t1), axis=AX.X, op=ALU.add), reads=[b1], writes=[bsm])
                S.op("pool", lambda: nc.gpsimd.tensor_scalar(out=sm[:, 0:4], in0=sm[:, 0:4], scalar1=1.0 / 128, scalar2=EPS,
                                                             op0=ALU.mult, op1=ALU.add), reads=[bsm], writes=[bsm])
                S.op("pool", lambda: nc.gpsimd.tensor_tensor(out=sm[:, 4:8], in0=sm[:, 0:4], in1=mh4[:], op=ALU.pow),
                     reads=[bsm, B_const], writes=[bsm])
                for h in range(4):
                    S.op("dve", lambda h=h: nc.vector.tensor_scalar(out=t1[:, h * 128:(h + 1) * 128], in0=osum[:, h * 128:(h + 1) * 128],
                                                                    scalar1=sm[:, 4 + h:5 + h], scalar2=None, op0=ALU.mult),
                         reads=[bos, bsm], writes=[b1])
                S.op("pool", lambda: nc.gpsimd.tensor_tensor(out=t1, in0=t1, in1=gn_b[:], op=ALU.mult), reads=[b1, B_gn], writes=[b1])
                ya, bya = yaR.next()
                S.op("dve", lambda: nc.vector.tensor_tensor(out=ya, in0=t1, in1=zs[:, c, :], op=ALU.mult), reads=[b1, B_z[c]], writes=[bya])
                pt, bpt = psum()
                ptb = pt.bitcast(BF16)
                for h in range(4):
                    S.op("pe", lambda h=h: nc.tensor.transpose(ptb[:, h * 128:(h + 1) * 128], ya[:, h * 128:(h + 1) * 128], ident_b),
                         reads=[bya, B_const], writes=[bpt])
                S.op("act", lambda: nc.scalar.copy(out=qT[:, :, c * 128:(c + 1) * 128], in_=v3(ptb[:, 0:512])),
                     reads=[bpt], writes=[B_q[h][c] for h in range(4)])

            def run_rr(gens):
                alive = [True] * len(gens)
                while any(alive):
                    for gi in range(len(gens)):
                        if alive[gi]:
                            try:
                                next(gens[gi])
                            except StopIteration:
                                alive[gi] = False

            def state_out(si_prev):
                for d, od in [(0, sf_d), (1, sb_d)]:
                    S.dma("sp", lambda d=d, od=od, si_prev=si_prev: nc.sync.dma_start(
                        out=od[si_prev].rearrange("h k v -> k h v"), in_=Sst[d][:].rearrange("p (h v) -> p h v", h=4)),
                        reads=[B_S[d]], stream="so")

            def state_init(si):
                for d in range(2):
                    if si == 2:
                        S.dma("sp", lambda d=d: nc.sync.dma_start(out=Sst[d][:], in_=s0_d[d]), writes=[B_S[d]], stream=f"s0{d}")
                    else:
                        S.op("pool", lambda d=d: nc.gpsimd.memset(Sst[d][:], 0.0), writes=[B_S[d]])
                    S.op("act", lambda d=d: nc.scalar.copy(out=Sbf[d][:], in_=Sst[d][:]), reads=[B_S[d]], writes=[B_Sb[d]])

            pending = []
            for si, (c0, n, r) in enumerate(SEQS):
                for t in range(n):
                    cf, cb = c0 + t, c0 + n - 1 - t
                    assert cf != cb
                    issue_wc(2)
                    shs = {cf: prep_shared(cf, 0)}
                    shs[cb] = prep_shared(cb, 1)
                    sts = [{}, {}]
                    run_rr([prep_dir(cf, 0, shs[cf], sts[0]), prep_dir(cb, 1, shs[cb], sts[1])])
                    run_rr([chain_stages(sts[0]), chain_stages(sts[1])] + pending + [gen_kv(cf, 0, shs[cf]), gen_kv(cb, 1, shs[cb])])
                    prep_finish(cf, 0, shs[cf], sts[0])
                    prep_finish(cb, 1, shs[cb], sts[1])
                    if t == 0:
                        if si > 0:
                            state_out(si - 1)
                        state_init(si)
                    pending = [scan_step(cf, 0, sts[0], (cf < cb)), scan_step(cb, 1, sts[1], (cf < cb))]
            run_rr(pending)
            issue_wc(100)
            S.barrier()
        S.barrier()
    S.barrier()

    S.mark('b')
    if debug:
        dd = dout("dbg_yaT", [128, 4, NT], BF16)
        S.dma("sp", lambda: nc.sync.dma_start(out=dd, in_=qT[:]), stream="dbg")
        S.barrier()

    with contextlib.ExitStack() as stC:
        wo = sb("wo", [128, 8, D], BF16, stack=stC)
        wd = sb("wd", [128, NF, D], BF16, stack=stC)
        B_wo = S.buf("wo")
        B_wd = S.buf("wd")
        wgu = [sb(f"wgu{i}", [128, 2, D], BF16, stack=stC) for i in range(2)]
        wguR = Ring(S, [t[:] for t in wgu], "wgu")
        wgu_b2 = [S.buf("wgu2_0"), S.buf("wgu2_1")]
        h2T = sb("h2T", [128, 8, 512], BF16, stack=stC)
        B_h2 = [S.buf(f"h2_{i}") for i in range(4)]
        actT = sb("actT", [128, NF, 512], BF16, stack=stC)
        B_act = [S.buf(f"act{f}") for f in range(NF)]
        x1s = [sb(f"x1_{j}", [128, 4, D], stack=stC) for j in range(2)]
        B_x1s = [[S.buf(f"x1_{j}_{i}") for i in range(4)] for j in range(2)]
        gb = sb("gb", [128, 2, D], stack=stC)
        B_gb = S.buf("gb")
        fgb = sb("fgb", [128, D], stack=stC)
        B_fg = S.buf("fgb")
        xh2 = [sb(f"xh2{i}", [128, D], BF16, stack=stC) for i in range(2)]
        xhR2 = Ring(S, [t[:] for t in xh2], "xh2")
        sqR2 = xhR2
        ss2 = [sb(f"ss2{i}", [128, 4], stack=stC) for i in range(4)]
        ssR2 = Ring(S, [t[:] for t in ss2], "ss2")
        sg = [sb(f"sg{i}", [128, 512], stack=stC) for i in range(2)]
        sgR = Ring(S, [t[:] for t in sg], "sg")
        sqJ = Ring(S, [], "sqJ")
        sqJ.items = [(it[0].bitcast(BF16), it[1]) for it in sgR.items]
        tc_ = [sb(f"tc{i}", [128, 512], stack=stC) for i in range(2)]
        tcR = Ring(S, [t[:] for t in tc_], "tc")

        S.dma("sp", lambda: nc.sync.dma_start(out=wo[:], in_=wo_s.rearrange("(kc p) n -> p kc n", p=128)),
              reads=B_wscr["wo"], writes=[B_wo], stream="wo")
        S.dma("sp", lambda: nc.sync.dma_start(out=fgb[:], in_=fg_d.partition_broadcast(128)[:, 0, :]), writes=[B_fg], stream="wo")
        S.dma("sp", lambda: nc.sync.dma_start(out=wd[:], in_=wd_s.rearrange("(fc p) n -> p fc n", p=128)),
              reads=B_wscr["wd"], writes=[B_wd], stream="wd")

        for st5 in range(5):
            r = 0 if st5 == 0 else 1
            x1 = x1s[st5 % 2]
            B_x1 = B_x1s[st5 % 2]
            if st5 in (0, 1):
                for gi, seg in enumerate([2, 5]):
                    S.dma("sp", lambda gi=gi, seg=seg, r=r: nc.sync.dma_start(
                        out=gb[:, gi, :], in_=mod_d[r:r + 1, seg * D:(seg + 1) * D].partition_broadcast(128)[:, 0, :]),
                        reads=B_mblk[seg * 2:seg * 2 + 2], writes=[B_gb], stream="gb")
            for sub in range(4):
                c = st5 * 4 + sub
                S.dma("sp", lambda sub=sub, c=c, x1=x1: nc.sync.dma_start(out=x1[:, sub, :], in_=x_d[c * 128:(c + 1) * 128, :]),
                      writes=[B_x1[sub]], stream=f"x1_{sub}")
            def c_stage_a(sub, x1=x1, B_x1=B_x1, st5=st5):
                c = st5 * 4 + sub
                for half in range(2):
                    pp, bp = psum()
                    for kc in range(8):
                        src = qT if kc < 4 else ybT
                        bsrc = B_q[kc][c] if kc < 4 else B_yb[c]
                        S.op("pe", lambda pp=pp, kc=kc, c=c, half=half, src=src: nc.tensor.matmul(
                            pp, lhsT=src[:, kc % 4, c * 128:(c + 1) * 128], rhs=wo[:, kc, half * 512:(half + 1) * 512],
                            start=(kc == 0), stop=(kc == 7)), reads=[bsrc, B_wo], writes=[bp])
                    tt, btt = tcR.next()
                    S.op("dve", lambda pp=pp, tt=tt, half=half: nc.vector.tensor_tensor(
                        out=tt, in0=pp, in1=gb[:, 0, half * 512:(half + 1) * 512], op=ALU.mult), reads=[bp, B_gb], writes=[btt])
                    S.op("pool", lambda tt=tt, sub=sub, half=half, x1=x1: nc.gpsimd.tensor_tensor(
                        out=x1[:, sub, half * 512:(half + 1) * 512], in0=x1[:, sub, half * 512:(half + 1) * 512], in1=tt, op=ALU.add),
                        reads=[btt, B_x1[sub]], writes=[B_x1[sub]])
                return norm_stage_a(x1[:, sub, :], B_x1[sub], sqJ, xhR2, ssR2)

            cur = c_stage_a(0)
            for sub in range(4):
                nxt = c_stage_a(sub + 1) if sub + 1 < 4 else None
                norm_stage_b(cur[0], cur[1], 3, 2, r, h2T, sub * 128, B_h2[sub], B_mod2)
                cur = nxt
            for f in range(NF):
                wt, bw = wguR.next()
                bw2 = wgu_b2[(wguR.i - 1) % 2]
                S.dma("sp", lambda wt=wt, f=f: nc.sync.dma_start(out=wt[:, 0, :], in_=wg_s[f * 128:(f + 1) * 128, :]),
                      reads=B_wscr["wg"], writes=[bw], stream=f"wgu{f % 3}")
                S.dma("sp", lambda wt=wt, f=f: nc.sync.dma_start(out=wt[:, 1, :], in_=wu_s[f * 128:(f + 1) * 128, :]),
                      reads=B_wscr["wu"], writes=[bw2], stream=f"wgu{f % 3}")
                pg_, bg_ = psum()
                pu_, bu_ = psum()
                for (pp, bp, wi) in [(pg_, bg_, 0), (pu_, bu_, 1)]:
                    for kc in range(8):
                        S.op("pe", lambda pp=pp, kc=kc, wt=wt, wi=wi: nc.tensor.matmul(
                            pp, lhsT=wt[:, wi, kc * 128:(kc + 1) * 128], rhs=h2T[:, kc, :], start=(kc == 0), stop=(kc == 7)),
                            reads=[bw if wi == 0 else bw2] + B_h2, writes=[bp])
                sgt, bsg = sgR.next()
                S.op("act", lambda pg_=pg_, sgt=sgt: nc.scalar.activation(out=sgt, in_=pg_, func=AF.Silu), reads=[bg_], writes=[bsg])
                S.op("dve", lambda pu_=pu_, sgt=sgt, f=f: nc.vector.tensor_tensor(out=actT[:, f, :], in0=pu_, in1=sgt, op=ALU.mult),
                     reads=[bu_, bsg], writes=[B_act[f]])
            for sub in range(4):
                c = st5 * 4 + sub
                for half in range(2):
                    pp, bp = psum()
                    for f in range(NF):
                        S.op("pe", lambda pp=pp, f=f, sub=sub, half=half: nc.tensor.matmul(
                            pp, lhsT=actT[:, f, sub * 128:(sub + 1) * 128], rhs=wd[:, f, half * 512:(half + 1) * 512],
                            start=(f == 0), stop=(f == NF - 1)), reads=[B_act[f], B_wd], writes=[bp])
                    tt, btt = tcR.next()
                    S.op("dve", lambda pp=pp, tt=tt, half=half: nc.vector.tensor_tensor(
                        out=tt, in0=pp, in1=gb[:, 1, half * 512:(half + 1) * 512], op=ALU.mult), reads=[bp, B_gb], writes=[btt])
                    S.op("pool", lambda tt=tt, sub=sub, half=half, x1=x1: nc.gpsimd.tensor_tensor(
                        out=x1[:, sub, half * 512:(half + 1) * 512], in0=x1[:, sub, half * 512:(half + 1) * 512], in1=tt, op=ALU.add),
                        reads=[btt, B_x1[sub]], writes=[B_x1[sub]])
                sq, bsq = sqR2.next()
                ss, bss = ssR2.next()
                S.op("act", lambda sq=sq, ss=ss, sub=sub, x1=x1: nc.scalar.activation(out=sq, in_=x1[:, sub, :], func=AF.Square,
                                                                              accum_out=ss[:, 0:1]), reads=[B_x1[sub]], writes=[bsq, bss])
                S.op("pool", lambda ss=ss: nc.gpsimd.tensor_scalar(out=ss[:, 1:2], in0=ss[:, 0:1], scalar1=1.0 / D, scalar2=EPS,
                                                                   op0=ALU.mult, op1=ALU.add), reads=[bss], writes=[bss])
                S.op("pool", lambda ss=ss: nc.gpsimd.tensor_tensor(out=ss[:, 2:3], in0=ss[:, 1:2], in1=cols[:, 1:2], op=ALU.pow),
                     reads=[bss, B_const], writes=[bss])
                S.op("dve", lambda ss=ss, sub=sub, x1=x1: nc.vector.scalar_tensor_tensor(
                    out=x1[:, sub, :], in0=x1[:, sub, :], scalar=ss[:, 2:3], in1=fgb[:], op0=ALU.mult, op1=ALU.mult),
                    reads=[B_x1[sub], bss, B_fg], writes=[B_x1[sub]])
                S.dma("sp", lambda sub=sub, c=c, x1=x1: nc.sync.dma_start(out=y_d[c * 128:(c + 1) * 128, :], in_=x1[:, sub, :]),
                      reads=[B_x1[sub]], stream=f"yo{sub}")
        S.barrier()

    if stop is not None:
        S.ops = S.ops[:S.marks[stop]]
    S.emit()
    es.close()
    return nc, S


def _consts():
    i = np.arange(128)
    ident = np.eye(128, dtype=np.float32)
    Uf = (i[:, None] <= i[None, :]).astype(np.float32)
    Ub = (i[:, None] >= i[None, :]).astype(np.float32)
    MNf = np.where(i[:, None] >= i[None, :], 0.0, NEG).astype(np.float32)
    MNb = np.where(i[:, None] <= i[None, :], 0.0, NEG).astype(np.float32)
    offd = (1.0 - ident).astype(np.float32)
    ones = np.ones((128, 128), np.float32)
    blk = lambda sz: (i[:, None] // sz == i[None, :] // sz)
    masks = [blk(16) & (i[:, None] != i[None, :])] + [(blk(2 * sz) & ~blk(sz)) for sz in (16, 32, 64)]
    consts = np.concatenate([ident, Uf, Ub, MNf, MNb, offd, ones] + [m.astype(np.float32) for m in masks], axis=1)
    sel = np.zeros((2, 2, 128), np.float32)
    sel[0, 0, :] = 1.0
    sel[1, 1, :] = 1.0
    return np.ascontiguousarray(consts), sel


_CACHE = {}


def make_in_maps(x_prompt, x_sample, state_fwd, state_bwd, c, c_ctx, w_ada, b_ada, norm1_g, norm2_g, w_in, conv_w,
                 a_log, dt_bias, dn_norm_g, sgu_ln_g, sgu_ln_b, sgu_w, sgu_b, w_out, w_gate, w_up, w_down, final_g):
    f = lambda a: np.ascontiguousarray(np.asarray(a, dtype=np.float32))
    consts, sel = _consts()
    shared = {
        "w_ada": f(w_ada[0]),
        "b_ada2": f(np.stack([b_ada[0], b_ada[0]], 0)),
        "w_in": f(w_in[0]),
        "w_out": f(w_out[0]),
        "wg_t": f(np.asarray(w_gate[0]).reshape(8, 128, NF, 128).transpose(2, 1, 0, 3).reshape(NF * 128, D)),
        "wu_t": f(np.asarray(w_up[0]).reshape(8, 128, NF, 128).transpose(2, 1, 0, 3).reshape(NF * 128, D)),
        "w_down": f(w_down[0]),
        "conv_wT": f(np.asarray(conv_w[0]).reshape(5, 12, 128).transpose(2, 1, 0)),
        "n1g": f(np.asarray(norm1_g[0]).reshape(8, 128).T),
        "n2g": f(np.asarray(norm2_g[0]).reshape(8, 128).T),
        "adt": f(np.concatenate([np.asarray(a_log[0]).reshape(-1), np.asarray(dt_bias[0]).reshape(-1)])[None, :]),
        "dn_g": f(np.asarray(dn_norm_g[0])[None, :]),
        "lngT": f(np.asarray(sgu_ln_g[0]).reshape(4, 128).T),
        "ln_b": f(np.asarray(sgu_ln_b[0])[None, :]),
        "sgu_wT": f(np.asarray(sgu_w[0]).transpose(2, 0, 1)),
        "sgu_b": f(np.asarray(sgu_b[0]).reshape(1, 512)),
        "final_g": f(np.asarray(final_g)[None, :]),
        "consts": consts,
        "sel": sel,
    }
    xp = np.asarray(x_prompt, np.float32)
    xs = np.asarray(x_sample, np.float32)
    maps = []
    for i in range(8):
        m = dict(shared)
        m["x"] = f(np.concatenate([xp[2 * i], xp[2 * i + 1], xs[i]], axis=0))
        cv = np.stack([np.asarray(c_ctx, np.float32), np.asarray(c[i], np.float32)], 0)
        m["cT"] = f(cv.reshape(2, 8, 128).transpose(2, 1, 0))
        s0 = np.stack([np.asarray(state_fwd[i, 0]), np.asarray(state_bwd[i, 0])], 0)
        m["s0"] = f(s0.transpose(0, 2, 1, 3).reshape(2, 128, 512))
        maps.append(m)
    return maps


def kernel(**inputs):
    if "nc" not in _CACHE:
        _CACHE["nc"] = build_nc(debug=False)[0]
    nc = _CACHE["nc"]
    maps = make_in_maps(**inputs)
    res = run_bass_kernel_spmd(nc, maps, core_ids=list(range(8)))
    y_prompt = np.zeros((16, 256, D), np.float32)
    y_sample = np.zeros((8, 2048, D), np.float32)
    nsf = np.zeros((16, 1, 4, 128, 128), np.float32)
    nsb = np.zeros((16, 1, 4, 128, 128), np.float32)
    for i in range(8):
        r = res.results[i]
        y = np.asarray(r["y"])
        y_prompt[2 * i] = y[0:256]
        y_prompt[2 * i + 1] = y[256:512]
        y_sample[i] = y[512:]
        sf = np.asarray(r["sf"])
        sbb = np.asarray(r["sb"])
        nsf[2 * i, 0] = sf[0]
        nsf[2 * i + 1, 0] = sf[1]
        nsb[2 * i, 0] = sbb[0]
        nsb[2 * i + 1, 0] = sbb[1]
    return (y_prompt, y_sample, nsf, nsb)
```
